# Optimizing a Trainium2 kernel written in Bass

```python
import math
import jax, jax.numpy as jnp
from jax import lax
import numpy as np

D_MODEL = 1024
BATCH = 8
SEQ = 4096
DEPTH = 2

EPS = 1e-6
ROPE_THETA = 500000.0
Q_BLOCK = 128

MLA_HEADS = 8
MLA_NOPE = 64
MLA_ROPE = 32
MLA_V = 64
Q_LORA = 384
KV_LORA = 256
MLA_WIDTH = MLA_HEADS * MLA_V

POOL_WINDOWS = (2, 4, 8, 16)
POOL_GROUP = 128
POOL_WIDTH = POOL_GROUP * len(POOL_WINDOWS)

IN_A = Q_LORA + KV_LORA + MLA_ROPE + POOL_WIDTH

DIFF_HEADS = 8
DIFF_HD = 64
DIFF_V = 2 * DIFF_HD
ROT_DIFF = DIFF_HD // 4
DIFF_QK_WIDTH = DIFF_HEADS * 2 * DIFF_HD
DIFF_V_WIDTH = DIFF_HEADS * DIFF_V

D_FF = -(-8 * D_MODEL // (3 * 256)) * 256

N_EVEN = (DEPTH + 1) // 2
N_ODD = DEPTH // 2

kernel_name = "hybrid_mla_pool_diffattn_adaln"


def rms_norm(x, g):
    xf = x.astype(jnp.float32)
    y = xf * lax.rsqrt(jnp.mean(xf * xf, axis=-1, keepdims=True) + EPS)
    return (y * g.astype(jnp.float32)).astype(x.dtype)


def rope_tables(positions, dim):
    inv = ROPE_THETA ** (-jnp.arange(0, dim, 2, dtype=jnp.float32) / dim)
    ang = positions.astype(jnp.float32)[..., None] * inv
    return jnp.cos(ang), jnp.sin(ang)


def apply_rope(x, cos, sin):
    x1, x2 = jnp.split(x, 2, axis=-1)
    cos = cos.astype(x.dtype)
    sin = sin.astype(x.dtype)
    return jnp.concatenate([x1 * cos - x2 * sin, x2 * cos + x1 * sin], axis=-1)


def query_blocks(t):
    b, s = t.shape[:2]
    t = t.reshape((b, s // Q_BLOCK, Q_BLOCK) + t.shape[2:])
    return jnp.moveaxis(t, 1, 0)


def merge_blocks(t):
    t = jnp.moveaxis(t, 0, 1)
    return t.reshape((t.shape[0], t.shape[1] * t.shape[2]) + t.shape[3:])


def causal_mask(blk, s):
    qpos = blk * Q_BLOCK + jnp.arange(Q_BLOCK)
    return jnp.arange(s)[None, :] <= qpos[:, None]


def mla_attention(q_nope, q_rope, k_nope, k_rope, v):
    s = k_nope.shape[1]
    scale = (MLA_NOPE + MLA_ROPE) ** -0.5

    def one_block(args):
        qn, qr, blk = args
        sc = (jnp.einsum('bqhd,bkhd->bhqk', qn, k_nope, preferred_element_type=jnp.float32)
              + jnp.einsum('bqhr,bkr->bhqk', qr, k_rope, preferred_element_type=jnp.float32))
        sc = jnp.where(causal_mask(blk, s), sc * scale, -jnp.inf)
        p = jax.nn.softmax(sc, axis=-1).astype(v.dtype)
        return jnp.einsum('bhqk,bkhd->bqhd', p, v)

    out = lax.map(one_block, (query_blocks(q_nope), query_blocks(q_rope),
                              jnp.arange(s // Q_BLOCK)))
    return merge_blocks(out)


def pool_mixer(u, w, b, scale):
    bsz, s, _ = u.shape
    uf = u.astype(jnp.float32)
    cs = jnp.cumsum(uf, axis=1)
    count = jnp.arange(1, s + 1, dtype=jnp.float32)[None, :, None]
    outs = []
    for g, win in enumerate(POOL_WINDOWS):
        csg = cs[..., g * POOL_GROUP:(g + 1) * POOL_GROUP]
        prev = jnp.pad(csg, ((0, 0), (win, 0), (0, 0)))[:, :s]
        mean = (csg - prev) / jnp.minimum(count, win)
        outs.append(mean - uf[..., g * POOL_GROUP:(g + 1) * POOL_GROUP])
    pooled = jnp.stack(outs, axis=2).astype(u.dtype)
    y = jnp.einsum('bsgc,gcd->bsgd', pooled, w) + b
    return y.reshape(bsz, s, POOL_WIDTH) * scale


def mla_pool_mixer(h, cos_r, sin_r, w_in, q_norm_g, kv_norm_g, w_uq, w_ukv,
                   p_w, p_b, p_scale, w_out):
    bsz, s, _ = h.shape
    proj = h @ w_in
    c_q, c_kv, k_rope, u = jnp.split(
        proj, [Q_LORA, Q_LORA + KV_LORA, Q_LORA + KV_LORA + MLA_ROPE], axis=-1)
    q = (rms_norm(c_q, q_norm_g) @ w_uq).reshape(bsz, s, MLA_HEADS, MLA_NOPE + MLA_ROPE)
    q_nope, q_rope = q[..., :MLA_NOPE], q[..., MLA_NOPE:]
    kv = (rms_norm(c_kv, kv_norm_g) @ w_ukv).reshape(bsz, s, MLA_HEADS, MLA_NOPE + MLA_V)
    k_nope, v = kv[..., :MLA_NOPE], kv[..., MLA_NOPE:]
    q_rope = apply_rope(q_rope, cos_r[:, :, None], sin_r[:, :, None])
    k_rope = apply_rope(k_rope, cos_r, sin_r)
    attn = mla_attention(q_nope, q_rope, k_nope, k_rope, v).reshape(bsz, s, MLA_WIDTH)
    pool = pool_mixer(u, p_w, p_b, p_scale)
    return jnp.concatenate([attn, pool], axis=-1) @ w_out


def diff_attention_mixer(h, cos_p, sin_p, w_qkv, lq1, lk1, lq2, lk2, subln_g, w_out,
                         lambda_init):
    bsz, s, _ = h.shape
    proj = h @ w_qkv
    q, k, v = jnp.split(proj, [DIFF_QK_WIDTH, 2 * DIFF_QK_WIDTH], axis=-1)
    q = q.reshape(bsz, s, DIFF_HEADS, 2, DIFF_HD)
    k = k.reshape(bsz, s, DIFF_HEADS, 2, DIFF_HD)
    v = v.reshape(bsz, s, DIFF_HEADS, DIFF_V)
    cp, sp = cos_p[:, :, None, None], sin_p[:, :, None, None]
    q = jnp.concatenate([apply_rope(q[..., :ROT_DIFF], cp, sp), q[..., ROT_DIFF:]], axis=-1)
    k = jnp.concatenate([apply_rope(k[..., :ROT_DIFF], cp, sp), k[..., ROT_DIFF:]], axis=-1)
    lam = (jnp.exp(jnp.sum(lq1.astype(jnp.float32) * lk1.astype(jnp.float32)))
           - jnp.exp(jnp.sum(lq2.astype(jnp.float32) * lk2.astype(jnp.float32)))
           + lambda_init)
    scale = DIFF_HD ** -0.5

    def one_block(args):
        qb, blk = args
        sc = jnp.einsum('bqhcd,bkhcd->bhcqk', qb, k, preferred_element_type=jnp.float32)
        sc = jnp.where(causal_mask(blk, s), sc * scale, -jnp.inf)
        p = jax.nn.softmax(sc, axis=-1)
        a = (p[:, :, 0] - lam * p[:, :, 1]).astype(v.dtype)
        return jnp.einsum('bhqk,bkhd->bqhd', a, v)

    out = merge_blocks(lax.map(one_block, (query_blocks(q), jnp.arange(s // Q_BLOCK))))
    out = rms_norm(out, subln_g) * (1.0 - lambda_init)
    return out.reshape(bsz, s, DIFF_V_WIDTH) @ w_out


def swiglu(h, w_gate_up, w_down):
    g, u = jnp.split(h @ w_gate_up, 2, axis=-1)
    return (jax.nn.silu(g) * u) @ w_down


def diff_lambda_init(layer_idx):
    return 0.8 - 0.6 * math.exp(-0.3 * layer_idx)


def setup_inputs(seed: int = 0) -> dict:
    key = jax.random.key(seed)
    ks = iter(jax.random.split(key, 32))

    def nrm(shape, scale):
        return jax.random.normal(next(ks), shape, jnp.float32) * scale

    def gain(shape):
        return 1.0 + nrm(shape, 0.05)

    D = D_MODEL
    x = nrm((BATCH, SEQ, D), 1.0)
    c = nrm((BATCH, D), 1.0)
    offsets = jax.random.randint(next(ks), (BATCH, 1), 0, 1024, dtype=jnp.int32)
    positions = (jnp.arange(SEQ, dtype=jnp.int32)[None, :] + offsets).astype(jnp.int32)
    return {
        "x": x,
        "c": c,
        "positions": positions,
        "ada_w": nrm((DEPTH, D, 6 * D), 0.5 * D ** -0.5),
        "ada_b": nrm((DEPTH, 6 * D), 0.02),
        "norm1_g": gain((DEPTH, D)),
        "norm2_g": gain((DEPTH, D)),
        "ffn_w_gate_up": nrm((DEPTH, D, 2 * D_FF), D ** -0.5),
        "ffn_w_down": nrm((DEPTH, D_FF, D), D_FF ** -0.5),
        "mla_w_in": nrm((N_EVEN, D, IN_A), D ** -0.5),
        "mla_q_norm_g": gain((N_EVEN, Q_LORA)),
        "mla_kv_norm_g": gain((N_EVEN, KV_LORA)),
        "mla_w_uq": nrm((N_EVEN, Q_LORA, MLA_HEADS * (MLA_NOPE + MLA_ROPE)), Q_LORA ** -0.5),
        "mla_w_ukv": nrm((N_EVEN, KV_LORA, MLA_HEADS * (MLA_NOPE + MLA_V)), KV_LORA ** -0.5),
        "pool_w": nrm((N_EVEN, len(POOL_WINDOWS), POOL_GROUP, POOL_GROUP), POOL_GROUP ** -0.5),
        "pool_b": nrm((N_EVEN, len(POOL_WINDOWS), POOL_GROUP), 0.02),
        "pool_scale": gain((N_EVEN, POOL_WIDTH)),
        "mix_a_w_out": nrm((N_EVEN, MLA_WIDTH + POOL_WIDTH, D), (MLA_WIDTH + POOL_WIDTH) ** -0.5),
        "diff_w_qkv": nrm((N_ODD, D, 2 * DIFF_QK_WIDTH + DIFF_V_WIDTH), D ** -0.5),
        "diff_lambda_q1": nrm((N_ODD, DIFF_HD), 0.1),
        "diff_lambda_k1": nrm((N_ODD, DIFF_HD), 0.1),
        "diff_lambda_q2": nrm((N_ODD, DIFF_HD), 0.1),
        "diff_lambda_k2": nrm((N_ODD, DIFF_HD), 0.1),
        "diff_subln_g": gain((N_ODD, DIFF_V)),
        "diff_w_out": nrm((N_ODD, DIFF_V_WIDTH, D), DIFF_V_WIDTH ** -0.5),
        "final_norm_g": gain((D,)),
    }


def reference(x, c, positions, ada_w, ada_b, norm1_g, norm2_g, ffn_w_gate_up, ffn_w_down,
              mla_w_in, mla_q_norm_g, mla_kv_norm_g, mla_w_uq, mla_w_ukv, pool_w, pool_b,
              pool_scale, mix_a_w_out, diff_w_qkv, diff_lambda_q1, diff_lambda_k1,
              diff_lambda_q2, diff_lambda_k2, diff_subln_g, diff_w_out, final_norm_g):
    cos_r, sin_r = rope_tables(positions, MLA_ROPE)
    cos_p, sin_p = rope_tables(positions, ROT_DIFF)
    cond = jax.nn.silu(c)
    for i in range(DEPTH):
        mod = cond @ ada_w[i] + ada_b[i]
        sh1, sc1, g1, sh2, sc2, g2 = [m[:, None, :] for m in jnp.split(mod, 6, axis=-1)]
        h = rms_norm(x, norm1_g[i]) * (1.0 + sc1) + sh1
        j = i // 2
        if i % 2 == 0:
            y = mla_pool_mixer(h, cos_r, sin_r, mla_w_in[j], mla_q_norm_g[j], mla_kv_norm_g[j],
                               mla_w_uq[j], mla_w_ukv[j], pool_w[j], pool_b[j],
                               pool_scale[j], mix_a_w_out[j])
        else:
            y = diff_attention_mixer(h, cos_p, sin_p, diff_w_qkv[j], diff_lambda_q1[j],
                                     diff_lambda_k1[j], diff_lambda_q2[j], diff_lambda_k2[j],
                                     diff_subln_g[j], diff_w_out[j], diff_lambda_init(i))
        x = x + g1 * y
        h = rms_norm(x, norm2_g[i]) * (1.0 + sc2) + sh2
        x = x + g2 * swiglu(h, ffn_w_gate_up[i], ffn_w_down[i])
    return rms_norm(x, final_norm_g)
```

```python
import contextlib
import math
import numpy as np
import ml_dtypes
import concourse.bass as bass
import concourse.mybir as mybir
from concourse.bass_utils import run_bass_kernel_spmd

F32 = mybir.dt.float32
BF16 = mybir.dt.bfloat16
I32 = mybir.dt.int32
AF = mybir.ActivationFunctionType
ALU = mybir.AluOpType
AX = mybir.AxisListType

P = 128
S = 4096
D = 1024
NCH = D // P
TB = 512
NTB = S // TB
EPS = 1e-6


class Sem:
    def __init__(self, nc, stack, name):
        self.h = stack.enter_context(nc.semaphore(name))
        self.val = 0
        self.name = name


class Slot:
    __slots__ = ("w", "r", "name", "excl")

    def __init__(self, name="", excl=False):
        self.w = []
        self.r = []
        self.name = name
        self.excl = excl


class Queue:
    def __init__(self, name, sem):
        self.name = name
        self.sem = sem
        self.ops = []
        self.seen = {}


class Rec:
    def __init__(self, nc, stack):
        self.nc = nc
        self.stack = stack
        self.q = {}
        for n in ("pe", "act", "dve", "pool", "sp"):
            self.q[n] = Queue(n, Sem(nc, stack, "q_" + n))
        self.dma_sems = {}
        self.all_dma_toks = []

    def dsem(self, name):
        if name not in self.dma_sems:
            self.dma_sems[name] = Sem(self.nc, self.stack, "d_" + name)
        return self.dma_sems[name]

    def _deps(self, reads, writes):
        deps = []
        for s in reads:
            deps += s.w
            if s.excl:
                deps += s.r
        for s in writes:
            deps += s.w
            deps += s.r
        return deps

    def _prune(self, q, deps):
        best = {}
        for (sem, v) in deps:
            if v > q.seen.get(sem, 0) and v > best.get(sem, (None, 0))[1]:
                best[sem] = (sem, v)
        out = list(best.values())
        for (sem, v) in out:
            q.seen[sem] = v
        return out

    def op(self, qn, fn, reads=(), writes=(), extra=(), signal=True):
        q = self.q[qn]
        deps = self._deps(reads, writes) + list(extra)
        waits = self._prune(q, deps)
        tok = None
        if signal:
            q.sem.val += 1
            tok = (q.sem, q.sem.val)
            q.ops.append((waits, fn, q.sem, 1))
            for s in reads:
                s.r.append(tok)
            for s in writes:
                s.w = [tok]
                s.r = []
        else:
            q.ops.append((waits, fn, None, 0))
        return tok

    def group(self, qn, fns, reads=(), writes=(), extra=()):
        q = self.q[qn]
        deps = self._deps(reads, writes) + list(extra)
        waits = self._prune(q, deps)
        q.sem.val += 1
        tok = (q.sem, q.sem.val)
        n = len(fns)
        for i, fn in enumerate(fns):
            q.ops.append((waits if i == 0 else [], fn,
                          q.sem if i == n - 1 else None, 1))
        for s in reads:
            s.r.append(tok)
        for s in writes:
            s.w = [tok]
            s.r = []
        return tok

    def dma(self, qn, out, in_, semname, reads=(), writes=(), extra=(), append_w=False):
        q = self.q[qn]
        sem = self.dsem(semname)
        deps = self._deps(reads, [] if append_w else writes) + list(extra)
        waits = self._prune(q, deps)
        sem.val += 16
        tok = (sem, sem.val)

        def fn(eng, out=out, in_=in_):
            return eng.dma_start(out=out, in_=in_)
        q.ops.append((waits, fn, sem, 16))
        for s in reads:
            s.r.append(tok)
        for s in writes:
            if append_w:
                s.w = s.w + [tok]
            else:
                s.w = [tok]
                s.r = []
        self.all_dma_toks.append(tok)
        return tok

    def barrier(self):
        toks = [(q.sem, q.sem.val) for q in self.q.values() if q.sem.val > 0]
        toks += [(s, s.val) for s in self.dma_sems.values() if s.val > 0]
        for qn, q in self.q.items():
            waits = self._prune(q, toks)
            if waits:
                q.ops.append((waits, None, None, 0))

    def replay(self):
        nc = self.nc
        with nc.Block() as block:
            def run(q):
                def body(eng):
                    for (waits, fn, isem, amt) in q.ops:
                        for (sem, v) in waits:
                            eng.wait_ge(sem.h, v)
                        if fn is None:
                            continue
                        ins = fn(eng)
                        if isem is not None:
                            ins.then_inc(isem.h, amt)
                return body
            block.tensor(run(self.q["pe"]))
            block.scalar(run(self.q["act"]))
            block.vector(run(self.q["dve"]))
            block.gpsimd(run(self.q["pool"]))
            block.sync(run(self.q["sp"]))
        for q in self.q.values():
            q.ops = []


HQ = 8
DFF = 2816
NF = DFF // P
MAGIC = 12582912.0
C1 = 6.28125
C2 = 2.0 * math.pi - 6.28125
INV2PI = 1.0 / (2.0 * math.pi)
LAM_INIT1 = 0.8 - 0.6 * math.exp(-0.3 * 1)
DBG = {}


def build_program(n_layers=2, stop=None):
    nc = bass.Bass("TRN2", target_bir_lowering=False)

    def din(name, shape, dt=F32):
        return nc.dram_tensor(name, shape, dt, kind="ExternalInput").ap()

    def dscr(name, shape, dt):
        return nc.dram_tensor(name, shape, dt, kind="Internal").ap()

    x_in = din("x", [S, D])
    cT_in = din("cT", [P, NCH])
    pos_in = din("pos", [1, S], I32)
    ada_w = din("ada_w", [2, D, 6 * D])
    adab_in = din("adab", [2, P, 48])
    n1g_in = din("n1g", [2, P, NCH])
    n2g_in = din("n2g", [2, P, NCH])
    fng_in = din("fng", [P, NCH])
    wgu_in = din("w_gu", [2, D, 2 * DFF])
    wd_in = din("w_d", [2, DFF, D])
    win_in = din("w_in", [D, 1184])
    qng_in = din("qng", [P, 3])
    kvng_in = din("kvng", [P, 2])
    wuq_in = din("w_uq", [384, 768])
    wukv_in = din("w_ukv", [256, 1024])
    poolw_in = din("pool_w", [4, P, P])
    poolb_in = din("pool_b", [P, 4])
    pools_in = din("pool_s", [P, 4])
    woutA_in = din("w_outA", [D, D])
    wqkv_in = din("w_qkv", [D, 3 * D])
    lamv_in = din("lamv", [P, 4, 64])
    subg_in = din("subg", [P, 1])
    woutD_in = din("w_outD", [D, D])
    identf_in = din("ident_f", [P, P])
    tri_in = din("tri", [P, P])
    rmat_in = din("rmat", [2, P, P])
    invrows_in = din("invrows", [P, 2])
    cntinv_in = din("cntinv", [P, 4, 16])
    out = nc.dram_tensor("out", [S, D], F32, kind="ExternalOutput").ap()

    xT_s = dscr("xT_s", [P, NCH, S], F32)
    tab_s = dscr("tab_s", [4, P, S], F32)
    wgu_s = dscr("wgu_s", [2, NF, P, NCH * 256], BF16)
    q_s = dscr("q_s", [P, 8, S], BF16)
    k_s = dscr("k_s", [P, 8, S], BF16)
    v_s = dscr("v_s", [P, 32 * 1024], BF16)
    cat_s = dscr("cat_s", [P, NCH, S], BF16)

    with contextlib.ExitStack() as gst:
        R = Rec(nc, gst)
        _uid = [0]

        def _un(name):
            _uid[0] += 1
            return f"sb{_uid[0]}_{name}"
        gsb = lambda name, shape, dt: gst.enter_context(nc.sbuf_tensor(_un(name), shape, dt))
        psb = [gst.enter_context(nc.psum_tensor(f"psb{i}", [P, TB], F32)) for i in range(8)]
        s_psb = [Slot(f"psb{i}", excl=True) for i in range(8)]

        class Banks:
            def __init__(self, ids):
                self.ids = list(ids)
                self.i = 0

            def next(self):
                b = self.ids[self.i % len(self.ids)]
                self.i += 1
                return b

        def ACT(out_, in_, func, reads, writes, **kw):
            return R.op("act", lambda e: e.activation(out=out_, in_=in_, func=func, **kw), reads=reads, writes=writes)

        def TT(q, out_, in0, in1, op, reads, writes):
            return R.op(q, lambda e: e.tensor_tensor(out=out_, in0=in0, in1=in1, op=op), reads=reads, writes=writes)

        def TS(q, out_, in0, s1, s2, op0, op1, reads, writes):
            if op1 is None:
                return R.op(q, lambda e: e.tensor_single_scalar(out=out_, in_=in0, scalar=s1, op=op0), reads=reads, writes=writes)
            return R.op(q, lambda e: e.tensor_scalar(out=out_, in0=in0, scalar1=s1, scalar2=s2, op0=op0, op1=op1),
                        reads=reads, writes=writes)

        def STT(out_, in0, scalar, in1, op0, op1, reads, writes):
            return R.op("dve", lambda e: e.scalar_tensor_tensor(out=out_, in0=in0, scalar=scalar, in1=in1, op0=op0, op1=op1),
                        reads=reads, writes=writes)

        def CP(q, out_, in_, reads, writes):
            if q == "act":
                return R.op("act", lambda e: e.copy(out=out_, in_=in_), reads=reads, writes=writes)
            return R.op(q, lambda e: e.tensor_copy(out=out_, in_=in_), reads=reads, writes=writes)

        def MM(out_, pairs, reads, writes):
            n = len(pairs)
            fns = []
            for i, (l, r) in enumerate(pairs):
                fns.append(lambda e, l=l, r=r, i=i: e.matmul(out_, lhsT=l, rhs=r, start=(i == 0), stop=(i == n - 1)))
            return R.group("pe", fns, reads=reads, writes=writes)

        def RECIP(out_, in_, reads, writes):
            return R.op("dve", lambda e: e.reciprocal(out=out_, in_=in_), reads=reads, writes=writes)

        def LOAD(q, out_, in_, sem, writes, reads=()):
            return R.dma(q, out_, in_, sem, reads=reads, writes=writes)

        def STORE(q, out_, in_, sem, reads):
            return R.dma(q, out_, in_, sem, reads=reads, writes=[Slot()])

        identf = gsb("identf", [P, P], F32)
        ones_bf = gsb("ones_bf", [P, P], BF16)
        tri_bf = gsb("tri_bf", [P, P], BF16)
        rmat_bf = gsb("rmat_bf", [P, 2, P], BF16)
        fng = gsb("fng", [P, NCH], F32)
        AB = gsb("AB", [P, 2, 6, NCH], F32)
        halfpi = gsb("halfpi", [P, 1], F32)
        epsc = gsb("epsc", [P, 1], F32)
        lam_neg = gsb("lam_neg", [P, 1], F32)
        subg2 = gsb("subg2", [P, 1], F32)
        s_c = Slot("consts")
        s_ones = Slot("ones")
        s_AB = Slot("AB")
        s_lam = Slot("lam")
        LOAD("sp", identf[:], identf_in, "c0", [s_c])
        R.dma("sp", fng[:], fng_in, "c0", writes=[s_c], append_w=True)
        R.dma("pool", tri_bf[:], tri_in, "c1", writes=[s_c], append_w=True)
        R.dma("pool", rmat_bf[:], rmat_in.rearrange("j p m -> p j m"), "c1", writes=[s_c], append_w=True)
        R.op("pool", lambda e: e.memset(ones_bf[:], 1.0), writes=[s_ones])
        R.op("pool", lambda e: e.memset(halfpi[:], math.pi / 2.0), writes=[s_ones])
        s_ones.w = [(R.q["pool"].sem, R.q["pool"].sem.val)]
        R.op("pool", lambda e: e.memset(epsc[:], EPS), writes=[Slot()])
        s_ones.w = [(R.q["pool"].sem, R.q["pool"].sem.val)]

        def rms_rstd(sq_tile, s_sq, src_chunks, s_src, n, dim, bank, rstd_tile, s_rstd, np_=P, sq_eng="act"):
            for c in range(n):
                if sq_eng == "act" or c % 2 == 0:
                    ACT(sq_tile[:np_, c, :], src_chunks[c], AF.Square, reads=s_src[c], writes=[s_sq[c]])
                else:
                    TT("pool", sq_tile[:np_, c, :], src_chunks[c], src_chunks[c], ALU.mult, reads=s_src[c], writes=[s_sq[c]])
            MM(psb[bank][:np_, :], [(ones_bf[:np_, :np_], sq_tile[:np_, c, :]) for c in range(n)],
               reads=s_sq[:n] + [s_ones], writes=[s_psb[bank]])
            ACT(rstd_tile[:np_, :], psb[bank][:np_, :], AF.Sqrt, reads=[s_psb[bank], s_ones], writes=[s_rstd],
                scale=1.0 / dim, bias=epsc[:np_, :])
            RECIP(rstd_tile[:np_, :], rstd_tile[:np_, :], reads=[s_rstd], writes=[s_rstd])

        with contextlib.ExitStack() as st:
            sb = lambda name, shape, dt: st.enter_context(nc.sbuf_tensor(_un(name), shape, dt))
            cT = sb("cT", [P, NCH], F32)
            cond = sb("cond", [P, NCH], F32)
            adab = sb("adab", [P, 2, 48], F32)
            n1g = sb("n1g", [P, 2, NCH], F32)
            n2g = sb("n2g", [P, 2, NCH], F32)
            modt = sb("modt", [P, 2, 48], F32)
            s_in = Slot()
            s_cond = Slot()
            s_modt = Slot()
            LOAD("sp", cT[:], cT_in, "a0", [s_in])
            R.dma("sp", adab[:], adab_in.rearrange("l p m -> p l m"), "a0", writes=[s_in], append_w=True)
            R.dma("sp", n1g[:], n1g_in.rearrange("l p m -> p l m"), "a0", writes=[s_in], append_w=True)
            R.dma("sp", n2g[:], n2g_in.rearrange("l p m -> p l m"), "a0", writes=[s_in], append_w=True)
            ACT(cond[:], cT[:], AF.Silu, reads=[s_in], writes=[s_cond])
            aw = [sb(f"aw{i}", [P, NCH, 512], F32) for i in range(2)]
            s_aw = [Slot() for _ in range(2)]
            for l in range(n_layers):
                for mg in range(12):
                    i = (l * 12 + mg) % 2
                    LOAD("sp", aw[i][:], ada_w[l, :, mg * 512:(mg + 1) * 512].rearrange("(kc p) n -> p kc n", p=P),
                         f"aw{i}", [s_aw[i]])
                    for j in range(4):
                        m = mg * 4 + j
                        MM(psb[l][:, m:m + 1], [(aw[i][:, kc, j * P:(j + 1) * P], cond[:, kc:kc + 1]) for kc in range(NCH)],
                           reads=[s_aw[i], s_cond], writes=[s_psb[l]])
                TT("dve", modt[:, l, :], psb[l][:, 0:48], adab[:, l, :], ALU.add, reads=[s_psb[l], s_in], writes=[s_modt])
                STT(AB[:, l, 0, :], modt[:, l, 8:16], 1.0, n1g[:, l, :], ALU.add, ALU.mult, reads=[s_modt, s_in], writes=[s_AB])
                CP("dve", AB[:, l, 1, :], modt[:, l, 0:8], reads=[s_modt], writes=[s_AB])
                CP("dve", AB[:, l, 2, :], modt[:, l, 16:24], reads=[s_modt], writes=[s_AB])
                STT(AB[:, l, 3, :], modt[:, l, 32:40], 1.0, n2g[:, l, :], ALU.add, ALU.mult, reads=[s_modt, s_in], writes=[s_AB])
                CP("dve", AB[:, l, 4, :], modt[:, l, 24:32], reads=[s_modt], writes=[s_AB])
                CP("dve", AB[:, l, 5, :], modt[:, l, 40:48], reads=[s_modt], writes=[s_AB])
            lamv = sb("lamv", [P, 4, 64], F32)
            lprod = sb("lprod", [P, 2, 64], F32)
            lsum = sb("lsum", [P, 2], F32)
            subg = sb("subg", [P, 1], F32)
            s_lv = Slot()
            s_lp = Slot()
            LOAD("sp", lamv[:], lamv_in, "a1", [s_lv])
            R.dma("sp", subg[:], subg_in, "a1", writes=[s_lv], append_w=True)
            TT("dve", lprod[:, 0, :], lamv[:, 0, :], lamv[:, 1, :], ALU.mult, reads=[s_lv], writes=[s_lp])
            TT("dve", lprod[:, 1, :], lamv[:, 2, :], lamv[:, 3, :], ALU.mult, reads=[s_lv], writes=[s_lp])
            R.op("dve", lambda e: e.reduce_sum(out=lsum[:], in_=lprod[:], axis=AX.X), reads=[s_lp], writes=[s_lp])
            ACT(lsum[:], lsum[:], AF.Exp, reads=[s_lp], writes=[s_lp])
            TT("dve", lam_neg[:], lsum[:, 1:2], lsum[:, 0:1], ALU.subtract, reads=[s_lp], writes=[s_lam])
            TS("dve", lam_neg[:], lam_neg[:], -LAM_INIT1, None, ALU.add, None, reads=[s_lam], writes=[s_lam])
            TS("dve", subg2[:], subg[:], 1.0 - LAM_INIT1, None, ALU.mult, None, reads=[s_lv], writes=[s_lam])

            posi = sb("posi", [P, S], I32)
            posf = sb("posf", [P, S], F32)
            invrows = sb("invrows", [P, 2], F32)
            ang = sb("ang", [P, S], F32)
            kk = sb("kk", [P, S], F32)
            rr = sb("rr", [P, S], F32)
            tS = sb("tS", [P, S], F32)
            tC = sb("tC", [P, S], F32)
            s_pos, s_ang, s_kk, s_rr, s_tS, s_tC = Slot(), Slot(), Slot(), Slot(), Slot(), Slot()
            LOAD("sp", posi[:], pos_in.broadcast_to([P, S]), "a2", [s_pos])
            R.dma("sp", invrows[:], invrows_in, "a2", writes=[s_pos], append_w=True)
            CP("dve", posf[:], posi[:], reads=[s_pos], writes=[s_pos])
            for j in range(2):
                TS("dve", ang[:], posf[:], invrows[:, j:j + 1], None, ALU.mult, None, reads=[s_pos], writes=[s_ang])
                TS("dve", kk[:], ang[:], INV2PI, MAGIC, ALU.mult, ALU.add, reads=[s_ang], writes=[s_kk])
                TS("dve", kk[:], kk[:], -MAGIC, None, ALU.add, None, reads=[s_kk], writes=[s_kk])
                STT(rr[:], kk[:], -C1, ang[:], ALU.mult, ALU.add, reads=[s_kk, s_ang], writes=[s_rr])
                STT(rr[:], kk[:], -C2, rr[:], ALU.mult, ALU.add, reads=[s_kk, s_rr], writes=[s_rr])
                ACT(tS[:], rr[:], AF.Sin, reads=[s_rr], writes=[s_tS])
                ACT(rr[:], rr[:], AF.Abs, reads=[s_rr], writes=[s_rr])
                ACT(tC[:], rr[:], AF.Sin, reads=[s_rr, s_ones], writes=[s_tC], scale=-1.0, bias=halfpi[:])
                STORE("sp", tab_s[2 * j], tC[:], "a3", reads=[s_tC])
                STORE("sp", tab_s[2 * j + 1], tS[:], "a3", reads=[s_tS])
            R.barrier()
            R.replay()
        if stop == "A0":
            return nc

        def wgu_prep(l, stg, s_stg, flist):
            for f in flist:
                i = f % len(stg)
                for half in range(2):
                    src = wgu_in[l, :, half * DFF + f * P: half * DFF + (f + 1) * P].rearrange("(kc p) n -> p kc n", p=P)
                    R.dma("pool", stg[i][:, :, half * P:(half + 1) * P], src, f"wgst{i}",
                          writes=[s_stg[i]], append_w=(half == 1))
                R.dma("pool", wgu_s[l, f], stg[i][:].rearrange("p kc n -> p (kc n)"), f"wgsto{i}", reads=[s_stg[i]], writes=[Slot()])

        with contextlib.ExitStack() as st:
            sb = lambda name, shape, dt: st.enter_context(nc.sbuf_tensor(_un(name), shape, dt))
            stg = [sb(f"stg{i}", [P, NCH, 256], BF16) for i in range(3)]
            s_stg = [Slot() for _ in range(3)]
            xin = [sb(f"xin{i}", [P, 4, D], F32) for i in range(2)]
            s_xin = [Slot() for _ in range(2)]
            xT = [sb(f"xT{i}", [P, NCH, TB], F32) for i in range(2)]
            s_xT = [[Slot() for c in range(NCH)] for i in range(2)]
            PB = Banks(range(8))
            for tb in range(NTB):
                b = tb % 2
                LOAD("sp", xin[b][:], x_in[tb * TB:(tb + 1) * TB, :].rearrange("(t p) d -> p t d", p=P), f"xin{b}", [s_xin[b]])
                for c in range(NCH):
                    k = PB.next()
                    fns = [lambda e, k=k, t=t, c=c, b=b: e.transpose(
                        out=psb[k][:, t * P:(t + 1) * P], in_=xin[b][:, t, c * P:(c + 1) * P], identity=identf[:])
                        for t in range(4)]
                    R.group("pe", fns, reads=[s_xin[b], s_c], writes=[s_psb[k]])
                    CP("act" if c % 2 == 0 else "dve", xT[b][:, c, :], psb[k][:], reads=[s_psb[k]], writes=[s_xT[b][c]])
                STORE("sp", xT_s[:, :, tb * TB:(tb + 1) * TB], xT[b][:], f"xTst{b}", reads=s_xT[b])
                wgu_prep(0, stg, s_stg, range(tb * 3, min(NF, tb * 3 + 3)))
            R.barrier()
            R.replay()
        if stop == "X0":
            return nc

        def load_tab(tile, j, tb, sem, slot):
            LOAD("sp", tile[:], tab_s[j, :, tb * TB:(tb + 1) * TB], sem, [slot])

        def ffn_block(l, tb, xb, s_xb, cat, s_cat, wout, s_wout, wd, s_wd, sq, s_sq, rstd, s_rstd, tmp, s_tmp,
                      h2, s_h2, actb, s_act, sg, s_sg, wg, s_wg, PB, wgi):
            for c in range(NCH):
                k = PB.next()
                MM(psb[k][:], [(wout[:, kc, c * P:(c + 1) * P], cat[:, kc, :]) for kc in range(NCH)],
                   reads=s_cat + [s_wout], writes=[s_psb[k]])
                STT(xb[:, c, :], psb[k][:], AB[:, l, 2, c:c + 1], xb[:, c, :], ALU.mult, ALU.add,
                    reads=[s_psb[k], s_AB], writes=[s_xb[c]])
            k = PB.next()
            rms_rstd(sq, s_sq, [xb[:, c, :] for c in range(NCH)], [[s_xb[c]] for c in range(NCH)], NCH, D, k, rstd, s_rstd,
                     sq_eng="mix")
            for c in range(NCH):
                t = c % 2
                STT(tmp[t][:], xb[:, c, :], AB[:, l, 3, c:c + 1], rstd[:], ALU.mult, ALU.mult,
                    reads=[s_xb[c], s_rstd, s_AB], writes=[s_tmp[t]])
                ACT(h2[:, c, :], tmp[t][:], AF.Identity, reads=[s_tmp[t], s_AB], writes=[s_h2[c]], bias=AB[:, l, 4, c:c + 1], scale=1.0)
            for f in range(NF):
                i = wgi[0] % len(wg)
                wgi[0] += 1
                LOAD("sp", wg[i][:].rearrange("p kc n -> p (kc n)"), wgu_s[l, f], f"wg{i}", [s_wg[i]])
                kg = PB.next()
                ku = PB.next()
                MM(psb[kg][:], [(wg[i][:, kc, 0:P], h2[:, kc, :]) for kc in range(NCH)], reads=s_h2 + [s_wg[i]], writes=[s_psb[kg]])
                MM(psb[ku][:], [(wg[i][:, kc, P:2 * P], h2[:, kc, :]) for kc in range(NCH)], reads=s_h2 + [s_wg[i]], writes=[s_psb[ku]])
                t = f % 2
                ACT(sg[t][:], psb[kg][:], AF.Silu, reads=[s_psb[kg]], writes=[s_sg[t]])
                TT("dve", actb[:, f, :], sg[t][:], psb[ku][:], ALU.mult, reads=[s_sg[t], s_psb[ku]], writes=[s_act[f]])
            for c in range(NCH):
                k = PB.next()
                MM(psb[k][:], [(wd[:, f, c * P:(c + 1) * P], actb[:, f, :]) for f in range(NF)], reads=s_act + [s_wd], writes=[s_psb[k]])
                STT(xb[:, c, :], psb[k][:], AB[:, l, 5, c:c + 1], xb[:, c, :], ALU.mult, ALU.add,
                    reads=[s_psb[k], s_AB], writes=[s_xb[c]])

        def norm1_block(l, xb, s_xb, sq, s_sq, rstd, s_rstd, tmp, s_tmp, h, s_h, PB):
            k = PB.next()
            rms_rstd(sq, s_sq, [xb[:, c, :] for c in range(NCH)], [[s_xb]] * NCH, NCH, D, k, rstd, s_rstd, sq_eng="mix")
            for c in range(NCH):
                t = c % 2
                STT(tmp[t][:], xb[:, c, :], AB[:, l, 0, c:c + 1], rstd[:], ALU.mult, ALU.mult,
                    reads=[s_xb, s_rstd, s_AB], writes=[s_tmp[t]])
                ACT(h[:, c, :], tmp[t][:], AF.Identity, reads=[s_tmp[t], s_AB], writes=[s_h[c]], bias=AB[:, l, 1, c:c + 1], scale=1.0)

        def post_phase(l, woutX_in, final):
            with contextlib.ExitStack() as st:
                sb = lambda name, shape, dt: st.enter_context(nc.sbuf_tensor(_un(name), shape, dt))
                wd = sb("wd", [P, NF, D], BF16)
                wout = sb("wout", [P, NCH, D], BF16)
                s_wd, s_wout = Slot(), Slot()
                LOAD("pool", wout[:], woutX_in.rearrange("(kc p) n -> p kc n", p=P), "wout", [s_wout])
                LOAD("pool", wd[:], wd_in[l].rearrange("(f p) n -> p f n", p=P), "wd", [s_wd])
                wg = [sb(f"wg{i}", [P, NCH, 256], BF16) for i in range(4)]
                s_wg = [Slot() for _ in range(4)]
                xb = [sb(f"xb{i}", [P, NCH, TB], F32) for i in range(2)]
                s_xb = [[Slot() for c in range(NCH)] for i in range(2)]
                cat = [sb(f"cat{i}", [P, NCH, TB], BF16) for i in range(2)]
                s_cat = [Slot() for _ in range(2)]
                sq = sb("sq", [P, NCH, TB], BF16)
                s_sq = [Slot() for _ in range(NCH)]
                rstd = sb("rstd", [P, TB], F32)
                s_rstd = Slot()
                tmp = [sb(f"tmp{i}", [P, TB], F32) for i in range(2)]
                s_tmp = [Slot() for _ in range(2)]
                h2 = sb("h2", [P, NCH, TB], BF16)
                s_h2 = [Slot() for _ in range(NCH)]
                actb = sb("actb", [P, NF, TB], BF16)
                s_act = [Slot() for _ in range(NF)]
                sg = [sb(f"sg{i}", [P, TB], F32) for i in range(2)]
                s_sg = [Slot() for _ in range(2)]
                if final:
                    ot = sb("ot", [P, 4, D], F32)
                    s_ot = [Slot() for _ in range(8)]
                    on = sb("on", [P, NCH, TB], F32)
                    s_on = [Slot() for _ in range(NCH)]
                PB = Banks(range(8))
                wgi = [0]
                for tb in range(NTB):
                    b = tb % 2
                    R.dma("sp", xb[b][:], xT_s[:, :, tb * TB:(tb + 1) * TB], f"xb{b}", writes=s_xb[b])
                    LOAD("sp", cat[b][:], cat_s[:, :, tb * TB:(tb + 1) * TB], f"cat{b}", [s_cat[b]])
                    ffn_block(l, tb, xb[b], s_xb[b], cat[b], [s_cat[b]], wout, s_wout, wd, s_wd, sq, s_sq, rstd, s_rstd,
                              tmp, s_tmp, h2, s_h2, actb, s_act, sg, s_sg, wg, s_wg, PB, wgi)
                    if not final:
                        STORE("sp", xT_s[:, :, tb * TB:(tb + 1) * TB], xb[b][:], f"xbst{b}", reads=s_xb[b])
                    else:
                        k = PB.next()
                        rms_rstd(sq, s_sq, [xb[b][:, c, :] for c in range(NCH)], [[s_xb[b][c]] for c in range(NCH)], NCH, D, k,
                                 rstd, s_rstd, sq_eng="mix")
                        for c in range(NCH):
                            STT(on[:, c, :], xb[b][:, c, :], fng[:, c:c + 1], rstd[:], ALU.mult, ALU.mult,
                                reads=[s_xb[b][c], s_rstd, s_c], writes=[s_on[c]])
                        for t in range(4):
                            for half in range(2):
                                k = PB.next()
                                fns = [lambda e, k=k, t=t, j=j, half=half: e.transpose(
                                    out=psb[k][:, j * P:(j + 1) * P], in_=on[:, half * 4 + j, t * P:(t + 1) * P],
                                    identity=identf[:]) for j in range(4)]
                                R.group("pe", fns, reads=s_on[half * 4:half * 4 + 4] + [s_c], writes=[s_psb[k]])
                                CP("act", ot[:, t, half * TB:(half + 1) * TB], psb[k][:], reads=[s_psb[k]], writes=[s_ot[t * 2 + half]])
                        STORE("sp", out[tb * TB:(tb + 1) * TB, :].rearrange("(t p) d -> p t d", p=P), ot[:], "ost", reads=s_ot)
                R.barrier()
                R.replay()

        SC0 = 96.0 ** -0.5
        with contextlib.ExitStack() as st:
            sb = lambda name, shape, dt: st.enter_context(nc.sbuf_tensor(_un(name), shape, dt))
            win = sb("win", [P, NCH, 1184], BF16)
            wuq = sb("wuq", [P, 3, 800], BF16)
            wukv = sb("wukv", [P, 2, 1024], BF16)
            poolw = sb("poolw", [P, 4, P], BF16)
            qng = sb("qng", [P, 3], F32)
            kvng = sb("kvng", [P, 2], F32)
            poolb = sb("poolb", [P, 4], F32)
            pools = sb("pools", [P, 4], F32)
            cntinv = sb("cntinv", [P, 4, 16], F32)
            s_w = Slot()
            LOAD("pool", win[:], win_in.rearrange("(kc p) n -> p kc n", p=P), "w0", [s_w])
            s_wq = Slot()
            R.op("pool", lambda e: e.memset(wuq[:, :, 768:800], 0.0), writes=[s_wq])
            R.dma("pool", wuq[:, :, 0:768], wuq_in.rearrange("(kc p) n -> p kc n", p=P), "w0", writes=[s_w], append_w=True)
            R.dma("pool", wukv[:], wukv_in.rearrange("(kc p) n -> p kc n", p=P), "w0", writes=[s_w], append_w=True)
            R.dma("pool", poolw[:], poolw_in.rearrange("g c d -> c g d"), "w0", writes=[s_w], append_w=True)
            R.dma("sp", qng[:], qng_in, "w1", writes=[s_w], append_w=True)
            R.dma("sp", kvng[:], kvng_in, "w1", writes=[s_w], append_w=True)
            R.dma("sp", poolb[:], poolb_in, "w1", writes=[s_w], append_w=True)
            R.dma("sp", pools[:], pools_in, "w1", writes=[s_w], append_w=True)
            R.dma("sp", cntinv[:], cntinv_in, "w1", writes=[s_w], append_w=True)
            xb = [sb(f"xb{i}", [P, NCH, TB], F32) for i in range(2)]
            s_xb = [Slot() for _ in range(2)]
            tabC = [sb(f"tabC{i}", [P, TB], F32) for i in range(2)]
            tabS = [sb(f"tabS{i}", [P, TB], F32) for i in range(2)]
            s_tab = [Slot() for _ in range(2)]
            sq = sb("sq", [P, NCH, TB], BF16)
            s_sq = [Slot() for _ in range(NCH)]
            rstd = sb("rstd", [P, TB], F32)
            s_rstd = Slot()
            rstq = sb("rstq", [P, TB], F32)
            s_rstq = Slot()
            rstk = sb("rstk", [P, TB], F32)
            s_rstk = Slot()
            tmp = [sb(f"tmp{i}", [P, TB], F32) for i in range(2)]
            s_tmp = [Slot() for _ in range(2)]
            h = sb("h", [P, NCH, TB], BF16)
            s_h = [Slot() for _ in range(NCH)]
            cq = sb("cq", [P, 5, TB], F32)
            s_cq = [Slot() for _ in range(5)]
            cn = sb("cn", [P, 5, TB], BF16)
            s_cn = [Slot() for _ in range(5)]
            u = sb("u", [P, 4, 16 + TB], F32)
            s_u = [Slot() for _ in range(4)]
            lv = [sb(f"lv{i}", [P, 16 + TB], F32) for i in range(2)]
            s_lv = [Slot() for _ in range(2)]
            pooled = sb("pooled", [P, 4, TB], BF16)
            s_pl = [Slot() for _ in range(4)]
            ptmp = sb("ptmp", [P, 16], F32)
            s_ptmp = Slot()
            qb = [sb(f"qb{i}", [P, TB], BF16) for i in range(2)]
            s_qb = [Slot() for _ in range(2)]
            t1 = [sb(f"t1{i}", [P, TB], F32) for i in range(2)]
            s_t1 = [Slot() for _ in range(2)]
            t2 = [sb(f"t2{i}", [P, TB], F32) for i in range(2)]
            s_t2 = [Slot() for _ in range(2)]
            qT = [sb(f"qT{i}", [P, 8, TB], BF16) for i in range(2)]
            s_qT = [[Slot() for hh in range(8)] for _ in range(2)]
            kT = [sb(f"kT{i}", [P, 8, TB], BF16) for i in range(2)]
            s_kT = [[Slot() for hh in range(8)] for _ in range(2)]
            s_kTr = [[Slot() for hh in range(8)] for _ in range(2)]
            vt = [sb(f"vt{i}", [P, 4, 8, 65], BF16) for i in range(2)]
            s_vt = [[Slot() for tt in range(4)] for _ in range(2)]
            catp = [sb(f"catp{i}", [P, 4, TB], BF16) for i in range(2)]
            s_catp = [[Slot() for g in range(4)] for _ in range(2)]
            for i in range(2):
                R.op("pool", lambda e, i=i: e.memset(vt[i][:], 1.0), writes=s_vt[i])
            R.op("pool", lambda e: e.memset(u[:], 0.0), writes=s_u)
            PB = Banks(range(8))
            for tb in range(DBG.get('l0p1_ntb', NTB)):
                b = tb % 2
                LOAD("sp", xb[b][:], xT_s[:, :, tb * TB:(tb + 1) * TB], f"xb{b}", [s_xb[b]])
                R.dma("sp", tabC[b][:], tab_s[0, :, tb * TB:(tb + 1) * TB], f"tab{b}", writes=[s_tab[b]])
                R.dma("sp", tabS[b][:], tab_s[1, :, tb * TB:(tb + 1) * TB], f"tab{b}", writes=[s_tab[b]], append_w=True)
                norm1_block(0, xb[b], s_xb[b], sq, s_sq, rstd, s_rstd, tmp, s_tmp, h, s_h, PB)
                if DBG.get('sec', 99) < 1:
                    continue
                for j in range(5):
                    k = PB.next()
                    MM(psb[k][:], [(win[:, kc, j * P:(j + 1) * P], h[:, kc, :]) for kc in range(NCH)], reads=s_h + [s_w], writes=[s_psb[k]])
                    CP("act", cq[:, j, :], psb[k][:], reads=[s_psb[k]], writes=[s_cq[j]])
                k = PB.next()
                rms_rstd(sq, s_sq, [cq[:, j, :] for j in range(3)], [[s_cq[j]] for j in range(3)], 3, 384, k, rstq, s_rstq, sq_eng="mix")
                for j in range(3):
                    STT(cn[:, j, :], cq[:, j, :], qng[:, j:j + 1], rstq[:], ALU.mult, ALU.mult, reads=[s_cq[j], s_rstq, s_w], writes=[s_cn[j]])
                k = PB.next()
                rms_rstd(sq[:, 3:5, :], s_sq[3:5], [cq[:, 3 + j, :] for j in range(2)], [[s_cq[3 + j]] for j in range(2)], 2, 256, k, rstk, s_rstk, sq_eng="mix")
                for j in range(2):
                    STT(cn[:, 3 + j, :], cq[:, 3 + j, :], kvng[:, j:j + 1], rstk[:], ALU.mult, ALU.mult,
                        reads=[s_cq[3 + j], s_rstk, s_w], writes=[s_cn[3 + j]])
                if DBG.get('sec', 99) < 2:
                    continue
                k = PB.next()
                MM(psb[k][64:96, :], [(win[:, kc, 640:672], h[:, kc, :]) for kc in range(NCH)], reads=s_h + [s_w], writes=[s_psb[k]])
                i2 = 0
                CP("act", qb[i2][64:96, :], psb[k][64:96, :], reads=[s_psb[k]], writes=[s_qb[i2]])
                k2 = PB.next()
                MM(psb[k2][64:96, :], [(rmat_bf[64:96, 0, 64:96], qb[i2][64:96, :])], reads=[s_qb[i2], s_c], writes=[s_psb[k2]])
                TT("dve", t1[i2][64:96, :], psb[k][64:96, :], tabC[b][64:96, :], ALU.mult, reads=[s_psb[k], s_tab[b]], writes=[s_t1[i2]])
                TT("dve", t2[i2][64:96, :], psb[k2][64:96, :], tabS[b][64:96, :], ALU.mult, reads=[s_psb[k2], s_tab[b]], writes=[s_t2[i2]])
                TT("pool", t1[i2][64:96, :], t1[i2][64:96, :], t2[i2][64:96, :], ALU.add, reads=[s_t2[i2]], writes=[s_t1[i2]])
                for hh in range(8):
                    CP("pool" if hh % 2 == 0 else "act", kT[b][64:96, hh, :], t1[i2][64:96, :], reads=[s_t1[i2]], writes=[s_kTr[b][hh]])
                if DBG.get('sec', 99) < 3:
                    continue
                for g in range(4):
                    k = PB.next()
                    MM(psb[k][:], [(win[:, kc, 672 + g * P:672 + (g + 1) * P], h[:, kc, :]) for kc in range(NCH)], reads=s_h + [s_w], writes=[s_psb[k]])
                    CP("act", u[:, g, 16:16 + TB], psb[k][:], reads=[s_psb[k]], writes=[s_u[g]])
                if DBG.get('sec', 99) < 4:
                    continue
                SUB = DBG.get('sub', 99)
                for hh in range(8):
                    k = PB.next()
                    i2 = hh % 2
                    MM(psb[k][:], [(wuq[:, kc, hh * 96:hh * 96 + P], cn[:, kc, :]) for kc in range(3)], reads=s_cn[0:3] + [s_w, s_wq], writes=[s_psb[k]])
                    CP("act", qb[i2][:], psb[k][:], reads=[s_psb[k]], writes=[s_qb[i2]])
                    if SUB < 1:
                        continue
                    k2 = PB.next()
                    MM(psb[k2][:], [(rmat_bf[:, 0, :], qb[i2][:])], reads=[s_qb[i2], s_c], writes=[s_psb[k2]])
                    if SUB < 2:
                        continue
                    TT("dve", t1[i2][:], psb[k][:], tabC[b][:], ALU.mult, reads=[s_psb[k], s_tab[b]], writes=[s_t1[i2]])
                    TT("dve", t2[i2][:], psb[k2][:], tabS[b][:], ALU.mult, reads=[s_psb[k2], s_tab[b]], writes=[s_t2[i2]])
                    if SUB < 3:
                        continue
                    TT("pool", qT[b][:, hh, :], t1[i2][:], t2[i2][:], ALU.add, reads=[s_t1[i2], s_t2[i2]], writes=[s_qT[b][hh]])
                if SUB >= 4:
                    STORE("sp", q_s[0:96, :, tb * TB:(tb + 1) * TB], qT[b][0:96, :, :], f"qst{b}", reads=s_qT[b])
                if DBG.get('sec', 99) < 5:
                    continue
                for hh in range(8):
                    k = PB.next()
                    MM(psb[k][0:64, :], [(wukv[:, kc, hh * 128:hh * 128 + 64], cn[:, 3 + kc, :]) for kc in range(2)], reads=s_cn[3:5] + [s_w], writes=[s_psb[k]])
                    CP("act" if hh % 2 == 0 else "dve", kT[b][0:64, hh, :], psb[k][0:64, :], reads=[s_psb[k]], writes=[s_kT[b][hh]])
                STORE("sp", k_s[0:96, :, tb * TB:(tb + 1) * TB], kT[b][0:96, :, :], f"kst{b}", reads=s_kT[b] + s_kTr[b])
                if DBG.get('sec', 99) < 6:
                    continue
                for tt in range(4):
                    k = PB.next()
                    MM(psb[k][:].rearrange("p (h e) -> p h e", e=64), [(cn[:, 3 + kc, tt * P:(tt + 1) * P],
                                    wukv[:, kc, :].rearrange("p (h e) -> p h e", e=128)[:, :, 64:128]) for kc in range(2)],
                       reads=s_cn[3:5] + [s_w], writes=[s_psb[k]])
                    CP("dve" if tt % 2 == 0 else "act", vt[b][:, tt, :, 0:64], psb[k][:].rearrange("p (h e) -> p h e", e=64),
                       reads=[s_psb[k]], writes=[s_vt[b][tt]])
                STORE("sp", v_s[:, tb * 4 * 520:(tb + 1) * 4 * 520], vt[b][:].rearrange("p t h e -> p (t h e)"), f"vst{b}", reads=s_vt[b])
                if DBG.get('sec', 99) < 7:
                    continue
                for g in range(4):
                    w = 2 << g
                    src = u[:, g, :]
                    s_src = s_u[g]
                    sh = 1
                    lvl = 0
                    while sh < w:
                        dst = lv[lvl % 2]
                        TT("pool", dst[:, sh:16 + TB], src[:, sh:16 + TB], src[:, 0:16 + TB - sh], ALU.add,
                           reads=[s_src], writes=[s_lv[lvl % 2]])
                        src = dst
                        s_src = s_lv[lvl % 2]
                        sh *= 2
                        lvl += 1
                    STT(pooled[:, g, :], src[:, 16:16 + TB], 1.0 / w, u[:, g, 16:16 + TB], ALU.mult, ALU.subtract,
                        reads=[s_src, s_u[g]], writes=[s_pl[g]])
                    if tb == 0:
                        TT("dve", ptmp[:], src[:, 16:32], cntinv[:, g, :], ALU.mult, reads=[s_src, s_w], writes=[s_ptmp])
                        TT("dve", pooled[:, g, 0:16], ptmp[:], u[:, g, 16:32], ALU.subtract, reads=[s_ptmp, s_u[g], s_pl[g]], writes=[s_pl[g]])
                    k = PB.next()
                    MM(psb[k][:], [(poolw[:, g, :], pooled[:, g, :])], reads=[s_pl[g], s_w], writes=[s_psb[k]])
                    TS("dve", catp[b][:, g, :], psb[k][:], poolb[:, g:g + 1], pools[:, g:g + 1], ALU.add, ALU.mult,
                       reads=[s_psb[k], s_w], writes=[s_catp[b][g]])
                    CP("pool", u[:, g, 0:16], u[:, g, TB:TB + 16], reads=[], writes=[s_u[g]])
                STORE("sp", cat_s[:, 4:8, tb * TB:(tb + 1) * TB], catp[b][:], f"cpst{b}", reads=s_catp[b])
            R.barrier()
            R.replay()
        if stop == "L0P1":
            return nc

        with contextlib.ExitStack() as st:
            sb = lambda name, shape, dt: st.enter_context(nc.sbuf_tensor(_un(name), shape, dt))
            stg = [sb(f"stg{i}", [P, NCH, 256], BF16) for i in range(3)]
            s_stg = [Slot() for _ in range(3)]
            va = sb("va", [P, 32, 8, 65], BF16)
            s_va = Slot()
            LOAD("sp", va[:].rearrange("p t h e -> p (t h e)"), v_s[:, 0:32 * 520], "va", [s_va])
            vg = sb("vg", [P, 32, 8, P], BF16)
            s_vgh = [Slot() for _ in range(8)]
            R.op("pool", lambda e: e.memset(vg[:], 1.0), writes=s_vgh)
            for hh in range(8):
                off = 0 if hh % 2 == 0 else 64
                CP("pool" if hh % 2 == 0 else "dve", vg[:, :, hh, off:off + 64], va[:, :, hh, 0:64], reads=[s_va], writes=[s_vgh[hh]])
            qh = [sb(f"qh{i}", [P, S], BF16) for i in range(2)]
            kh = [sb(f"kh{i}", [P, S], BF16) for i in range(2)]
            s_qk = [Slot() for _ in range(2)]
            pt = [sb(f"pt{i}", [P, TB], BF16) for i in range(4)]
            s_pt = [Slot() for _ in range(4)]
            rinv = [sb(f"rinv{i}", [P, TB], F32) for i in range(2)]
            s_rinv = [Slot() for _ in range(2)]
            attn = sb("attn", [P, 4, S], BF16)
            s_attn = [Slot() for _ in range(4)]
            SB_ = Banks([0, 1, 2, 3])
            OB = Banks([4, 5])
            pti = 0
            prep_done = 0
            for hh in range(8):
                i = hh % 2
                LOAD("sp", qh[i][0:96, :], q_s[0:96, hh, :], f"qh{i}", [s_qk[i]])
                R.dma("sp", kh[i][0:96, :], k_s[0:96, hh, :], f"qh{i}", writes=[s_qk[i]], append_w=True)
                odd = hh % 2
                for qblk in range(NTB):
                    ob = OB.next()
                    nkt = 4 * qblk + 4
                    for kt in range(nkt):
                        j = kt - 4 * qblk
                        c0 = max(j, 0) * P
                        sbk = SB_.next()
                        MM(psb[sbk][:, c0:TB], [(kh[i][0:96, kt * P:(kt + 1) * P], qh[i][0:96, qblk * TB + c0:(qblk + 1) * TB])],
                           reads=[s_qk[i]], writes=[s_psb[sbk]])
                        p_i = pti % 4
                        pti += 1
                        ACT(pt[p_i][:, c0:TB], psb[sbk][:, c0:TB], AF.Exp, reads=[s_psb[sbk]], writes=[s_pt[p_i]], scale=SC0)
                        if j >= 0:
                            TT("pool", pt[p_i][:, c0:c0 + P], pt[p_i][:, c0:c0 + P], tri_bf[:], ALU.mult, reads=[s_pt[p_i], s_c], writes=[s_pt[p_i]])
                        R.group("pe", [lambda e, ob=ob, kt=kt, hh=hh, p_i=p_i, c0=c0, nkt=nkt: e.matmul(
                            psb[ob][:, c0:TB], lhsT=vg[:, kt, hh, :], rhs=pt[p_i][:, c0:TB], start=(kt == 0), stop=(kt == nkt - 1))],
                            reads=[s_pt[p_i], s_vgh[hh]], writes=[s_psb[ob]])
                    ri = (hh * NTB + qblk) % 2
                    if odd == 0:
                        RECIP(rinv[ri][0:64, :], psb[ob][64:128, :], reads=[s_psb[ob]], writes=[s_rinv[ri]])
                        TT("dve", attn[0:64, hh // 2, qblk * TB:(qblk + 1) * TB], psb[ob][0:64, :], rinv[ri][0:64, :], ALU.mult,
                           reads=[s_psb[ob], s_rinv[ri]], writes=[s_attn[hh // 2]])
                    else:
                        RECIP(rinv[ri][64:128, :], psb[ob][0:64, :], reads=[s_psb[ob]], writes=[s_rinv[ri]])
                        TT("dve", attn[64:128, hh // 2, qblk * TB:(qblk + 1) * TB], psb[ob][64:128, :], rinv[ri][64:128, :], ALU.mult,
                           reads=[s_psb[ob], s_rinv[ri]], writes=[s_attn[hh // 2]])
                    if n_layers > 1 and prep_done < NF and (hh * NTB + qblk) % 2 == 0:
                        wgu_prep(1, stg, s_stg, [prep_done])
                        prep_done += 1
                if odd:
                    STORE("sp", cat_s[:, hh // 2, :], attn[:, hh // 2, :], "ast", reads=[s_attn[hh // 2]])
            R.barrier()
            R.replay()
        if stop == "L0P2":
            return nc

        post_phase(0, woutA_in, final=(n_layers == 1))

        if n_layers > 1:
            SC1 = 64.0 ** -0.5
            with contextlib.ExitStack() as st:
                sb = lambda name, shape, dt: st.enter_context(nc.sbuf_tensor(_un(name), shape, dt))
                wqkv = sb("wqkv", [P, NCH, 3 * D], BF16)
                s_w = Slot()
                for j in range(3):
                    R.dma("pool", wqkv[:, :, j * D:(j + 1) * D], wqkv_in[:, j * D:(j + 1) * D].rearrange("(kc p) n -> p kc n", p=P),
                          "w0", writes=[s_w], append_w=(j > 0))
                xb = [sb(f"xb{i}", [P, NCH, TB], F32) for i in range(2)]
                s_xb = [Slot() for _ in range(2)]
                tabC = [sb(f"tabC{i}", [P, TB], F32) for i in range(2)]
                tabS = [sb(f"tabS{i}", [P, TB], F32) for i in range(2)]
                s_tab = [Slot() for _ in range(2)]
                sq = sb("sq", [P, NCH, TB], BF16)
                s_sq = [Slot() for _ in range(NCH)]
                rstd = sb("rstd", [P, TB], F32)
                s_rstd = Slot()
                tmp = [sb(f"tmp{i}", [P, TB], F32) for i in range(2)]
                s_tmp = [Slot() for _ in range(2)]
                h = sb("h", [P, NCH, TB], BF16)
                s_h = [Slot() for _ in range(NCH)]
                qb = [sb(f"qb{i}", [P, TB], BF16) for i in range(2)]
                s_qb = [Slot() for _ in range(2)]
                t1 = [sb(f"t1{i}", [P, TB], F32) for i in range(2)]
                s_t1 = [Slot() for _ in range(2)]
                t2 = [sb(f"t2{i}", [P, TB], F32) for i in range(2)]
                s_t2 = [Slot() for _ in range(2)]
                qk = [sb(f"qk{i}", [P, 16, TB], BF16) for i in range(2)]
                s_qkT = [[Slot() for c in range(16)] for _ in range(2)]
                vt = [sb(f"vt{i}", [P, 4, D], BF16) for i in range(2)]
                s_vt = [[Slot() for c in range(8)] for _ in range(2)]
                PB = Banks(range(8))
                for tb in range(NTB):
                    b = tb % 2
                    LOAD("sp", xb[b][:], xT_s[:, :, tb * TB:(tb + 1) * TB], f"xb{b}", [s_xb[b]])
                    R.dma("sp", tabC[b][:], tab_s[2, :, tb * TB:(tb + 1) * TB], f"tab{b}", writes=[s_tab[b]])
                    R.dma("sp", tabS[b][:], tab_s[3, :, tb * TB:(tb + 1) * TB], f"tab{b}", writes=[s_tab[b]], append_w=True)
                    norm1_block(1, xb[b], s_xb[b], sq, s_sq, rstd, s_rstd, tmp, s_tmp, h, s_h, PB)
                    for c in range(16):
                        k = PB.next()
                        i2 = c % 2
                        MM(psb[k][:], [(wqkv[:, kc, c * P:(c + 1) * P], h[:, kc, :]) for kc in range(NCH)], reads=s_h + [s_w], writes=[s_psb[k]])
                        CP("act", qb[i2][:], psb[k][:], reads=[s_psb[k]], writes=[s_qb[i2]])
                        k2 = PB.next()
                        MM(psb[k2][:], [(rmat_bf[:, 1, :], qb[i2][:])], reads=[s_qb[i2], s_c], writes=[s_psb[k2]])
                        TT("dve", t1[i2][:], psb[k][:], tabC[b][:], ALU.mult, reads=[s_psb[k], s_tab[b]], writes=[s_t1[i2]])
                        TT("dve", t2[i2][:], psb[k2][:], tabS[b][:], ALU.mult, reads=[s_psb[k2], s_tab[b]], writes=[s_t2[i2]])
                        TT("pool", qk[b][:, c, :], t1[i2][:], t2[i2][:], ALU.add, reads=[s_t1[i2], s_t2[i2]], writes=[s_qkT[b][c]])
                    STORE("sp", q_s[:, :, tb * TB:(tb + 1) * TB], qk[b][:, 0:8, :], f"qst{b}", reads=s_qkT[b][0:8])
                    STORE("sp", k_s[:, :, tb * TB:(tb + 1) * TB], qk[b][:, 8:16, :], f"kst{b}", reads=s_qkT[b][8:16])
                    for tt in range(4):
                        for half in range(2):
                            k = PB.next()
                            MM(psb[k][:], [(h[:, kc, tt * P:(tt + 1) * P], wqkv[:, kc, 2 * D + half * TB:2 * D + (half + 1) * TB])
                                            for kc in range(NCH)], reads=s_h + [s_w], writes=[s_psb[k]])
                            CP("act" if half == 0 else "dve", vt[b][:, tt, half * TB:(half + 1) * TB], psb[k][:], reads=[s_psb[k]],
                               writes=[s_vt[b][tt * 2 + half]])
                    STORE("sp", v_s[:, tb * 4 * D:(tb + 1) * 4 * D], vt[b][:].rearrange("p t d -> p (t d)"), f"vst{b}", reads=s_vt[b])
                R.barrier()
                R.replay()

            with contextlib.ExitStack() as st:
                sb = lambda name, shape, dt: st.enter_context(nc.sbuf_tensor(_un(name), shape, dt))
                va = sb("va", [P, 32, D], BF16)
                s_va = Slot()
                LOAD("sp", va[:].rearrange("p t d -> p (t d)"), v_s[:, :], "va", [s_va])
                qh = [sb(f"qh{i}", [P, S], BF16) for i in range(2)]
                kh = [sb(f"kh{i}", [P, S], BF16) for i in range(2)]
                s_qk = [Slot() for _ in range(2)]
                pt = [sb(f"pt{i}", [P, TB], BF16) for i in range(4)]
                s_pt = [Slot() for _ in range(4)]
                rinv = [sb(f"rinv{i}", [P, TB], F32) for i in range(2)]
                s_rinv = [Slot() for _ in range(2)]
                a0 = sb("a0", [P, TB], F32)
                a1 = sb("a1", [P, TB], F32)
                s_a0, s_a1 = Slot(), Slot()
                sqa = sb("sqa", [P, 1, TB], BF16)
                s_sqa = [Slot()]
                rsa = sb("rsa", [P, TB], F32)
                s_rsa = Slot()
                attn = [sb(f"attn{i}", [P, S], BF16) for i in range(2)]
                s_attn = [Slot() for _ in range(2)]
                SB_ = Banks([0, 1, 2])
                pti = 0
                for hh in range(8):
                    i = hh % 2
                    LOAD("sp", qh[i][:], q_s[:, hh, :], f"qh{i}", [s_qk[i]])
                    R.dma("sp", kh[i][:], k_s[:, hh, :], f"qh{i}", writes=[s_qk[i]], append_w=True)
                    for qblk in range(NTB):
                        nkt = 4 * qblk + 4
                        for kt in range(nkt):
                            j = kt - 4 * qblk
                            c0 = max(j, 0) * P
                            for comp in range(2):
                                r0 = comp * 64
                                sbk = SB_.next()
                                MM(psb[sbk][:, c0:TB], [(kh[i][r0:r0 + 64, kt * P:(kt + 1) * P], qh[i][r0:r0 + 64, qblk * TB + c0:(qblk + 1) * TB])],
                                   reads=[s_qk[i]], writes=[s_psb[sbk]])
                                p_i = pti % 4
                                pti += 1
                                ACT(pt[p_i][:, c0:TB], psb[sbk][:, c0:TB], AF.Exp, reads=[s_psb[sbk]], writes=[s_pt[p_i]], scale=SC1)
                                if j >= 0:
                                    TT("pool", pt[p_i][:, c0:c0 + P], pt[p_i][:, c0:c0 + P], tri_bf[:], ALU.mult, reads=[s_pt[p_i], s_c], writes=[s_pt[p_i]])
                                ob = 3 + comp
                                sbn = 5 + comp
                                R.group("pe", [lambda e, ob=ob, kt=kt, hh=hh, p_i=p_i, c0=c0, nkt=nkt: e.matmul(
                                    psb[ob][:, c0:TB], lhsT=va[:, kt, hh * P:(hh + 1) * P], rhs=pt[p_i][:, c0:TB], start=(kt == 0), stop=(kt == nkt - 1)),
                                    lambda e, sbn=sbn, kt=kt, p_i=p_i, c0=c0, nkt=nkt: e.matmul(
                                    psb[sbn][:, c0:TB], lhsT=ones_bf[:], rhs=pt[p_i][:, c0:TB], start=(kt == 0), stop=(kt == nkt - 1))],
                                    reads=[s_pt[p_i], s_va, s_ones], writes=[s_psb[ob], s_psb[sbn]])
                        RECIP(rinv[0][:], psb[5][:], reads=[s_psb[5]], writes=[s_rinv[0]])
                        TT("dve", a0[:], psb[3][:], rinv[0][:], ALU.mult, reads=[s_psb[3], s_rinv[0]], writes=[s_a0])
                        RECIP(rinv[1][:], psb[6][:], reads=[s_psb[6]], writes=[s_rinv[1]])
                        TT("dve", a1[:], psb[4][:], rinv[1][:], ALU.mult, reads=[s_psb[4], s_rinv[1]], writes=[s_a1])
                        STT(a0[:], a1[:], lam_neg[:, 0:1], a0[:], ALU.mult, ALU.add, reads=[s_a1, s_a0, s_lam], writes=[s_a0])
                        rms_rstd(sqa, s_sqa, [a0[:]], [[s_a0]], 1, 128, 7, rsa, s_rsa, sq_eng="mix")
                        STT(attn[i][:, qblk * TB:(qblk + 1) * TB], a0[:], subg2[:, 0:1], rsa[:], ALU.mult, ALU.mult,
                            reads=[s_a0, s_rsa, s_lam], writes=[s_attn[i]])
                    STORE("sp", cat_s[:, hh, :], attn[i][:], f"ast{i}", reads=[s_attn[i]])
                R.barrier()
                R.replay()

            post_phase(1, woutD_in, final=True)

        gst.callback(lambda: None)
    return nc


def _consts():
    ident = np.eye(P, dtype=np.float32)
    tri = (np.arange(P)[None, :] >= np.arange(P)[:, None]).astype(np.float32)
    rm = np.zeros((2, P, P), np.float32)
    for i in range(16):
        rm[0, 80 + i, 64 + i] = -1.0
        rm[0, 64 + i, 80 + i] = 1.0
    for c in range(2):
        for i in range(8):
            rm[1, c * 64 + 8 + i, c * 64 + i] = -1.0
            rm[1, c * 64 + i, c * 64 + 8 + i] = 1.0
    theta = np.float32(500000.0)
    inv32 = (theta ** (-(np.arange(0, 32, 2, dtype=np.float32) / np.float32(32)))).astype(np.float32)
    inv16 = (theta ** (-(np.arange(0, 16, 2, dtype=np.float32) / np.float32(16)))).astype(np.float32)
    invrows = np.zeros((P, 2), np.float32)
    for i in range(16):
        invrows[64 + i, 0] = inv32[i]
        invrows[80 + i, 0] = inv32[i]
    for c in range(2):
        for i in range(8):
            invrows[c * 64 + i, 1] = inv16[i]
            invrows[c * 64 + 8 + i, 1] = inv16[i]
    cnt = np.zeros((P, 4, 16), np.float32)
    for g, w in enumerate((2, 4, 8, 16)):
        cnt[:, g, :] = (1.0 / np.minimum(np.arange(1, 17), w)).astype(np.float32)[None, :]
    return ident, tri, rm, invrows, cnt


def _cols(v, n):
    return np.ascontiguousarray(np.asarray(v, np.float32).reshape(n, P).T)


_NC_CACHE = {}


def kernel(**inputs):
    g = lambda k: np.asarray(inputs[k])
    x = g("x").astype(np.float32)
    ident, tri, rm, invrows, cnt = _consts()
    if "nc" not in _NC_CACHE:
        _NC_CACHE["nc"] = build_program()
    nc = _NC_CACHE["nc"]
    shared = {
        "ada_w": np.ascontiguousarray(g("ada_w"), dtype=np.float32),
        "adab": np.ascontiguousarray(g("ada_b").astype(np.float32).reshape(2, 48, P).transpose(0, 2, 1)),
        "n1g": np.ascontiguousarray(g("norm1_g").astype(np.float32).reshape(2, NCH, P).transpose(0, 2, 1)),
        "n2g": np.ascontiguousarray(g("norm2_g").astype(np.float32).reshape(2, NCH, P).transpose(0, 2, 1)),
        "fng": _cols(g("final_norm_g"), NCH),
        "w_gu": np.ascontiguousarray(g("ffn_w_gate_up"), dtype=np.float32),
        "w_d": np.ascontiguousarray(g("ffn_w_down"), dtype=np.float32),
        "w_in": np.ascontiguousarray(g("mla_w_in")[0], dtype=np.float32),
        "qng": _cols(g("mla_q_norm_g")[0], 3),
        "kvng": _cols(g("mla_kv_norm_g")[0], 2),
        "w_uq": np.ascontiguousarray(g("mla_w_uq")[0], dtype=np.float32),
        "w_ukv": np.ascontiguousarray(g("mla_w_ukv")[0], dtype=np.float32),
        "pool_w": np.ascontiguousarray(g("pool_w")[0], dtype=np.float32),
        "pool_b": _cols(g("pool_b")[0].reshape(-1), 4),
        "pool_s": _cols(g("pool_scale")[0], 4),
        "w_outA": np.ascontiguousarray(g("mix_a_w_out")[0], dtype=np.float32),
        "w_qkv": np.ascontiguousarray(g("diff_w_qkv")[0], dtype=np.float32),
        "lamv": np.ascontiguousarray(np.broadcast_to(np.stack([
            g("diff_lambda_q1")[0], g("diff_lambda_k1")[0], g("diff_lambda_q2")[0], g("diff_lambda_k2")[0]]).astype(np.float32)[None],
            (P, 4, 64))),
        "subg": np.ascontiguousarray(g("diff_subln_g")[0].astype(np.float32).reshape(P, 1)),
        "w_outD": np.ascontiguousarray(g("diff_w_out")[0], dtype=np.float32),
        "ident_f": ident, "tri": tri, "rmat": rm, "invrows": invrows, "cntinv": cnt,
    }
    c = g("c").astype(np.float32)
    pos = g("positions").astype(np.int32)
    in_maps = []
    ncores = int(_NC_CACHE.get("ncores", 8))
    for b in range(ncores):
        m = dict(shared)
        m["x"] = np.ascontiguousarray(x[b])
        m["cT"] = _cols(c[b], NCH)
        m["pos"] = np.ascontiguousarray(pos[b][None, :])
        in_maps.append(m)
    res = run_bass_kernel_spmd(nc, in_maps, core_ids=list(range(ncores)))
    return np.stack([np.asarray(r["out"], dtype=np.float32) for r in res.results], axis=0)
```

```python
import contextlib
import math
import numpy as np
import ml_dtypes
import concourse.bass as bass
import concourse.mybir as mybir
from concourse.bass_utils import run_bass_kernel_spmd

F32 = mybir.dt.float32
BF16 = mybir.dt.bfloat16
I32 = mybir.dt.int32
AF = mybir.ActivationFunctionType
ALU = mybir.AluOpType
AX = mybir.AxisListType

P = 128
S = 4096
D = 1024
NCH = D // P
TB = 512
NTB = S // TB
EPS = 1e-6


class Sem:
    def __init__(self, nc, stack, name):
        self.h = stack.enter_context(nc.semaphore(name))
        self.val = 0
        self.name = name


class Slot:
    __slots__ = ("w", "r", "name", "excl")

    def __init__(self, name="", excl=False):
        self.w = []
        self.r = []
        self.name = name
        self.excl = excl


class Queue:
    def __init__(self, name, sem):
        self.name = name
        self.sem = sem
        self.ops = []
        self.seen = {}


class Rec:
    def __init__(self, nc, stack):
        self.nc = nc
        self.stack = stack
        self.q = {}
        for n in ("pe", "act", "dve", "pool", "sp"):
            self.q[n] = Queue(n, Sem(nc, stack, "q_" + n))
        self.dma_sems = {}
        self.all_dma_toks = []

    def dsem(self, name):
        if name not in self.dma_sems:
            self.dma_sems[name] = Sem(self.nc, self.stack, "d_" + name)
        return self.dma_sems[name]

    def _deps(self, reads, writes):
        deps = []
        for s in reads:
            deps += s.w
            if s.excl:
                deps += s.r
        for s in writes:
            deps += s.w
            deps += s.r
        return deps

    def _prune(self, q, deps):
        best = {}
        for (sem, v) in deps:
            if v > q.seen.get(sem, 0) and v > best.get(sem, (None, 0))[1]:
                best[sem] = (sem, v)
        out = list(best.values())
        for (sem, v) in out:
            q.seen[sem] = v
        return out

    def op(self, qn, fn, reads=(), writes=(), extra=(), signal=True):
        q = self.q[qn]
        deps = self._deps(reads, writes) + list(extra)
        waits = self._prune(q, deps)
        tok = None
        if signal:
            q.sem.val += 1
            tok = (q.sem, q.sem.val)
            q.ops.append((waits, fn, q.sem, 1))
            for s in reads:
                s.r.append(tok)
            for s in writes:
                s.w = [tok]
                s.r = []
        else:
            q.ops.append((waits, fn, None, 0))
        return tok

    def group(self, qn, fns, reads=(), writes=(), extra=()):
        q = self.q[qn]
        deps = self._deps(reads, writes) + list(extra)
        waits = self._prune(q, deps)
        q.sem.val += 1
        tok = (q.sem, q.sem.val)
        n = len(fns)
        for i, fn in enumerate(fns):
            q.ops.append((waits if i == 0 else [], fn,
                          q.sem if i == n - 1 else None, 1))
        for s in reads:
            s.r.append(tok)
        for s in writes:
            s.w = [tok]
            s.r = []
        return tok

    def dma(self, qn, out, in_, semname, reads=(), writes=(), extra=(), append_w=False):
        q = self.q[qn]
        sem = self.dsem(semname)
        deps = self._deps(reads, [] if append_w else writes) + list(extra)
        waits = self._prune(q, deps)
        sem.val += 16
        tok = (sem, sem.val)

        def fn(eng, out=out, in_=in_):
            return eng.dma_start(out=out, in_=in_)
        q.ops.append((waits, fn, sem, 16))
        for s in reads:
            s.r.append(tok)
        for s in writes:
            if append_w:
                s.w = s.w + [tok]
            else:
                s.w = [tok]
                s.r = []
        self.all_dma_toks.append(tok)
        return tok

    def barrier(self):
        toks = [(q.sem, q.sem.val) for q in self.q.values() if q.sem.val > 0]
        toks += [(s, s.val) for s in self.dma_sems.values() if s.val > 0]
        for qn, q in self.q.items():
            waits = self._prune(q, toks)
            if waits:
                q.ops.append((waits, None, None, 0))

    def replay(self):
        nc = self.nc
        with nc.Block() as block:
            def run(q):
                def body(eng):
                    for (waits, fn, isem, amt) in q.ops:
                        for (sem, v) in waits:
                            eng.wait_ge(sem.h, v)
                        if fn is None:
                            continue
                        ins = fn(eng)
                        if isem is not None:
                            ins.then_inc(isem.h, amt)
                return body
            block.tensor(run(self.q["pe"]))
            block.scalar(run(self.q["act"]))
            block.vector(run(self.q["dve"]))
            block.gpsimd(run(self.q["pool"]))
            block.sync(run(self.q["sp"]))
        for q in self.q.values():
            q.ops = []


HQ = 8
DFF = 2816
NF = DFF // P
MAGIC = 12582912.0
C1 = 6.28125
C2 = 2.0 * math.pi - 6.28125
INV2PI = 1.0 / (2.0 * math.pi)
LAM_INIT1 = 0.8 - 0.6 * math.exp(-0.3 * 1)
DBG = {}


def build_program(n_layers=2, stop=None):
    nc = bass.Bass("TRN2", target_bir_lowering=False)

    def din(name, shape, dt=F32):
        return nc.dram_tensor(name, shape, dt, kind="ExternalInput").ap()

    def dscr(name, shape, dt):
        return nc.dram_tensor(name, shape, dt, kind="Internal").ap()

    x_in = din("x", [S, D])
    cT_in = din("cT", [P, NCH])
    pos_in = din("pos", [1, S], I32)
    ada_w = din("ada_w", [2, D, 6 * D])
    adab_in = din("adab", [2, P, 48])
    n1g_in = din("n1g", [2, P, NCH])
    n2g_in = din("n2g", [2, P, NCH])
    fng_in = din("fng", [P, NCH])
    wgu_in = din("w_gu", [2, D, 2 * DFF])
    wd_in = din("w_d", [2, DFF, D])
    win_in = din("w_in", [D, 1184])
    qng_in = din("qng", [P, 3])
    kvng_in = din("kvng", [P, 2])
    wuq_in = din("w_uq", [384, 768])
    wukv_in = din("w_ukv", [256, 1024])
    poolw_in = din("pool_w", [4, P, P])
    poolb_in = din("pool_b", [P, 4])
    pools_in = din("pool_s", [P, 4])
    woutA_in = din("w_outA", [D, D])
    wqkv_in = din("w_qkv", [D, 3 * D])
    lamv_in = din("lamv", [P, 4, 64])
    subg_in = din("subg", [P, 1])
    woutD_in = din("w_outD", [D, D])
    identf_in = din("ident_f", [P, P])
    tri_in = din("tri", [P, P])
    rmat_in = din("rmat", [2, P, P])
    invrows_in = din("invrows", [P, 2])
    cntinv_in = din("cntinv", [P, 4, 16])
    out = nc.dram_tensor("out", [S, D], F32, kind="ExternalOutput").ap()

    xT_s = dscr("xT_s", [P, NCH, S], F32)
    tab_s = dscr("tab_s", [4, P, S], F32)
    wgu_s = dscr("wgu_s", [2, NF, P, NCH * 256], BF16)
    q_s = dscr("q_s", [P, 8, S], BF16)
    k_s = dscr("k_s", [P, 8, S], BF16)
    v_s = dscr("v_s", [P, 32 * 1024], BF16)
    cat_s = dscr("cat_s", [P, NCH, S], BF16)

    with contextlib.ExitStack() as gst:
        R = Rec(nc, gst)
        _uid = [0]

        def _un(name):
            _uid[0] += 1
            return f"sb{_uid[0]}_{name}"
        gsb = lambda name, shape, dt: gst.enter_context(nc.sbuf_tensor(_un(name), shape, dt))
        psall = gst.enter_context(nc.psum_tensor("psall", [P, 8, TB], F32))
        psb = [psall[:, i, :] for i in range(8)]
        s_psb = [Slot(f"psb{i}", excl=True) for i in range(8)]

        class Banks:
            def __init__(self, ids):
                self.ids = list(ids)
                self.i = 0

            def next(self):
                b = self.ids[self.i % len(self.ids)]
                self.i += 1
                return b

        def ACT(out_, in_, func, reads, writes, **kw):
            return R.op("act", lambda e: e.activation(out=out_, in_=in_, func=func, **kw), reads=reads, writes=writes)

        def TT(q, out_, in0, in1, op, reads, writes):
            return R.op(q, lambda e: e.tensor_tensor(out=out_, in0=in0, in1=in1, op=op), reads=reads, writes=writes)

        def TS(q, out_, in0, s1, s2, op0, op1, reads, writes):
            if op1 is None:
                return R.op(q, lambda e: e.tensor_single_scalar(out=out_, in_=in0, scalar=s1, op=op0), reads=reads, writes=writes)
            return R.op(q, lambda e: e.tensor_scalar(out=out_, in0=in0, scalar1=s1, scalar2=s2, op0=op0, op1=op1),
                        reads=reads, writes=writes)

        def STT(out_, in0, scalar, in1, op0, op1, reads, writes):
            return R.op("dve", lambda e: e.scalar_tensor_tensor(out=out_, in0=in0, scalar=scalar, in1=in1, op0=op0, op1=op1),
                        reads=reads, writes=writes)

        def CP(q, out_, in_, reads, writes):
            if q == "act":
                return R.op("act", lambda e: e.copy(out=out_, in_=in_), reads=reads, writes=writes)
            return R.op(q, lambda e: e.tensor_copy(out=out_, in_=in_), reads=reads, writes=writes)

        def MM(out_, pairs, reads, writes):
            n = len(pairs)
            fns = []
            for i, (l, r) in enumerate(pairs):
                fns.append(lambda e, l=l, r=r, i=i: e.matmul(out_, lhsT=l, rhs=r, start=(i == 0), stop=(i == n - 1)))
            return R.group("pe", fns, reads=reads, writes=writes)

        def RECIP(out_, in_, reads, writes):
            return R.op("dve", lambda e: e.reciprocal(out=out_, in_=in_), reads=reads, writes=writes)

        def LOAD(q, out_, in_, sem, writes, reads=()):
            return R.dma(q, out_, in_, sem, reads=reads, writes=writes)

        def STORE(q, out_, in_, sem, reads):
            return R.dma(q, out_, in_, sem, reads=reads, writes=[Slot()])

        identf = gsb("identf", [P, P], F32)
        ones_bf = gsb("ones_bf", [P, P], BF16)
        tri_bf = gsb("tri_bf", [P, P], BF16)
        rmat_bf = gsb("rmat_bf", [P, 2, P], BF16)
        fng = gsb("fng", [P, NCH], F32)
        AB = gsb("AB", [P, 2, 6, NCH], F32)
        halfpi = gsb("halfpi", [P, 1], F32)
        epsc = gsb("epsc", [P, 1], F32)
        lam_neg = gsb("lam_neg", [P, 1], F32)
        subg2 = gsb("subg2", [P, 1], F32)
        s_c = Slot("consts")
        s_ones = Slot("ones")
        s_AB = Slot("AB")
        s_lam = Slot("lam")
        LOAD("sp", identf[:], identf_in, "c0", [s_c])
        R.dma("sp", fng[:], fng_in, "c0", writes=[s_c], append_w=True)
        R.dma("pool", tri_bf[:], tri_in, "c1", writes=[s_c], append_w=True)
        R.dma("pool", rmat_bf[:], rmat_in.rearrange("j p m -> p j m"), "c1", writes=[s_c], append_w=True)
        R.op("pool", lambda e: e.memset(ones_bf[:], 1.0), writes=[s_ones])
        R.op("pool", lambda e: e.memset(halfpi[:], math.pi / 2.0), writes=[s_ones])
        s_ones.w = [(R.q["pool"].sem, R.q["pool"].sem.val)]
        R.op("pool", lambda e: e.memset(epsc[:], EPS), writes=[Slot()])
        s_ones.w = [(R.q["pool"].sem, R.q["pool"].sem.val)]

        def rms_rstd(sq_tile, s_sq, src_chunks, s_src, n, dim, bank, rstd_tile, s_rstd, np_=P, sq_eng="act"):
            for c in range(n):
                if sq_eng == "act" or c % 2 == 0:
                    ACT(sq_tile[:np_, c, :], src_chunks[c], AF.Square, reads=s_src[c], writes=[s_sq[c]])
                else:
                    TT("pool", sq_tile[:np_, c, :], src_chunks[c], src_chunks[c], ALU.mult, reads=s_src[c], writes=[s_sq[c]])
            MM(psb[bank][:np_, :], [(ones_bf[:np_, :np_], sq_tile[:np_, c, :]) for c in range(n)],
               reads=s_sq[:n] + [s_ones], writes=[s_psb[bank]])
            ACT(rstd_tile[:np_, :], psb[bank][:np_, :], AF.Ln, reads=[s_psb[bank], s_ones], writes=[s_rstd],
                scale=1.0 / dim, bias=epsc[:np_, :])
            ACT(rstd_tile[:np_, :], rstd_tile[:np_, :], AF.Exp, reads=[s_rstd], writes=[s_rstd], scale=-0.5)

        with contextlib.ExitStack() as st:
            sb = lambda name, shape, dt: st.enter_context(nc.sbuf_tensor(_un(name), shape, dt))
            cT = sb("cT", [P, NCH], F32)
            cond = sb("cond", [P, NCH], F32)
            adab = sb("adab", [P, 2, 48], F32)
            n1g = sb("n1g", [P, 2, NCH], F32)
            n2g = sb("n2g", [P, 2, NCH], F32)
            modt = sb("modt", [P, 2, 48], F32)
            s_in = Slot()
            s_cond = Slot()
            s_modt = Slot()
            LOAD("sp", cT[:], cT_in, "a0", [s_in])
            R.dma("sp", adab[:], adab_in.rearrange("l p m -> p l m"), "a0", writes=[s_in], append_w=True)
            R.dma("sp", n1g[:], n1g_in.rearrange("l p m -> p l m"), "a0", writes=[s_in], append_w=True)
            R.dma("sp", n2g[:], n2g_in.rearrange("l p m -> p l m"), "a0", writes=[s_in], append_w=True)
            ACT(cond[:], cT[:], AF.Silu, reads=[s_in], writes=[s_cond])
            aw = [sb(f"aw{i}", [P, NCH, 512], F32) for i in range(2)]
            s_aw = [Slot() for _ in range(2)]
            for l in range(n_layers):
                for mg in range(12):
                    i = (l * 12 + mg) % 2
                    LOAD("sp", aw[i][:], ada_w[l, :, mg * 512:(mg + 1) * 512].rearrange("(kc p) n -> p kc n", p=P),
                         f"aw{i}", [s_aw[i]])
                    for j in range(4):
                        m = mg * 4 + j
                        MM(psb[l][:, m:m + 1], [(aw[i][:, kc, j * P:(j + 1) * P], cond[:, kc:kc + 1]) for kc in range(NCH)],
                           reads=[s_aw[i], s_cond], writes=[s_psb[l]])
                TT("dve", modt[:, l, :], psb[l][:, 0:48], adab[:, l, :], ALU.add, reads=[s_psb[l], s_in], writes=[s_modt])
                STT(AB[:, l, 0, :], modt[:, l, 8:16], 1.0, n1g[:, l, :], ALU.add, ALU.mult, reads=[s_modt, s_in], writes=[s_AB])
                CP("dve", AB[:, l, 1, :], modt[:, l, 0:8], reads=[s_modt], writes=[s_AB])
                CP("dve", AB[:, l, 2, :], modt[:, l, 16:24], reads=[s_modt], writes=[s_AB])
                STT(AB[:, l, 3, :], modt[:, l, 32:40], 1.0, n2g[:, l, :], ALU.add, ALU.mult, reads=[s_modt, s_in], writes=[s_AB])
                CP("dve", AB[:, l, 4, :], modt[:, l, 24:32], reads=[s_modt], writes=[s_AB])
                CP("dve", AB[:, l, 5, :], modt[:, l, 40:48], reads=[s_modt], writes=[s_AB])
            lamv = sb("lamv", [P, 4, 64], F32)
            lprod = sb("lprod", [P, 2, 64], F32)
            lsum = sb("lsum", [P, 2], F32)
            subg = sb("subg", [P, 1], F32)
            s_lv = Slot()
            s_lp = Slot()
            LOAD("sp", lamv[:], lamv_in, "a1", [s_lv])
            R.dma("sp", subg[:], subg_in, "a1", writes=[s_lv], append_w=True)
            TT("dve", lprod[:, 0, :], lamv[:, 0, :], lamv[:, 1, :], ALU.mult, reads=[s_lv], writes=[s_lp])
            TT("dve", lprod[:, 1, :], lamv[:, 2, :], lamv[:, 3, :], ALU.mult, reads=[s_lv], writes=[s_lp])
            R.op("dve", lambda e: e.reduce_sum(out=lsum[:], in_=lprod[:], axis=AX.X), reads=[s_lp], writes=[s_lp])
            ACT(lsum[:], lsum[:], AF.Exp, reads=[s_lp], writes=[s_lp])
            TT("dve", lam_neg[:], lsum[:, 1:2], lsum[:, 0:1], ALU.subtract, reads=[s_lp], writes=[s_lam])
            TS("dve", lam_neg[:], lam_neg[:], -LAM_INIT1, None, ALU.add, None, reads=[s_lam], writes=[s_lam])
            TS("dve", subg2[:], subg[:], 1.0 - LAM_INIT1, None, ALU.mult, None, reads=[s_lv], writes=[s_lam])

            posi = sb("posi", [P, S], I32)
            posf = sb("posf", [P, S], F32)
            invrows = sb("invrows", [P, 2], F32)
            ang = sb("ang", [P, S], F32)
            kk = sb("kk", [P, S], F32)
            rr = sb("rr", [P, S], F32)
            tS = sb("tS", [P, S], F32)
            tC = sb("tC", [P, S], F32)
            s_pos, s_ang, s_kk, s_rr, s_tS, s_tC = Slot(), Slot(), Slot(), Slot(), Slot(), Slot()
            LOAD("sp", posi[:], pos_in.broadcast_to([P, S]), "a2", [s_pos])
            R.dma("sp", invrows[:], invrows_in, "a2", writes=[s_pos], append_w=True)
            CP("dve", posf[:], posi[:], reads=[s_pos], writes=[s_pos])
            for j in range(2):
                TS("dve", ang[:], posf[:], invrows[:, j:j + 1], None, ALU.mult, None, reads=[s_pos], writes=[s_ang])
                TS("dve", kk[:], ang[:], INV2PI, MAGIC, ALU.mult, ALU.add, reads=[s_ang], writes=[s_kk])
                TS("dve", kk[:], kk[:], -MAGIC, None, ALU.add, None, reads=[s_kk], writes=[s_kk])
                STT(rr[:], kk[:], -C1, ang[:], ALU.mult, ALU.add, reads=[s_kk, s_ang], writes=[s_rr])
                STT(rr[:], kk[:], -C2, rr[:], ALU.mult, ALU.add, reads=[s_kk, s_rr], writes=[s_rr])
                ACT(tS[:], rr[:], AF.Sin, reads=[s_rr], writes=[s_tS])
                ACT(rr[:], rr[:], AF.Abs, reads=[s_rr], writes=[s_rr])
                ACT(tC[:], rr[:], AF.Sin, reads=[s_rr, s_ones], writes=[s_tC], scale=-1.0, bias=halfpi[:])
                STORE("sp", tab_s[2 * j], tC[:], "a3", reads=[s_tC])
                STORE("sp", tab_s[2 * j + 1], tS[:], "a3", reads=[s_tS])
            R.barrier()
            R.replay()
        if stop == "A0":
            return nc

        def wgu_prep(l, stg, s_stg, flist):
            for f in flist:
                i = f % len(stg)
                for half in range(2):
                    src = wgu_in[l, :, half * DFF + f * P: half * DFF + (f + 1) * P].rearrange("(kc p) n -> p kc n", p=P)
                    R.dma("pool", stg[i][:, :, half * P:(half + 1) * P], src, f"wgst{i}",
                          writes=[s_stg[i]], append_w=(half == 1))
                R.dma("pool", wgu_s[l, f], stg[i][:].rearrange("p kc n -> p (kc n)"), f"wgsto{i}", reads=[s_stg[i]], writes=[Slot()])

        with contextlib.ExitStack() as st:
            sb = lambda name, shape, dt: st.enter_context(nc.sbuf_tensor(_un(name), shape, dt))
            stg = [sb(f"stg{i}", [P, NCH, 256], BF16) for i in range(3)]
            s_stg = [Slot() for _ in range(3)]
            xin = [sb(f"xin{i}", [P, 4, D], F32) for i in range(2)]
            s_xin = [Slot() for _ in range(2)]
            xT = [sb(f"xT{i}", [P, NCH, TB], F32) for i in range(2)]
            s_xT = [[Slot() for c in range(NCH)] for i in range(2)]
            PB = Banks(range(8))
            for tb in range(NTB):
                b = tb % 2
                LOAD("sp", xin[b][:], x_in[tb * TB:(tb + 1) * TB, :].rearrange("(t p) d -> p t d", p=P), f"xin{b}", [s_xin[b]])
                for c in range(NCH):
                    k = PB.next()
                    fns = [lambda e, k=k, t=t, c=c, b=b: e.transpose(
                        out=psb[k][:, t * P:(t + 1) * P], in_=xin[b][:, t, c * P:(c + 1) * P], identity=identf[:])
                        for t in range(4)]
                    R.group("pe", fns, reads=[s_xin[b], s_c], writes=[s_psb[k]])
                    CP("act" if c % 2 == 0 else "dve", xT[b][:, c, :], psb[k][:], reads=[s_psb[k]], writes=[s_xT[b][c]])
                STORE("sp", xT_s[:, :, tb * TB:(tb + 1) * TB], xT[b][:], f"xTst{b}", reads=s_xT[b])
                wgu_prep(0, stg, s_stg, range(tb * 3, min(NF, tb * 3 + 3)))
            R.barrier()
            R.replay()
        if stop == "X0":
            return nc

        def load_tab(tile, j, tb, sem, slot):
            LOAD("sp", tile[:], tab_s[j, :, tb * TB:(tb + 1) * TB], sem, [slot])

        def ffn_block(l, tb, xb, s_xb, cat, s_cat, wout, s_wout, wd, s_wd, sq, s_sq, rstd, s_rstd, tmp, s_tmp,
                      h2, s_h2, actb, s_act, sg, s_sg, wg, s_wg, PB, wgi):
            for c in range(NCH):
                k = PB.next()
                MM(psb[k][:], [(wout[:, kc, c * P:(c + 1) * P], cat[:, kc, :]) for kc in range(NCH)],
                   reads=s_cat + [s_wout], writes=[s_psb[k]])
                STT(xb[:, c, :], psb[k][:], AB[:, l, 2, c:c + 1], xb[:, c, :], ALU.mult, ALU.add,
                    reads=[s_psb[k], s_AB], writes=[s_xb[c]])
            k = PB.next()
            rms_rstd(sq, s_sq, [xb[:, c, :] for c in range(NCH)], [[s_xb[c]] for c in range(NCH)], NCH, D, k, rstd, s_rstd,
                     sq_eng="mix")
            for c in range(NCH):
                t = c % 2
                STT(tmp[t][:], xb[:, c, :], AB[:, l, 3, c:c + 1], rstd[:], ALU.mult, ALU.mult,
                    reads=[s_xb[c], s_rstd, s_AB], writes=[s_tmp[t]])
                ACT(h2[:, c, :], tmp[t][:], AF.Identity, reads=[s_tmp[t], s_AB], writes=[s_h2[c]], bias=AB[:, l, 4, c:c + 1], scale=1.0)
            for f in range(NF):
                i = wgi[0] % len(wg)
                wgi[0] += 1
                LOAD("sp", wg[i][:].rearrange("p kc n -> p (kc n)"), wgu_s[l, f], f"wg{i}", [s_wg[i]])
                kg = PB.next()
                ku = PB.next()
                MM(psb[kg][:], [(wg[i][:, kc, 0:P], h2[:, kc, :]) for kc in range(NCH)], reads=s_h2 + [s_wg[i]], writes=[s_psb[kg]])
                MM(psb[ku][:], [(wg[i][:, kc, P:2 * P], h2[:, kc, :]) for kc in range(NCH)], reads=s_h2 + [s_wg[i]], writes=[s_psb[ku]])
                t = f % 2
                ACT(sg[t][:], psb[kg][:], AF.Silu, reads=[s_psb[kg]], writes=[s_sg[t]])
                TT("dve", actb[:, f, :], sg[t][:], psb[ku][:], ALU.mult, reads=[s_sg[t], s_psb[ku]], writes=[s_act[f]])
            for c in range(NCH):
                k = PB.next()
                MM(psb[k][:], [(wd[:, f, c * P:(c + 1) * P], actb[:, f, :]) for f in range(NF)], reads=s_act + [s_wd], writes=[s_psb[k]])
                STT(xb[:, c, :], psb[k][:], AB[:, l, 5, c:c + 1], xb[:, c, :], ALU.mult, ALU.add,
                    reads=[s_psb[k], s_AB], writes=[s_xb[c]])

        def norm1_block(l, xb, s_xb, sq, s_sq, rstd, s_rstd, tmp, s_tmp, h, s_h, PB):
            k = PB.next()
            rms_rstd(sq, s_sq, [xb[:, c, :] for c in range(NCH)], [[s_xb]] * NCH, NCH, D, k, rstd, s_rstd, sq_eng="mix")
            for c in range(NCH):
                t = c % 2
                STT(tmp[t][:], xb[:, c, :], AB[:, l, 0, c:c + 1], rstd[:], ALU.mult, ALU.mult,
                    reads=[s_xb, s_rstd, s_AB], writes=[s_tmp[t]])
                ACT(h[:, c, :], tmp[t][:], AF.Identity, reads=[s_tmp[t], s_AB], writes=[s_h[c]], bias=AB[:, l, 1, c:c + 1], scale=1.0)

        def post_phase(l, woutX_in, final):
            with contextlib.ExitStack() as st:
                sb = lambda name, shape, dt: st.enter_context(nc.sbuf_tensor(_un(name), shape, dt))
                wd = sb("wd", [P, NF, D], BF16)
                wout = sb("wout", [P, NCH, D], BF16)
                s_wd, s_wout = Slot(), Slot()
                LOAD("pool", wout[:], woutX_in.rearrange("(kc p) n -> p kc n", p=P), "wout", [s_wout])
                LOAD("pool", wd[:], wd_in[l].rearrange("(f p) n -> p f n", p=P), "wd", [s_wd])
                wg = [sb(f"wg{i}", [P, NCH, 256], BF16) for i in range(4)]
                s_wg = [Slot() for _ in range(4)]
                xb = [sb(f"xb{i}", [P, NCH, TB], F32) for i in range(2)]
                s_xb = [[Slot() for c in range(NCH)] for i in range(2)]
                cat = [sb(f"cat{i}", [P, NCH, TB], BF16) for i in range(2)]
                s_cat = [Slot() for _ in range(2)]
                sq = sb("sq", [P, NCH, TB], BF16)
                s_sq = [Slot() for _ in range(NCH)]
                rstd = sb("rstd", [P, TB], F32)
                s_rstd = Slot()
                tmp = [sb(f"tmp{i}", [P, TB], F32) for i in range(2)]
                s_tmp = [Slot() for _ in range(2)]
                h2 = sb("h2", [P, NCH, TB], BF16)
                s_h2 = [Slot() for _ in range(NCH)]
                actb = sb("actb", [P, NF, TB], BF16)
                s_act = [Slot() for _ in range(NF)]
                sg = [sb(f"sg{i}", [P, TB], F32) for i in range(2)]
                s_sg = [Slot() for _ in range(2)]
                if final:
                    ot = sb("ot", [P, 4, D], F32)
                    s_ot = [Slot() for _ in range(8)]
                    on = sb("on", [P, NCH, TB], F32)
                    s_on = [Slot() for _ in range(NCH)]
                PB = Banks(range(8))
                wgi = [0]
                for tb in range(NTB):
                    b = tb % 2
                    R.dma("sp", xb[b][:], xT_s[:, :, tb * TB:(tb + 1) * TB], f"xb{b}", writes=s_xb[b])
                    LOAD("sp", cat[b][:], cat_s[:, :, tb * TB:(tb + 1) * TB], f"cat{b}", [s_cat[b]])
                    ffn_block(l, tb, xb[b], s_xb[b], cat[b], [s_cat[b]], wout, s_wout, wd, s_wd, sq, s_sq, rstd, s_rstd,
                              tmp, s_tmp, h2, s_h2, actb, s_act, sg, s_sg, wg, s_wg, PB, wgi)
                    if not final:
                        STORE("sp", xT_s[:, :, tb * TB:(tb + 1) * TB], xb[b][:], f"xbst{b}", reads=s_xb[b])
                    else:
                        k = PB.next()
                        rms_rstd(sq, s_sq, [xb[b][:, c, :] for c in range(NCH)], [[s_xb[b][c]] for c in range(NCH)], NCH, D, k,
                                 rstd, s_rstd, sq_eng="mix")
                        for c in range(NCH):
                            STT(on[:, c, :], xb[b][:, c, :], fng[:, c:c + 1], rstd[:], ALU.mult, ALU.mult,
                                reads=[s_xb[b][c], s_rstd, s_c], writes=[s_on[c]])
                        for t in range(4):
                            for half in range(2):
                                k = PB.next()
                                fns = [lambda e, k=k, t=t, j=j, half=half: e.transpose(
                                    out=psb[k][:, j * P:(j + 1) * P], in_=on[:, half * 4 + j, t * P:(t + 1) * P],
                                    identity=identf[:]) for j in range(4)]
                                R.group("pe", fns, reads=s_on[half * 4:half * 4 + 4] + [s_c], writes=[s_psb[k]])
                                CP("act", ot[:, t, half * TB:(half + 1) * TB], psb[k][:], reads=[s_psb[k]], writes=[s_ot[t * 2 + half]])
                        STORE("sp", out[tb * TB:(tb + 1) * TB, :].rearrange("(t p) d -> p t d", p=P), ot[:], "ost", reads=s_ot)
                R.barrier()
                R.replay()

        SC0 = 96.0 ** -0.5
        with contextlib.ExitStack() as st:
            sb = lambda name, shape, dt: st.enter_context(nc.sbuf_tensor(_un(name), shape, dt))
            win = sb("win", [P, NCH, 1184], BF16)
            wuq = sb("wuq", [P, 3, 800], BF16)
            wukv = sb("wukv", [P, 2, 1024], BF16)
            poolw = sb("poolw", [P, 4, P], BF16)
            qng = sb("qng", [P, 3], F32)
            kvng = sb("kvng", [P, 2], F32)
            poolb = sb("poolb", [P, 4], F32)
            pools = sb("pools", [P, 4], F32)
            cntinv = sb("cntinv", [P, 4, 16], F32)
            s_w = Slot()
            LOAD("pool", win[:], win_in.rearrange("(kc p) n -> p kc n", p=P), "w0", [s_w])
            s_wq = Slot()
            R.op("pool", lambda e: e.memset(wuq[:, :, 768:800], 0.0), writes=[s_wq])
            R.dma("pool", wuq[:, :, 0:768], wuq_in.rearrange("(kc p) n -> p kc n", p=P), "w0", writes=[s_w], append_w=True)
            R.dma("pool", wukv[:], wukv_in.rearrange("(kc p) n -> p kc n", p=P), "w0", writes=[s_w], append_w=True)
            R.dma("pool", poolw[:], poolw_in.rearrange("g c d -> c g d"), "w0", writes=[s_w], append_w=True)
            R.dma("sp", qng[:], qng_in, "w1", writes=[s_w], append_w=True)
            R.dma("sp", kvng[:], kvng_in, "w1", writes=[s_w], append_w=True)
            R.dma("sp", poolb[:], poolb_in, "w1", writes=[s_w], append_w=True)
            R.dma("sp", pools[:], pools_in, "w1", writes=[s_w], append_w=True)
            R.dma("sp", cntinv[:], cntinv_in, "w1", writes=[s_w], append_w=True)
            xb = [sb(f"xb{i}", [P, NCH, TB], F32) for i in range(2)]
            s_xb = [Slot() for _ in range(2)]
            tabC = [sb(f"tabC{i}", [P, TB], F32) for i in range(2)]
            tabS = [sb(f"tabS{i}", [P, TB], F32) for i in range(2)]
            s_tab = [Slot() for _ in range(2)]
            sq = sb("sq", [P, NCH, TB], BF16)
            s_sq = [Slot() for _ in range(NCH)]
            rstd = sb("rstd", [P, TB], F32)
            s_rstd = Slot()
            rstq = sb("rstq", [P, TB], F32)
            s_rstq = Slot()
            rstk = sb("rstk", [P, TB], F32)
            s_rstk = Slot()
            tmp = [sb(f"tmp{i}", [P, TB], F32) for i in range(2)]
            s_tmp = [Slot() for _ in range(2)]
            h = sb("h", [P, NCH, TB], BF16)
            s_h = [Slot() for _ in range(NCH)]
            cq = sb("cq", [P, 5, TB], F32)
            s_cq = [Slot() for _ in range(5)]
            cn = sb("cn", [P, 5, TB], BF16)
            s_cn = [Slot() for _ in range(5)]
            u = sb("u", [P, 4, 16 + TB], F32)
            s_u = [Slot() for _ in range(4)]
            lv = [sb(f"lv{i}", [P, 16 + TB], F32) for i in range(2)]
            s_lv = [Slot() for _ in range(2)]
            pooled = sb("pooled", [P, 4, TB], BF16)
            s_pl = [Slot() for _ in range(4)]
            ptmp = sb("ptmp", [P, 16], F32)
            s_ptmp = Slot()
            qb = [sb(f"qb{i}", [P, TB], BF16) for i in range(2)]
            s_qb = [Slot() for _ in range(2)]
            t1 = [sb(f"t1{i}", [P, TB], F32) for i in range(2)]
            s_t1 = [Slot() for _ in range(2)]
            t2 = [sb(f"t2{i}", [P, TB], F32) for i in range(2)]
            s_t2 = [Slot() for _ in range(2)]
            qT = [sb(f"qT{i}", [P, 8, TB], BF16) for i in range(2)]
            s_qT = [[Slot() for hh in range(8)] for _ in range(2)]
            kT = [sb(f"kT{i}", [P, 8, TB], BF16) for i in range(2)]
            s_kT = [[Slot() for hh in range(8)] for _ in range(2)]
            s_kTr = [[Slot() for hh in range(8)] for _ in range(2)]
            vt = [sb(f"vt{i}", [P, 4, 8, 65], BF16) for i in range(2)]
            s_vt = [[Slot() for tt in range(4)] for _ in range(2)]
            catp = [sb(f"catp{i}", [P, 4, TB], BF16) for i in range(2)]
            s_catp = [[Slot() for g in range(4)] for _ in range(2)]
            for i in range(2):
                R.op("pool", lambda e, i=i: e.memset(vt[i][:], 1.0), writes=s_vt[i])
            R.op("pool", lambda e: e.memset(u[:], 0.0), writes=s_u)
            PB = Banks(range(8))
            for tb in range(DBG.get('l0p1_ntb', NTB)):
                b = tb % 2
                LOAD("sp", xb[b][:], xT_s[:, :, tb * TB:(tb + 1) * TB], f"xb{b}", [s_xb[b]])
                R.dma("sp", tabC[b][:], tab_s[0, :, tb * TB:(tb + 1) * TB], f"tab{b}", writes=[s_tab[b]])
                R.dma("sp", tabS[b][:], tab_s[1, :, tb * TB:(tb + 1) * TB], f"tab{b}", writes=[s_tab[b]], append_w=True)
                norm1_block(0, xb[b], s_xb[b], sq, s_sq, rstd, s_rstd, tmp, s_tmp, h, s_h, PB)
                if DBG.get('sec', 99) < 1:
                    continue
                for j in range(5):
                    k = PB.next()
                    MM(psb[k][:], [(win[:, kc, j * P:(j + 1) * P], h[:, kc, :]) for kc in range(NCH)], reads=s_h + [s_w], writes=[s_psb[k]])
                    CP("act", cq[:, j, :], psb[k][:], reads=[s_psb[k]], writes=[s_cq[j]])
                k = PB.next()
                rms_rstd(sq, s_sq, [cq[:, j, :] for j in range(3)], [[s_cq[j]] for j in range(3)], 3, 384, k, rstq, s_rstq, sq_eng="mix")
                for j in range(3):
                    STT(cn[:, j, :], cq[:, j, :], qng[:, j:j + 1], rstq[:], ALU.mult, ALU.mult, reads=[s_cq[j], s_rstq, s_w], writes=[s_cn[j]])
                k = PB.next()
                rms_rstd(sq[:, 3:5, :], s_sq[3:5], [cq[:, 3 + j, :] for j in range(2)], [[s_cq[3 + j]] for j in range(2)], 2, 256, k, rstk, s_rstk, sq_eng="mix")
                for j in range(2):
                    STT(cn[:, 3 + j, :], cq[:, 3 + j, :], kvng[:, j:j + 1], rstk[:], ALU.mult, ALU.mult,
                        reads=[s_cq[3 + j], s_rstk, s_w], writes=[s_cn[3 + j]])
                if DBG.get('sec', 99) < 2:
                    continue
                k = PB.next()
                MM(psb[k][64:96, :], [(win[:, kc, 640:672], h[:, kc, :]) for kc in range(NCH)], reads=s_h + [s_w], writes=[s_psb[k]])
                i2 = 0
                CP("act", qb[i2][64:96, :], psb[k][64:96, :], reads=[s_psb[k]], writes=[s_qb[i2]])
                k2 = PB.next()
                MM(psb[k2][64:96, :], [(rmat_bf[64:96, 0, 64:96], qb[i2][64:96, :])], reads=[s_qb[i2], s_c], writes=[s_psb[k2]])
                TT("dve", t1[i2][64:96, :], psb[k][64:96, :], tabC[b][64:96, :], ALU.mult, reads=[s_psb[k], s_tab[b]], writes=[s_t1[i2]])
                TT("dve", t2[i2][64:96, :], psb[k2][64:96, :], tabS[b][64:96, :], ALU.mult, reads=[s_psb[k2], s_tab[b]], writes=[s_t2[i2]])
                TT("pool", t1[i2][64:96, :], t1[i2][64:96, :], t2[i2][64:96, :], ALU.add, reads=[s_t2[i2]], writes=[s_t1[i2]])
                for hh in range(8):
                    CP("pool" if hh % 2 == 0 else "act", kT[b][64:96, hh, :], t1[i2][64:96, :], reads=[s_t1[i2]], writes=[s_kTr[b][hh]])
                if DBG.get('sec', 99) < 3:
                    continue
                for g in range(4):
                    k = PB.next()
                    MM(psb[k][:], [(win[:, kc, 672 + g * P:672 + (g + 1) * P], h[:, kc, :]) for kc in range(NCH)], reads=s_h + [s_w], writes=[s_psb[k]])
                    CP("act", u[:, g, 16:16 + TB], psb[k][:], reads=[s_psb[k]], writes=[s_u[g]])
                if DBG.get('sec', 99) < 4:
                    continue
                SUB = DBG.get('sub', 99)
                for hh in range(8):
                    k = PB.next()
                    i2 = hh % 2
                    MM(psb[k][:], [(wuq[:, kc, hh * 96:hh * 96 + P], cn[:, kc, :]) for kc in range(3)], reads=s_cn[0:3] + [s_w, s_wq], writes=[s_psb[k]])
                    CP("act", qb[i2][:], psb[k][:], reads=[s_psb[k]], writes=[s_qb[i2]])
                    if SUB < 1:
                        continue
                    k2 = PB.next()
                    MM(psb[k2][:], [(rmat_bf[:, 0, :], qb[i2][:])], reads=[s_qb[i2], s_c], writes=[s_psb[k2]])
                    if SUB < 2:
                        continue
                    TT("dve", t1[i2][:], psb[k][:], tabC[b][:], ALU.mult, reads=[s_psb[k], s_tab[b]], writes=[s_t1[i2]])
                    TT("dve", t2[i2][:], psb[k2][:], tabS[b][:], ALU.mult, reads=[s_psb[k2], s_tab[b]], writes=[s_t2[i2]])
                    if SUB < 3:
                        continue
                    TT("pool", qT[b][:, hh, :], t1[i2][:], t2[i2][:], ALU.add, reads=[s_t1[i2], s_t2[i2]], writes=[s_qT[b][hh]])
                if SUB >= 4:
                    STORE("sp", q_s[0:96, :, tb * TB:(tb + 1) * TB], qT[b][0:96, :, :], f"qst{b}", reads=s_qT[b])
                if DBG.get('sec', 99) < 5:
                    continue
                for hh in range(8):
                    k = PB.next()
                    MM(psb[k][0:64, :], [(wukv[:, kc, hh * 128:hh * 128 + 64], cn[:, 3 + kc, :]) for kc in range(2)], reads=s_cn[3:5] + [s_w], writes=[s_psb[k]])
                    CP("act" if hh % 2 == 0 else "dve", kT[b][0:64, hh, :], psb[k][0:64, :], reads=[s_psb[k]], writes=[s_kT[b][hh]])
                STORE("sp", k_s[0:96, :, tb * TB:(tb + 1) * TB], kT[b][0:96, :, :], f"kst{b}", reads=s_kT[b] + s_kTr[b])
                if DBG.get('sec', 99) < 6:
                    continue
                for tt in range(4):
                    k = PB.next()
                    MM(psb[k][:].rearrange("p (h e) -> p h e", e=64), [(cn[:, 3 + kc, tt * P:(tt + 1) * P],
                                    wukv[:, kc, :].rearrange("p (h e) -> p h e", e=128)[:, :, 64:128]) for kc in range(2)],
                       reads=s_cn[3:5] + [s_w], writes=[s_psb[k]])
                    CP("dve" if tt % 2 == 0 else "act", vt[b][:, tt, :, 0:64], psb[k][:].rearrange("p (h e) -> p h e", e=64),
                       reads=[s_psb[k]], writes=[s_vt[b][tt]])
                STORE("sp", v_s[:, tb * 4 * 520:(tb + 1) * 4 * 520], vt[b][:].rearrange("p t h e -> p (t h e)"), f"vst{b}", reads=s_vt[b])
                if DBG.get('sec', 99) < 7:
                    continue
                for g in range(4):
                    w = 2 << g
                    src = u[:, g, :]
                    s_src = s_u[g]
                    sh = 1
                    lvl = 0
                    while sh < w:
                        dst = lv[lvl % 2]
                        TT("pool", dst[:, sh:16 + TB], src[:, sh:16 + TB], src[:, 0:16 + TB - sh], ALU.add,
                           reads=[s_src], writes=[s_lv[lvl % 2]])
                        src = dst
                        s_src = s_lv[lvl % 2]
                        sh *= 2
                        lvl += 1
                    STT(pooled[:, g, :], src[:, 16:16 + TB], 1.0 / w, u[:, g, 16:16 + TB], ALU.mult, ALU.subtract,
                        reads=[s_src, s_u[g]], writes=[s_pl[g]])
                    if tb == 0:
                        TT("dve", ptmp[:], src[:, 16:32], cntinv[:, g, :], ALU.mult, reads=[s_src, s_w], writes=[s_ptmp])
                        TT("dve", pooled[:, g, 0:16], ptmp[:], u[:, g, 16:32], ALU.subtract, reads=[s_ptmp, s_u[g], s_pl[g]], writes=[s_pl[g]])
                    k = PB.next()
                    MM(psb[k][:], [(poolw[:, g, :], pooled[:, g, :])], reads=[s_pl[g], s_w], writes=[s_psb[k]])
                    TS("dve", catp[b][:, g, :], psb[k][:], poolb[:, g:g + 1], pools[:, g:g + 1], ALU.add, ALU.mult,
                       reads=[s_psb[k], s_w], writes=[s_catp[b][g]])
                    CP("pool", u[:, g, 0:16], u[:, g, TB:TB + 16], reads=[], writes=[s_u[g]])
                STORE("sp", cat_s[:, 4:8, tb * TB:(tb + 1) * TB], catp[b][:], f"cpst{b}", reads=s_catp[b])
            R.barrier()
            R.replay()
        if stop == "L0P1":
            return nc

        with contextlib.ExitStack() as st:
            sb = lambda name, shape, dt: st.enter_context(nc.sbuf_tensor(_un(name), shape, dt))
            stg = [sb(f"stg{i}", [P, NCH, 256], BF16) for i in range(3)]
            s_stg = [Slot() for _ in range(3)]
            va = sb("va", [P, 32, 8, 65], BF16)
            s_va = Slot()
            LOAD("sp", va[:].rearrange("p t h e -> p (t h e)"), v_s[:, 0:32 * 520], "va", [s_va])
            vg = sb("vg", [P, 32, 8, P], BF16)
            s_vgh = [Slot() for _ in range(8)]
            R.op("pool", lambda e: e.memset(vg[:], 1.0), writes=s_vgh)
            for hh in range(8):
                off = 0 if hh % 2 == 0 else 64
                CP("pool" if hh % 2 == 0 else "dve", vg[:, :, hh, off:off + 64], va[:, :, hh, 0:64], reads=[s_va], writes=[s_vgh[hh]])
            qh = [sb(f"qh{i}", [P, S], BF16) for i in range(2)]
            kh = [sb(f"kh{i}", [P, S], BF16) for i in range(2)]
            s_qk = [Slot() for _ in range(2)]
            rinv = [sb(f"rinv{i}", [P, TB], F32) for i in range(2)]
            s_rinv = [Slot() for _ in range(2)]
            attn = sb("attn", [P, 4, S], BF16)
            s_attn = [Slot() for _ in range(4)]
            NPT = 6
            pt = [sb(f"ptx{i}", [P, TB], BF16) for i in range(NPT)]
            s_pt = [Slot() for _ in range(NPT)]
            SBK = [0, 1, 2, 3]
            OBK = [4, 5]
            LA = 2
            items = []
            for hh in range(8):
                for qblk in range(NTB):
                    nkt = 4 * qblk + 4
                    for kt in range(nkt):
                        items.append((hh, qblk, kt, nkt))
            prep_done = [0]

            def load_head(hh):
                i = hh % 2
                LOAD("sp", qh[i][0:96, :], q_s[0:96, hh, :], f"qh{i}", [s_qk[i]])
                R.dma("sp", kh[i][0:96, :], k_s[0:96, hh, :], f"qh{i}", writes=[s_qk[i]], append_w=True)

            def emitA(n):
                hh, qblk, kt, nkt = items[n]
                i = hh % 2
                if qblk == 0 and kt == 0:
                    if hh == 0:
                        load_head(0)
                    if hh + 1 < 8:
                        load_head(hh + 1)
                j = kt - 4 * qblk
                c0 = max(j, 0) * P
                sbk = SBK[n % len(SBK)]
                MM(psb[sbk][:, c0:TB], [(kh[i][0:96, kt * P:(kt + 1) * P], qh[i][0:96, qblk * TB + c0:(qblk + 1) * TB])],
                   reads=[s_qk[i]], writes=[s_psb[sbk]])
                p_i = n % NPT
                ACT(pt[p_i][:, c0:TB], psb[sbk][:, c0:TB], AF.Exp, reads=[s_psb[sbk]], writes=[s_pt[p_i]], scale=SC0)
                if j >= 0:
                    TT("pool", pt[p_i][:, c0:c0 + P], pt[p_i][:, c0:c0 + P], tri_bf[:], ALU.mult, reads=[s_pt[p_i], s_c], writes=[s_pt[p_i]])

            def emitC(n):
                hh, qblk, kt, nkt = items[n]
                blk = hh * NTB + qblk
                ob = OBK[blk % 2]
                j = kt - 4 * qblk
                c0 = max(j, 0) * P
                p_i = n % NPT
                R.group("pe", [lambda e: e.matmul(psb[ob][:, c0:TB], lhsT=vg[:, kt, hh, :], rhs=pt[p_i][:, c0:TB],
                                                  start=(kt == 0), stop=(kt == nkt - 1))],
                        reads=[s_pt[p_i], s_vgh[hh]], writes=[s_psb[ob]])
                if kt == nkt - 1:
                    ri = blk % 2
                    odd = hh % 2
                    if odd == 0:
                        RECIP(rinv[ri][0:64, :], psb[ob][64:128, :], reads=[s_psb[ob]], writes=[s_rinv[ri]])
                        TT("dve", attn[0:64, hh // 2, qblk * TB:(qblk + 1) * TB], psb[ob][0:64, :], rinv[ri][0:64, :], ALU.mult,
                           reads=[s_psb[ob], s_rinv[ri]], writes=[s_attn[hh // 2]])
                    else:
                        RECIP(rinv[ri][64:128, :], psb[ob][0:64, :], reads=[s_psb[ob]], writes=[s_rinv[ri]])
                        TT("dve", attn[64:128, hh // 2, qblk * TB:(qblk + 1) * TB], psb[ob][64:128, :], rinv[ri][64:128, :], ALU.mult,
                           reads=[s_psb[ob], s_rinv[ri]], writes=[s_attn[hh // 2]])
                    if n_layers > 1 and prep_done[0] < NF and blk % 2 == 0:
                        wgu_prep(1, stg, s_stg, [prep_done[0]])
                        prep_done[0] += 1
                    if odd and qblk == NTB - 1:
                        STORE("sp", cat_s[:, hh // 2, :], attn[:, hh // 2, :], "ast", reads=[s_attn[hh // 2]])

            for n in range(len(items) + LA):
                if n < len(items):
                    emitA(n)
                if n - LA >= 0:
                    emitC(n - LA)
            R.barrier()
            R.replay()
        if stop == "L0P2":
            return nc

        post_phase(0, woutA_in, final=(n_layers == 1))
        if stop == "L0P3":
            return nc

        if n_layers > 1:
            SC1 = 64.0 ** -0.5
            with contextlib.ExitStack() as st:
                sb = lambda name, shape, dt: st.enter_context(nc.sbuf_tensor(_un(name), shape, dt))
                wqkv = sb("wqkv", [P, NCH, 3 * D], BF16)
                s_w = Slot()
                for j in range(3):
                    R.dma("pool", wqkv[:, :, j * D:(j + 1) * D], wqkv_in[:, j * D:(j + 1) * D].rearrange("(kc p) n -> p kc n", p=P),
                          "w0", writes=[s_w], append_w=(j > 0))
                xb = [sb(f"xb{i}", [P, NCH, TB], F32) for i in range(2)]
                s_xb = [Slot() for _ in range(2)]
                tabC = [sb(f"tabC{i}", [P, TB], F32) for i in range(2)]
                tabS = [sb(f"tabS{i}", [P, TB], F32) for i in range(2)]
                s_tab = [Slot() for _ in range(2)]
                sq = sb("sq", [P, NCH, TB], BF16)
                s_sq = [Slot() for _ in range(NCH)]
                rstd = sb("rstd", [P, TB], F32)
                s_rstd = Slot()
                tmp = [sb(f"tmp{i}", [P, TB], F32) for i in range(2)]
                s_tmp = [Slot() for _ in range(2)]
                h = sb("h", [P, NCH, TB], BF16)
                s_h = [Slot() for _ in range(NCH)]
                qb = [sb(f"qb{i}", [P, TB], BF16) for i in range(2)]
                s_qb = [Slot() for _ in range(2)]
                t1 = [sb(f"t1{i}", [P, TB], F32) for i in range(2)]
                s_t1 = [Slot() for _ in range(2)]
                t2 = [sb(f"t2{i}", [P, TB], F32) for i in range(2)]
                s_t2 = [Slot() for _ in range(2)]
                qk = [sb(f"qk{i}", [P, 16, TB], BF16) for i in range(2)]
                s_qkT = [[Slot() for c in range(16)] for _ in range(2)]
                vt = [sb(f"vt{i}", [P, 4, D], BF16) for i in range(2)]
                s_vt = [[Slot() for c in range(8)] for _ in range(2)]
                PB = Banks(range(8))
                for tb in range(NTB):
                    b = tb % 2
                    LOAD("sp", xb[b][:], xT_s[:, :, tb * TB:(tb + 1) * TB], f"xb{b}", [s_xb[b]])
                    R.dma("sp", tabC[b][:], tab_s[2, :, tb * TB:(tb + 1) * TB], f"tab{b}", writes=[s_tab[b]])
                    R.dma("sp", tabS[b][:], tab_s[3, :, tb * TB:(tb + 1) * TB], f"tab{b}", writes=[s_tab[b]], append_w=True)
                    norm1_block(1, xb[b], s_xb[b], sq, s_sq, rstd, s_rstd, tmp, s_tmp, h, s_h, PB)
                    for c in range(16):
                        k = PB.next()
                        i2 = c % 2
                        MM(psb[k][:], [(wqkv[:, kc, c * P:(c + 1) * P], h[:, kc, :]) for kc in range(NCH)], reads=s_h + [s_w], writes=[s_psb[k]])
                        CP("act", qb[i2][:], psb[k][:], reads=[s_psb[k]], writes=[s_qb[i2]])
                        k2 = PB.next()
                        MM(psb[k2][:], [(rmat_bf[:, 1, :], qb[i2][:])], reads=[s_qb[i2], s_c], writes=[s_psb[k2]])
                        TT("dve", t1[i2][:], psb[k][:], tabC[b][:], ALU.mult, reads=[s_psb[k], s_tab[b]], writes=[s_t1[i2]])
                        TT("dve", t2[i2][:], psb[k2][:], tabS[b][:], ALU.mult, reads=[s_psb[k2], s_tab[b]], writes=[s_t2[i2]])
                        TT("pool", qk[b][:, c, :], t1[i2][:], t2[i2][:], ALU.add, reads=[s_t1[i2], s_t2[i2]], writes=[s_qkT[b][c]])
                    STORE("sp", q_s[:, :, tb * TB:(tb + 1) * TB], qk[b][:, 0:8, :], f"qst{b}", reads=s_qkT[b][0:8])
                    STORE("sp", k_s[:, :, tb * TB:(tb + 1) * TB], qk[b][:, 8:16, :], f"kst{b}", reads=s_qkT[b][8:16])
                    for tt in range(4):
                        for half in range(2):
                            k = PB.next()
                            MM(psb[k][:], [(h[:, kc, tt * P:(tt + 1) * P], wqkv[:, kc, 2 * D + half * TB:2 * D + (half + 1) * TB])
                                            for kc in range(NCH)], reads=s_h + [s_w], writes=[s_psb[k]])
                            CP("act" if half == 0 else "dve", vt[b][:, tt, half * TB:(half + 1) * TB], psb[k][:], reads=[s_psb[k]],
                               writes=[s_vt[b][tt * 2 + half]])
                    STORE("sp", v_s[:, tb * 4 * D:(tb + 1) * 4 * D], vt[b][:].rearrange("p t d -> p (t d)"), f"vst{b}", reads=s_vt[b])
                R.barrier()
                R.replay()
            if stop == "L1P1":
                return nc

            with contextlib.ExitStack() as st:
                sb = lambda name, shape, dt: st.enter_context(nc.sbuf_tensor(_un(name), shape, dt))
                va = sb("va", [P, 32, D], BF16)
                s_va = Slot()
                LOAD("sp", va[:].rearrange("p t d -> p (t d)"), v_s[:, :], "va", [s_va])
                qh = [sb(f"qh{i}", [P, S], BF16) for i in range(2)]
                kh = [sb(f"kh{i}", [P, S], BF16) for i in range(2)]
                s_qk = [Slot() for _ in range(2)]
                rinv = [sb(f"rinv{i}", [P, TB], F32) for i in range(2)]
                s_rinv = [Slot() for _ in range(2)]
                a0 = sb("a0", [P, TB], F32)
                a1 = sb("a1", [P, TB], F32)
                s_a0, s_a1 = Slot(), Slot()
                sqa = sb("sqa", [P, 1, TB], BF16)
                s_sqa = [Slot()]
                rsa = sb("rsa", [P, TB], F32)
                s_rsa = Slot()
                attn = [sb(f"attn{i}", [P, S], BF16) for i in range(2)]
                s_attn = [Slot() for _ in range(2)]
                NPT = 5
                ptp = [sb(f"pty{i}", [P, 2, TB], BF16) for i in range(NPT)]
                s_pt = [Slot() for _ in range(NPT)]
                tri2 = sb("tri2", [P, 2, P], BF16)
                s_tri2 = Slot()
                CP("pool", tri2[:, 0, :], tri_bf[:], reads=[s_c], writes=[s_tri2])
                CP("pool", tri2[:, 1, :], tri_bf[:], reads=[s_c, s_tri2], writes=[s_tri2])
                ones_f = sb("ones_f", [P, P], F32)
                s_onesf = Slot()
                R.op("pool", lambda e: e.memset(ones_f[:], 1.0), writes=[s_onesf])
                EA = [[sb(f"EA{b}{k}", [P, TB], F32) for k in range(3)] for b in range(2)]
                s_EA = [[Slot() for k in range(3)] for b in range(2)]
                rv = [sb(f"rv{i}", [P, TB], F32) for i in range(2)]
                s_rv = [Slot() for _ in range(2)]
                PAIRS = [0, 2]
                LA = 2
                items = []
                for hh in range(8):
                    for qblk in range(NTB):
                        nkt = 4 * qblk + 4
                        for kt in range(nkt):
                            items.append((hh, qblk, kt, nkt))
                        items.append((hh, qblk, -1, nkt))
                        items.append((hh, qblk, -2, nkt))

                def load_head(hh):
                    i = hh % 2
                    LOAD("sp", qh[i][:], q_s[:, hh, :], f"qh{i}", [s_qk[i]])
                    R.dma("sp", kh[i][:], k_s[:, hh, :], f"qh{i}", writes=[s_qk[i]], append_w=True)

                def emitA(n):
                    hh, qblk, kt, nkt = items[n]
                    if kt < 0:
                        return
                    i = hh % 2
                    blk = hh * NTB + qblk
                    bb = blk % 2
                    if qblk == 0 and kt == 0:
                        if hh == 0:
                            load_head(0)
                        if hh + 1 < 8:
                            load_head(hh + 1)
                    j = kt - 4 * qblk
                    c0 = max(j, 0) * P
                    p0 = PAIRS[n % 2]
                    R.group("pe", [lambda e: e.matmul(psb[p0][:, c0:TB], lhsT=kh[i][0:64, kt * P:(kt + 1) * P],
                                                      rhs=qh[i][0:64, qblk * TB + c0:(qblk + 1) * TB], start=True, stop=True),
                                   lambda e: e.matmul(psb[p0 + 1][:, c0:TB], lhsT=kh[i][64:128, kt * P:(kt + 1) * P],
                                                      rhs=qh[i][64:128, qblk * TB + c0:(qblk + 1) * TB], start=True, stop=True)],
                            reads=[s_qk[i]], writes=[s_psb[p0], s_psb[p0 + 1]])
                    p_i = n % NPT
                    ACT(ptp[p_i][:, :, c0:TB], psall[:, p0:p0 + 2, c0:TB], AF.Exp, reads=[s_psb[p0], s_psb[p0 + 1]],
                        writes=[s_pt[p_i]], scale=SC1)
                    if j >= 0:
                        TT("pool", ptp[p_i][:, :, c0:c0 + P], ptp[p_i][:, :, c0:c0 + P], tri2[:], ALU.mult,
                           reads=[s_pt[p_i], s_tri2], writes=[s_pt[p_i]])
                    if kt == 0:
                        CP("dve", EA[bb][0][:, c0:TB], ptp[p_i][:, 0, c0:TB], reads=[s_pt[p_i]], writes=[s_EA[bb][0]])
                        CP("dve", EA[bb][1][:, c0:TB], ptp[p_i][:, 1, c0:TB], reads=[s_pt[p_i]], writes=[s_EA[bb][1]])
                    else:
                        TT("dve", EA[bb][0][:, c0:TB], EA[bb][0][:, c0:TB], ptp[p_i][:, 0, c0:TB], ALU.add,
                           reads=[s_pt[p_i]], writes=[s_EA[bb][0]])
                        if kt % 2 == 0:
                            TT("dve", EA[bb][1][:, c0:TB], EA[bb][1][:, c0:TB], ptp[p_i][:, 1, c0:TB], ALU.add,
                               reads=[s_pt[p_i]], writes=[s_EA[bb][1]])
                        elif kt == 1:
                            if c0 > 0:
                                R.op("pool", lambda e: e.memset(EA[bb][2][:, 0:c0], 0.0), writes=[s_EA[bb][2]])
                            CP("pool", EA[bb][2][:, c0:TB], ptp[p_i][:, 1, c0:TB], reads=[s_pt[p_i]], writes=[s_EA[bb][2]])
                        else:
                            TT("pool", EA[bb][2][:, c0:TB], EA[bb][2][:, c0:TB], ptp[p_i][:, 1, c0:TB], ALU.add,
                               reads=[s_pt[p_i]], writes=[s_EA[bb][2]])

                def emitC(n):
                    hh, qblk, kt, nkt = items[n]
                    i = hh % 2
                    blk = hh * NTB + qblk
                    bb = blk % 2
                    o0, o1 = (4, 5) if bb == 0 else (6, 7)
                    p0 = PAIRS[n % 2]
                    if kt >= 0:
                        j = kt - 4 * qblk
                        c0 = max(j, 0) * P
                        p_i = n % NPT
                        R.group("pe", [lambda e: e.matmul(psb[o0][:, c0:TB], lhsT=va[:, kt, hh * P:(hh + 1) * P], rhs=ptp[p_i][:, 0, c0:TB],
                                                          start=(kt == 0), stop=(kt == nkt - 1)),
                                       lambda e: e.matmul(psb[o1][:, c0:TB], lhsT=va[:, kt, hh * P:(hh + 1) * P], rhs=ptp[p_i][:, 1, c0:TB],
                                                          start=(kt == 0), stop=(kt == nkt - 1))],
                                reads=[s_pt[p_i], s_va], writes=[s_psb[o0], s_psb[o1]])
                    elif kt == -1:
                        R.group("pe", [lambda e: e.matmul(psb[p0][:], lhsT=ones_f[:], rhs=EA[bb][0][:], start=True, stop=True),
                                       lambda e: e.matmul(psb[p0 + 1][:], lhsT=ones_f[:], rhs=EA[bb][1][:], start=True, stop=False),
                                       lambda e: e.matmul(psb[p0 + 1][:], lhsT=ones_f[:], rhs=EA[bb][2][:], start=False, stop=True)],
                                reads=[s_EA[bb][0], s_EA[bb][1], s_EA[bb][2], s_onesf], writes=[s_psb[p0], s_psb[p0 + 1]])
                        for c in range(2):
                            ACT(rv[c][:], psb[p0 + c][:], AF.Ln, reads=[s_psb[p0 + c]], writes=[s_rv[c]])
                            ACT(rv[c][:], rv[c][:], AF.Exp, reads=[s_rv[c]], writes=[s_rv[c]], scale=-1.0)
                        TT("dve", a0[:], psb[o0][:], rv[0][:], ALU.mult, reads=[s_psb[o0], s_rv[0]], writes=[s_a0])
                        TT("dve", a1[:], psb[o1][:], rv[1][:], ALU.mult, reads=[s_psb[o1], s_rv[1]], writes=[s_a1])
                        STT(a0[:], a1[:], lam_neg[:, 0:1], a0[:], ALU.mult, ALU.add, reads=[s_a1, s_a0, s_lam], writes=[s_a0])
                        TT("dve", sqa[:, 0, :], a0[:], a0[:], ALU.mult, reads=[s_a0], writes=[s_sqa[0]])
                    else:
                        MM(psb[p0][:], [(ones_bf[:], sqa[:, 0, :])], reads=[s_sqa[0], s_ones], writes=[s_psb[p0]])
                        ACT(rsa[:], psb[p0][:], AF.Ln, reads=[s_psb[p0], s_ones], writes=[s_rsa], scale=1.0 / 128, bias=epsc[:])
                        ACT(rsa[:], rsa[:], AF.Exp, reads=[s_rsa], writes=[s_rsa], scale=-0.5)
                        STT(attn[i][:, qblk * TB:(qblk + 1) * TB], a0[:], subg2[:, 0:1], rsa[:], ALU.mult, ALU.mult,
                            reads=[s_a0, s_rsa, s_lam], writes=[s_attn[i]])
                        if qblk == NTB - 1:
                            STORE("sp", cat_s[:, hh, :], attn[i][:], f"ast{i}", reads=[s_attn[i]])

                for n in range(len(items) + LA):
                    if n < len(items):
                        emitA(n)
                    if n - LA >= 0:
                        emitC(n - LA)
                R.barrier()
                R.replay()
            if stop == "L1P2":
                return nc

            post_phase(1, woutD_in, final=True)

        gst.callback(lambda: None)
    return nc


def _consts():
    ident = np.eye(P, dtype=np.float32)
    tri = (np.arange(P)[None, :] >= np.arange(P)[:, None]).astype(np.float32)
    rm = np.zeros((2, P, P), np.float32)
    for i in range(16):
        rm[0, 80 + i, 64 + i] = -1.0
        rm[0, 64 + i, 80 + i] = 1.0
    for c in range(2):
        for i in range(8):
            rm[1, c * 64 + 8 + i, c * 64 + i] = -1.0
            rm[1, c * 64 + i, c * 64 + 8 + i] = 1.0
    theta = np.float32(500000.0)
    inv32 = (theta ** (-(np.arange(0, 32, 2, dtype=np.float32) / np.float32(32)))).astype(np.float32)
    inv16 = (theta ** (-(np.arange(0, 16, 2, dtype=np.float32) / np.float32(16)))).astype(np.float32)
    invrows = np.zeros((P, 2), np.float32)
    for i in range(16):
        invrows[64 + i, 0] = inv32[i]
        invrows[80 + i, 0] = inv32[i]
    for c in range(2):
        for i in range(8):
            invrows[c * 64 + i, 1] = inv16[i]
            invrows[c * 64 + 8 + i, 1] = inv16[i]
    cnt = np.zeros((P, 4, 16), np.float32)
    for g, w in enumerate((2, 4, 8, 16)):
        cnt[:, g, :] = (1.0 / np.minimum(np.arange(1, 17), w)).astype(np.float32)[None, :]
    return ident, tri, rm, invrows, cnt


def _cols(v, n):
    return np.ascontiguousarray(np.asarray(v, np.float32).reshape(n, P).T)


_NC_CACHE = {}


def kernel(**inputs):
    g = lambda k: np.asarray(inputs[k])
    x = g("x").astype(np.float32)
    ident, tri, rm, invrows, cnt = _consts()
    if "nc" not in _NC_CACHE:
        _NC_CACHE["nc"] = build_program()
    nc = _NC_CACHE["nc"]
    shared = {
        "ada_w": np.ascontiguousarray(g("ada_w"), dtype=np.float32),
        "adab": np.ascontiguousarray(g("ada_b").astype(np.float32).reshape(2, 48, P).transpose(0, 2, 1)),
        "n1g": np.ascontiguousarray(g("norm1_g").astype(np.float32).reshape(2, NCH, P).transpose(0, 2, 1)),
        "n2g": np.ascontiguousarray(g("norm2_g").astype(np.float32).reshape(2, NCH, P).transpose(0, 2, 1)),
        "fng": _cols(g("final_norm_g"), NCH),
        "w_gu": np.ascontiguousarray(g("ffn_w_gate_up"), dtype=np.float32),
        "w_d": np.ascontiguousarray(g("ffn_w_down"), dtype=np.float32),
        "w_in": np.ascontiguousarray(g("mla_w_in")[0], dtype=np.float32),
        "qng": _cols(g("mla_q_norm_g")[0], 3),
        "kvng": _cols(g("mla_kv_norm_g")[0], 2),
        "w_uq": np.ascontiguousarray(g("mla_w_uq")[0], dtype=np.float32),
        "w_ukv": np.ascontiguousarray(g("mla_w_ukv")[0], dtype=np.float32),
        "pool_w": np.ascontiguousarray(g("pool_w")[0], dtype=np.float32),
        "pool_b": _cols(g("pool_b")[0].reshape(-1), 4),
        "pool_s": _cols(g("pool_scale")[0], 4),
        "w_outA": np.ascontiguousarray(g("mix_a_w_out")[0], dtype=np.float32),
        "w_qkv": np.ascontiguousarray(g("diff_w_qkv")[0], dtype=np.float32),
        "lamv": np.ascontiguousarray(np.broadcast_to(np.stack([
            g("diff_lambda_q1")[0], g("diff_lambda_k1")[0], g("diff_lambda_q2")[0], g("diff_lambda_k2")[0]]).astype(np.float32)[None],
            (P, 4, 64))),
        "subg": np.ascontiguousarray(g("diff_subln_g")[0].astype(np.float32).reshape(P, 1)),
        "w_outD": np.ascontiguousarray(g("diff_w_out")[0], dtype=np.float32),
        "ident_f": ident, "tri": tri, "rmat": rm, "invrows": invrows, "cntinv": cnt,
    }
    c = g("c").astype(np.float32)
    pos = g("positions").astype(np.int32)
    in_maps = []
    ncores = int(_NC_CACHE.get("ncores", 8))
    for b in range(ncores):
        m = dict(shared)
        m["x"] = np.ascontiguousarray(x[b])
        m["cT"] = _cols(c[b], NCH)
        m["pos"] = np.ascontiguousarray(pos[b][None, :])
        in_maps.append(m)
    res = run_bass_kernel_spmd(nc, in_maps, core_ids=list(range(ncores)))
    return np.stack([np.asarray(r["out"], dtype=np.float32) for r in res.results], axis=0)
```

```python
import contextlib
import math
import numpy as np
import ml_dtypes
import concourse.bass as bass
import concourse.mybir as mybir
from concourse.bass_utils import run_bass_kernel_spmd

F32 = mybir.dt.float32
BF16 = mybir.dt.bfloat16
I32 = mybir.dt.int32
AF = mybir.ActivationFunctionType
ALU = mybir.AluOpType
AX = mybir.AxisListType

P = 128
S = 4096
D = 1024
NCH = D // P
TB = 512
NTB = S // TB
EPS = 1e-6


class Sem:
    def __init__(self, nc, stack, name):
        self.h = stack.enter_context(nc.semaphore(name))
        self.val = 0
        self.name = name


class Slot:
    __slots__ = ("w", "r", "name", "excl")

    def __init__(self, name="", excl=False):
        self.w = []
        self.r = []
        self.name = name
        self.excl = excl


class Queue:
    def __init__(self, name, sem):
        self.name = name
        self.sem = sem
        self.ops = []
        self.seen = {}


class Rec:
    def __init__(self, nc, stack):
        self.nc = nc
        self.stack = stack
        self.q = {}
        for n in ("pe", "act", "dve", "pool", "sp"):
            self.q[n] = Queue(n, Sem(nc, stack, "q_" + n))
        self.dma_sems = {}
        self.all_dma_toks = []

    def dsem(self, name):
        if name not in self.dma_sems:
            self.dma_sems[name] = Sem(self.nc, self.stack, "d_" + name)
        return self.dma_sems[name]

    def _deps(self, reads, writes):
        deps = []
        for s in reads:
            deps += s.w
            if s.excl:
                deps += s.r
        for s in writes:
            deps += s.w
            deps += s.r
        return deps

    def _prune(self, q, deps):
        best = {}
        for (sem, v) in deps:
            if q.name == "pe" and sem is q.sem:
                continue
            if v > q.seen.get(sem, 0) and v > best.get(sem, (None, 0))[1]:
                best[sem] = (sem, v)
        out = list(best.values())
        for (sem, v) in out:
            q.seen[sem] = v
        return out

    def op(self, qn, fn, reads=(), writes=(), extra=(), signal=True):
        q = self.q[qn]
        deps = self._deps(reads, writes) + list(extra)
        waits = self._prune(q, deps)
        tok = None
        if signal:
            q.sem.val += 1
            tok = (q.sem, q.sem.val)
            q.ops.append((waits, fn, q.sem, 1))
            for s in reads:
                s.r.append(tok)
            for s in writes:
                s.w = [tok]
                s.r = []
        else:
            q.ops.append((waits, fn, None, 0))
        return tok

    def group(self, qn, fns, reads=(), writes=(), extra=()):
        q = self.q[qn]
        deps = self._deps(reads, writes) + list(extra)
        waits = self._prune(q, deps)
        q.sem.val += 1
        tok = (q.sem, q.sem.val)
        n = len(fns)
        for i, fn in enumerate(fns):
            q.ops.append((waits if i == 0 else [], fn,
                          q.sem if i == n - 1 else None, 1))
        for s in reads:
            s.r.append(tok)
        for s in writes:
            s.w = [tok]
            s.r = []
        return tok

    def dma(self, qn, out, in_, semname, reads=(), writes=(), extra=(), append_w=False):
        q = self.q[qn]
        sem = self.dsem(semname)
        deps = self._deps(reads, [] if append_w else writes) + list(extra)
        waits = self._prune(q, deps)
        sem.val += 16
        tok = (sem, sem.val)

        def fn(eng, out=out, in_=in_):
            return eng.dma_start(out=out, in_=in_)
        q.ops.append((waits, fn, sem, 16))
        for s in reads:
            s.r.append(tok)
        for s in writes:
            if append_w:
                s.w = s.w + [tok]
            else:
                s.w = [tok]
                s.r = []
        self.all_dma_toks.append(tok)
        return tok

    def barrier(self):
        toks = [(q.sem, q.sem.val) for q in self.q.values() if q.sem.val > 0]
        toks += [(s, s.val) for s in self.dma_sems.values() if s.val > 0]
        for qn, q in self.q.items():
            waits = self._prune(q, toks)
            if waits:
                q.ops.append((waits, None, None, 0))

    def replay(self):
        nc = self.nc
        with nc.Block() as block:
            def run(q):
                def body(eng):
                    for (waits, fn, isem, amt) in q.ops:
                        for (sem, v) in waits:
                            eng.wait_ge(sem.h, v)
                        if fn is None:
                            continue
                        ins = fn(eng)
                        if isem is not None:
                            ins.then_inc(isem.h, amt)
                return body
            block.tensor(run(self.q["pe"]))
            block.scalar(run(self.q["act"]))
            block.vector(run(self.q["dve"]))
            block.gpsimd(run(self.q["pool"]))
            block.sync(run(self.q["sp"]))
        for q in self.q.values():
            q.ops = []


HQ = 8
DFF = 2816
NF = DFF // P
MAGIC = 12582912.0
C1 = 6.28125
C2 = 2.0 * math.pi - 6.28125
INV2PI = 1.0 / (2.0 * math.pi)
LAM_INIT1 = 0.8 - 0.6 * math.exp(-0.3 * 1)
DBG = {}


def build_program(n_layers=2, stop=None):
    nc = bass.Bass("TRN2", target_bir_lowering=False)

    def din(name, shape, dt=F32):
        return nc.dram_tensor(name, shape, dt, kind="ExternalInput").ap()

    def dscr(name, shape, dt):
        return nc.dram_tensor(name, shape, dt, kind="Internal").ap()

    x_in = din("x", [S, D])
    cT_in = din("cT", [P, NCH])
    pos_in = din("pos", [1, S], I32)
    ada_w = din("ada_w", [2, D, 6 * D])
    adab_in = din("adab", [2, P, 48])
    n1g_in = din("n1g", [2, P, NCH])
    n2g_in = din("n2g", [2, P, NCH])
    fng_in = din("fng", [P, NCH])
    wgu_in = din("w_gu", [2, D, 2 * DFF])
    wd_in = din("w_d", [2, DFF, D])
    win_in = din("w_in", [D, 1184])
    qng_in = din("qng", [P, 3])
    kvng_in = din("kvng", [P, 2])
    wuq_in = din("w_uq", [384, 768])
    wukv_in = din("w_ukv", [256, 1024])
    poolw_in = din("pool_w", [4, P, P])
    poolb_in = din("pool_b", [P, 4])
    pools_in = din("pool_s", [P, 4])
    woutA_in = din("w_outA", [D, D])
    wqkv_in = din("w_qkv", [D, 3 * D])
    lamv_in = din("lamv", [P, 4, 64])
    subg_in = din("subg", [P, 1])
    woutD_in = din("w_outD", [D, D])
    identf_in = din("ident_f", [P, P])
    tri_in = din("tri", [P, P])
    maskb_in = din("maskb", [P, P])
    rmat_in = din("rmat", [2, P, P])
    invrows_in = din("invrows", [P, 2])
    cntinv_in = din("cntinv", [P, 4, 16])
    out = nc.dram_tensor("out", [S, D], F32, kind="ExternalOutput").ap()

    xT_s = dscr("xT_s", [P, NCH, S], F32)
    tab_s = dscr("tab_s", [4, P, S], F32)
    wgu_s = dscr("wgu_s", [2, NF, P, NCH * 256], BF16)
    q_s = dscr("q_s", [P, 8, S], BF16)
    k_s = dscr("k_s", [P, 8, S], BF16)
    v_s = dscr("v_s", [P, 32 * 1024], BF16)
    cat_s = dscr("cat_s", [P, NCH, S], BF16)

    with contextlib.ExitStack() as gst:
        R = Rec(nc, gst)
        _uid = [0]

        def _un(name):
            _uid[0] += 1
            return f"sb{_uid[0]}_{name}"
        gsb = lambda name, shape, dt: gst.enter_context(nc.sbuf_tensor(_un(name), shape, dt))
        psall = gst.enter_context(nc.psum_tensor("psall", [P, 8, TB], F32))
        psb = [psall[:, i, :] for i in range(8)]
        s_psb = [Slot(f"psb{i}", excl=True) for i in range(8)]

        class Banks:
            def __init__(self, ids):
                self.ids = list(ids)
                self.i = 0

            def next(self):
                b = self.ids[self.i % len(self.ids)]
                self.i += 1
                return b

        def ACT(out_, in_, func, reads, writes, **kw):
            return R.op("act", lambda e: e.activation(out=out_, in_=in_, func=func, **kw), reads=reads, writes=writes)

        def TT(q, out_, in0, in1, op, reads, writes):
            return R.op(q, lambda e: e.tensor_tensor(out=out_, in0=in0, in1=in1, op=op), reads=reads, writes=writes)

        def TS(q, out_, in0, s1, s2, op0, op1, reads, writes):
            if op1 is None:
                return R.op(q, lambda e: e.tensor_single_scalar(out=out_, in_=in0, scalar=s1, op=op0), reads=reads, writes=writes)
            return R.op(q, lambda e: e.tensor_scalar(out=out_, in0=in0, scalar1=s1, scalar2=s2, op0=op0, op1=op1),
                        reads=reads, writes=writes)

        def STT(out_, in0, scalar, in1, op0, op1, reads, writes):
            return R.op("dve", lambda e: e.scalar_tensor_tensor(out=out_, in0=in0, scalar=scalar, in1=in1, op0=op0, op1=op1),
                        reads=reads, writes=writes)

        def CP(q, out_, in_, reads, writes):
            if q == "act":
                return R.op("act", lambda e: e.copy(out=out_, in_=in_), reads=reads, writes=writes)
            return R.op(q, lambda e: e.tensor_copy(out=out_, in_=in_), reads=reads, writes=writes)

        def MM(out_, pairs, reads, writes):
            n = len(pairs)
            fns = []
            for i, (l, r) in enumerate(pairs):
                fns.append(lambda e, l=l, r=r, i=i: e.matmul(out_, lhsT=l, rhs=r, start=(i == 0), stop=(i == n - 1)))
            return R.group("pe", fns, reads=reads, writes=writes)

        def RECIP(out_, in_, reads, writes):
            return R.op("dve", lambda e: e.reciprocal(out=out_, in_=in_), reads=reads, writes=writes)

        def LOAD(q, out_, in_, sem, writes, reads=()):
            return R.dma(q, out_, in_, sem, reads=reads, writes=writes)

        def STORE(q, out_, in_, sem, reads):
            return R.dma(q, out_, in_, sem, reads=reads, writes=[Slot()])

        identf = gsb("identf", [P, P], F32)
        ones_bf = gsb("ones_bf", [P, P], BF16)
        tri_bf = gsb("tri_bf", [P, P], BF16)
        rmat_bf = gsb("rmat_bf", [P, 2, P], BF16)
        maskb_bf = gsb("maskb_bf", [P, P], BF16)
        ident_bf = gsb("ident_bf", [P, P], BF16)
        fng = gsb("fng", [P, NCH], F32)
        AB = gsb("AB", [P, 2, 6, NCH], F32)
        halfpi = gsb("halfpi", [P, 1], F32)
        epsc = gsb("epsc", [P, 1], F32)
        lam_neg = gsb("lam_neg", [P, 1], F32)
        subg2 = gsb("subg2", [P, 1], F32)
        s_c = Slot("consts")
        s_ones = Slot("ones")
        s_AB = Slot("AB")
        s_lam = Slot("lam")
        LOAD("sp", identf[:], identf_in, "c0", [s_c])
        R.dma("sp", fng[:], fng_in, "c0", writes=[s_c], append_w=True)
        R.dma("pool", tri_bf[:], tri_in, "c1", writes=[s_c], append_w=True)
        R.dma("pool", maskb_bf[:], maskb_in, "c1", writes=[s_c], append_w=True)
        R.dma("pool", ident_bf[:], identf_in, "c1", writes=[s_c], append_w=True)
        R.dma("pool", rmat_bf[:], rmat_in.rearrange("j p m -> p j m"), "c1", writes=[s_c], append_w=True)
        R.op("pool", lambda e: e.memset(ones_bf[:], 1.0), writes=[s_ones])
        R.op("pool", lambda e: e.memset(halfpi[:], math.pi / 2.0), writes=[s_ones])
        s_ones.w = [(R.q["pool"].sem, R.q["pool"].sem.val)]
        R.op("pool", lambda e: e.memset(epsc[:], EPS), writes=[Slot()])
        s_ones.w = [(R.q["pool"].sem, R.q["pool"].sem.val)]

        def rms_rstd(sq_tile, s_sq, src_chunks, s_src, n, dim, bank, rstd_tile, s_rstd, np_=P, sq_eng="act"):
            for c in range(n):
                if sq_eng == "act" or c % 2 == 0:
                    ACT(sq_tile[:np_, c, :], src_chunks[c], AF.Square, reads=s_src[c], writes=[s_sq[c]])
                else:
                    TT("pool", sq_tile[:np_, c, :], src_chunks[c], src_chunks[c], ALU.mult, reads=s_src[c], writes=[s_sq[c]])
            MM(psb[bank][:np_, :], [(ones_bf[:np_, :np_], sq_tile[:np_, c, :]) for c in range(n)],
               reads=s_sq[:n] + [s_ones], writes=[s_psb[bank]])
            ACT(rstd_tile[:np_, :], psb[bank][:np_, :], AF.Ln, reads=[s_psb[bank], s_ones], writes=[s_rstd],
                scale=1.0 / dim, bias=epsc[:np_, :])
            ACT(rstd_tile[:np_, :], rstd_tile[:np_, :], AF.Exp, reads=[s_rstd], writes=[s_rstd], scale=-0.5)

        with contextlib.ExitStack() as st:
            sb = lambda name, shape, dt: st.enter_context(nc.sbuf_tensor(_un(name), shape, dt))
            cT = sb("cT", [P, NCH], F32)
            cond = sb("cond", [P, NCH], F32)
            adab = sb("adab", [P, 2, 48], F32)
            n1g = sb("n1g", [P, 2, NCH], F32)
            n2g = sb("n2g", [P, 2, NCH], F32)
            modt = sb("modt", [P, 2, 48], F32)
            s_in = Slot()
            s_cond = Slot()
            s_modt = Slot()
            LOAD("sp", cT[:], cT_in, "a0", [s_in])
            R.dma("sp", adab[:], adab_in.rearrange("l p m -> p l m"), "a0", writes=[s_in], append_w=True)
            R.dma("sp", n1g[:], n1g_in.rearrange("l p m -> p l m"), "a0", writes=[s_in], append_w=True)
            R.dma("sp", n2g[:], n2g_in.rearrange("l p m -> p l m"), "a0", writes=[s_in], append_w=True)
            ACT(cond[:], cT[:], AF.Silu, reads=[s_in], writes=[s_cond])
            aw = [sb(f"aw{i}", [P, NCH, 512], F32) for i in range(2)]
            s_aw = [Slot() for _ in range(2)]
            for l in range(n_layers):
                for mg in range(12):
                    i = (l * 12 + mg) % 2
                    LOAD("sp", aw[i][:], ada_w[l, :, mg * 512:(mg + 1) * 512].rearrange("(kc p) n -> p kc n", p=P),
                         f"aw{i}", [s_aw[i]])
                    for j in range(4):
                        m = mg * 4 + j
                        MM(psb[l][:, m:m + 1], [(aw[i][:, kc, j * P:(j + 1) * P], cond[:, kc:kc + 1]) for kc in range(NCH)],
                           reads=[s_aw[i], s_cond], writes=[s_psb[l]])
                TT("dve", modt[:, l, :], psb[l][:, 0:48], adab[:, l, :], ALU.add, reads=[s_psb[l], s_in], writes=[s_modt])
                STT(AB[:, l, 0, :], modt[:, l, 8:16], 1.0, n1g[:, l, :], ALU.add, ALU.mult, reads=[s_modt, s_in], writes=[s_AB])
                CP("dve", AB[:, l, 1, :], modt[:, l, 0:8], reads=[s_modt], writes=[s_AB])
                CP("dve", AB[:, l, 2, :], modt[:, l, 16:24], reads=[s_modt], writes=[s_AB])
                STT(AB[:, l, 3, :], modt[:, l, 32:40], 1.0, n2g[:, l, :], ALU.add, ALU.mult, reads=[s_modt, s_in], writes=[s_AB])
                CP("dve", AB[:, l, 4, :], modt[:, l, 24:32], reads=[s_modt], writes=[s_AB])
                CP("dve", AB[:, l, 5, :], modt[:, l, 40:48], reads=[s_modt], writes=[s_AB])
            lamv = sb("lamv", [P, 4, 64], F32)
            lprod = sb("lprod", [P, 2, 64], F32)
            lsum = sb("lsum", [P, 2], F32)
            subg = sb("subg", [P, 1], F32)
            s_lv = Slot()
            s_lp = Slot()
            LOAD("sp", lamv[:], lamv_in, "a1", [s_lv])
            R.dma("sp", subg[:], subg_in, "a1", writes=[s_lv], append_w=True)
            TT("dve", lprod[:, 0, :], lamv[:, 0, :], lamv[:, 1, :], ALU.mult, reads=[s_lv], writes=[s_lp])
            TT("dve", lprod[:, 1, :], lamv[:, 2, :], lamv[:, 3, :], ALU.mult, reads=[s_lv], writes=[s_lp])
            R.op("dve", lambda e: e.reduce_sum(out=lsum[:], in_=lprod[:], axis=AX.X), reads=[s_lp], writes=[s_lp])
            ACT(lsum[:], lsum[:], AF.Exp, reads=[s_lp], writes=[s_lp])
            TT("dve", lam_neg[:], lsum[:, 1:2], lsum[:, 0:1], ALU.subtract, reads=[s_lp], writes=[s_lam])
            TS("dve", lam_neg[:], lam_neg[:], -LAM_INIT1, None, ALU.add, None, reads=[s_lam], writes=[s_lam])
            TS("dve", subg2[:], subg[:], 1.0 - LAM_INIT1, None, ALU.mult, None, reads=[s_lv], writes=[s_lam])

            posi = sb("posi", [P, S], I32)
            posf = sb("posf", [P, S], F32)
            invrows = sb("invrows", [P, 2], F32)
            ang = sb("ang", [P, S], F32)
            kk = sb("kk", [P, S], F32)
            rr = sb("rr", [P, S], F32)
            tS = sb("tS", [P, S], F32)
            tC = sb("tC", [P, S], F32)
            s_pos, s_ang, s_kk, s_rr, s_tS, s_tC = Slot(), Slot(), Slot(), Slot(), Slot(), Slot()
            LOAD("sp", posi[:], pos_in.broadcast_to([P, S]), "a2", [s_pos])
            R.dma("sp", invrows[:], invrows_in, "a2", writes=[s_pos], append_w=True)
            CP("dve", posf[:], posi[:], reads=[s_pos], writes=[s_pos])
            for j in range(2):
                TS("dve", ang[:], posf[:], invrows[:, j:j + 1], None, ALU.mult, None, reads=[s_pos], writes=[s_ang])
                TS("dve", kk[:], ang[:], INV2PI, MAGIC, ALU.mult, ALU.add, reads=[s_ang], writes=[s_kk])
                TS("dve", kk[:], kk[:], -MAGIC, None, ALU.add, None, reads=[s_kk], writes=[s_kk])
                STT(rr[:], kk[:], -C1, ang[:], ALU.mult, ALU.add, reads=[s_kk, s_ang], writes=[s_rr])
                STT(rr[:], kk[:], -C2, rr[:], ALU.mult, ALU.add, reads=[s_kk, s_rr], writes=[s_rr])
                ACT(tS[:], rr[:], AF.Sin, reads=[s_rr], writes=[s_tS], scale=0.5)
                ACT(rr[:], rr[:], AF.Abs, reads=[s_rr], writes=[s_rr])
                ACT(tC[:], rr[:], AF.Sin, reads=[s_rr, s_ones], writes=[s_tC], scale=-0.5, bias=halfpi[:])
                STT(tS[:], tS[:], 2.0, tC[:], ALU.mult, ALU.mult, reads=[s_tS, s_tC], writes=[s_tS])
                ACT(tC[:], rr[:], AF.Sin, reads=[s_rr, s_ones], writes=[s_tC], scale=-1.0, bias=halfpi[:])
                STORE("sp", tab_s[2 * j], tC[:], "a3", reads=[s_tC])
                STORE("sp", tab_s[2 * j + 1], tS[:], "a3", reads=[s_tS])
            R.barrier()
            R.replay()
        if stop == "A0":
            return nc

        def wgu_prep(l, stg, s_stg, flist):
            for f in flist:
                i = f % len(stg)
                for half in range(2):
                    src = wgu_in[l, :, half * DFF + f * P: half * DFF + (f + 1) * P].rearrange("(kc p) n -> p kc n", p=P)
                    R.dma("pool", stg[i][:, :, half * P:(half + 1) * P], src, f"wgst{i}",
                          writes=[s_stg[i]], append_w=(half == 1))
                R.dma("pool", wgu_s[l, f], stg[i][:].rearrange("p kc n -> p (kc n)"), f"wgsto{i}", reads=[s_stg[i]], writes=[Slot()])

        with contextlib.ExitStack() as st:
            sb = lambda name, shape, dt: st.enter_context(nc.sbuf_tensor(_un(name), shape, dt))
            stg = [sb(f"stg{i}", [P, NCH, 256], BF16) for i in range(3)]
            s_stg = [Slot() for _ in range(3)]
            xin = [sb(f"xin{i}", [P, 4, D], F32) for i in range(2)]
            s_xin = [Slot() for _ in range(2)]
            xT = [sb(f"xT{i}", [P, NCH, TB], F32) for i in range(2)]
            s_xT = [[Slot() for c in range(NCH)] for i in range(2)]
            PB = Banks(range(8))
            for tb in range(NTB):
                b = tb % 2
                LOAD("sp", xin[b][:], x_in[tb * TB:(tb + 1) * TB, :].rearrange("(t p) d -> p t d", p=P), f"xin{b}", [s_xin[b]])
                for c in range(NCH):
                    k = PB.next()
                    fns = [lambda e, k=k, t=t, c=c, b=b: e.transpose(
                        out=psb[k][:, t * P:(t + 1) * P], in_=xin[b][:, t, c * P:(c + 1) * P], identity=identf[:])
                        for t in range(4)]
                    R.group("pe", fns, reads=[s_xin[b], s_c], writes=[s_psb[k]])
                    CP("act" if c % 2 == 0 else "dve", xT[b][:, c, :], psb[k][:], reads=[s_psb[k]], writes=[s_xT[b][c]])
                STORE("sp", xT_s[:, :, tb * TB:(tb + 1) * TB], xT[b][:], f"xTst{b}", reads=s_xT[b])
                wgu_prep(0, stg, s_stg, range(tb * 3, min(NF, tb * 3 + 3)))
            R.barrier()
            R.replay()
        if stop == "X0":
            return nc

        def load_tab(tile, j, tb, sem, slot):
            LOAD("sp", tile[:], tab_s[j, :, tb * TB:(tb + 1) * TB], sem, [slot])

        def ffn_block(l, tb, xb, s_xb, cat, s_cat, wout, s_wout, wd, s_wd, sq, s_sq, rstd, s_rstd, tmp, s_tmp,
                      h2, s_h2, actb, s_act, sg, s_sg, wg, s_wg, PB, wgi):
            for c in range(NCH):
                k = PB.next()
                MM(psb[k][:], [(wout[:, kc, c * P:(c + 1) * P], cat[:, kc, :]) for kc in range(NCH)],
                   reads=s_cat + [s_wout], writes=[s_psb[k]])
                STT(xb[:, c, :], psb[k][:], AB[:, l, 2, c:c + 1], xb[:, c, :], ALU.mult, ALU.add,
                    reads=[s_psb[k], s_AB], writes=[s_xb[c]])
            k = PB.next()
            rms_rstd(sq, s_sq, [xb[:, c, :] for c in range(NCH)], [[s_xb[c]] for c in range(NCH)], NCH, D, k, rstd, s_rstd,
                     sq_eng="mix")
            for c in range(NCH):
                t = c % 2
                STT(tmp[t][:], xb[:, c, :], AB[:, l, 3, c:c + 1], rstd[:], ALU.mult, ALU.mult,
                    reads=[s_xb[c], s_rstd, s_AB], writes=[s_tmp[t]])
                ACT(h2[:, c, :], tmp[t][:], AF.Identity, reads=[s_tmp[t], s_AB], writes=[s_h2[c]], bias=AB[:, l, 4, c:c + 1], scale=1.0)
            for f in range(NF):
                i = wgi[0] % len(wg)
                wgi[0] += 1
                LOAD("sp", wg[i][:].rearrange("p kc n -> p (kc n)"), wgu_s[l, f], f"wg{i}", [s_wg[i]])
                kg = PB.next()
                ku = PB.next()
                MM(psb[kg][:], [(wg[i][:, kc, 0:P], h2[:, kc, :]) for kc in range(NCH)], reads=s_h2 + [s_wg[i]], writes=[s_psb[kg]])
                MM(psb[ku][:], [(wg[i][:, kc, P:2 * P], h2[:, kc, :]) for kc in range(NCH)], reads=s_h2 + [s_wg[i]], writes=[s_psb[ku]])
                t = f % 2
                ACT(sg[t][:], psb[kg][:], AF.Silu, reads=[s_psb[kg]], writes=[s_sg[t]])
                TT("dve", actb[:, f, :], sg[t][:], psb[ku][:], ALU.mult, reads=[s_sg[t], s_psb[ku]], writes=[s_act[f]])
            for c in range(NCH):
                k = PB.next()
                MM(psb[k][:], [(wd[:, f, c * P:(c + 1) * P], actb[:, f, :]) for f in range(NF)], reads=s_act + [s_wd], writes=[s_psb[k]])
                STT(xb[:, c, :], psb[k][:], AB[:, l, 5, c:c + 1], xb[:, c, :], ALU.mult, ALU.add,
                    reads=[s_psb[k], s_AB], writes=[s_xb[c]])

        def norm1_block(l, xb, s_xb, sq, s_sq, rstd, s_rstd, tmp, s_tmp, h, s_h, PB):
            k = PB.next()
            rms_rstd(sq, s_sq, [xb[:, c, :] for c in range(NCH)], [[s_xb]] * NCH, NCH, D, k, rstd, s_rstd, sq_eng="mix")
            for c in range(NCH):
                t = c % 2
                STT(tmp[t][:], xb[:, c, :], AB[:, l, 0, c:c + 1], rstd[:], ALU.mult, ALU.mult,
                    reads=[s_xb, s_rstd, s_AB], writes=[s_tmp[t]])
                ACT(h[:, c, :], tmp[t][:], AF.Identity, reads=[s_tmp[t], s_AB], writes=[s_h[c]], bias=AB[:, l, 1, c:c + 1], scale=1.0)

        def post_phase(l, woutX_in, final):
            with contextlib.ExitStack() as st:
                sb = lambda name, shape, dt: st.enter_context(nc.sbuf_tensor(_un(name), shape, dt))
                wd = sb("wd", [P, NF, D], BF16)
                wout = sb("wout", [P, NCH, D], BF16)
                s_wd, s_wout = Slot(), Slot()
                LOAD("pool", wout[:], woutX_in.rearrange("(kc p) n -> p kc n", p=P), "wout", [s_wout])
                LOAD("pool", wd[:], wd_in[l].rearrange("(f p) n -> p f n", p=P), "wd", [s_wd])
                wg = [sb(f"wg{i}", [P, NCH, 256], BF16) for i in range(4)]
                s_wg = [Slot() for _ in range(4)]
                xb = [sb(f"xb{i}", [P, NCH, TB], F32) for i in range(2)]
                s_xb = [[Slot() for c in range(NCH)] for i in range(2)]
                cat = [sb(f"cat{i}", [P, NCH, TB], BF16) for i in range(2)]
                s_cat = [Slot() for _ in range(2)]
                sq = sb("sq", [P, NCH, TB], BF16)
                s_sq = [Slot() for _ in range(NCH)]
                rstd = sb("rstd", [P, TB], F32)
                s_rstd = Slot()
                tmp = [sb(f"tmp{i}", [P, TB], F32) for i in range(2)]
                s_tmp = [Slot() for _ in range(2)]
                h2 = sb("h2", [P, NCH, TB], BF16)
                s_h2 = [Slot() for _ in range(NCH)]
                actb = sb("actb", [P, NF, TB], BF16)
                s_act = [Slot() for _ in range(NF)]
                sg = [sb(f"sg{i}", [P, TB], F32) for i in range(2)]
                s_sg = [Slot() for _ in range(2)]
                if final:
                    ot = sb("ot", [P, 4, D], F32)
                    s_ot = [Slot() for _ in range(8)]
                    on = sb("on", [P, NCH, TB], F32)
                    s_on = [Slot() for _ in range(NCH)]
                PB = Banks(range(8))
                wgi = [0]
                for tb in range(NTB):
                    b = tb % 2
                    R.dma("sp", xb[b][:], xT_s[:, :, tb * TB:(tb + 1) * TB], f"xb{b}", writes=s_xb[b])
                    LOAD("sp", cat[b][:], cat_s[:, :, tb * TB:(tb + 1) * TB], f"cat{b}", [s_cat[b]])
                    ffn_block(l, tb, xb[b], s_xb[b], cat[b], [s_cat[b]], wout, s_wout, wd, s_wd, sq, s_sq, rstd, s_rstd,
                              tmp, s_tmp, h2, s_h2, actb, s_act, sg, s_sg, wg, s_wg, PB, wgi)
                    if not final:
                        STORE("sp", xT_s[:, :, tb * TB:(tb + 1) * TB], xb[b][:], f"xbst{b}", reads=s_xb[b])
                    else:
                        k = PB.next()
                        rms_rstd(sq, s_sq, [xb[b][:, c, :] for c in range(NCH)], [[s_xb[b][c]] for c in range(NCH)], NCH, D, k,
                                 rstd, s_rstd, sq_eng="mix")
                        for c in range(NCH):
                            STT(on[:, c, :], xb[b][:, c, :], fng[:, c:c + 1], rstd[:], ALU.mult, ALU.mult,
                                reads=[s_xb[b][c], s_rstd, s_c], writes=[s_on[c]])
                        for t in range(4):
                            for half in range(2):
                                k = PB.next()
                                fns = [lambda e, k=k, t=t, j=j, half=half: e.transpose(
                                    out=psb[k][:, j * P:(j + 1) * P], in_=on[:, half * 4 + j, t * P:(t + 1) * P],
                                    identity=identf[:]) for j in range(4)]
                                R.group("pe", fns, reads=s_on[half * 4:half * 4 + 4] + [s_c], writes=[s_psb[k]])
                                CP("act", ot[:, t, half * TB:(half + 1) * TB], psb[k][:], reads=[s_psb[k]], writes=[s_ot[t * 2 + half]])
                        STORE("sp", out[tb * TB:(tb + 1) * TB, :].rearrange("(t p) d -> p t d", p=P), ot[:], "ost", reads=s_ot)
                R.barrier()
                R.replay()

        SC0 = 96.0 ** -0.5
        with contextlib.ExitStack() as st:
            sb = lambda name, shape, dt: st.enter_context(nc.sbuf_tensor(_un(name), shape, dt))
            win = sb("win", [P, NCH, 1184], BF16)
            wuq = sb("wuq", [P, 3, 800], BF16)
            wukv = sb("wukv", [P, 2, 1024], BF16)
            poolw = sb("poolw", [P, 4, P], BF16)
            qng = sb("qng", [P, 3], F32)
            kvng = sb("kvng", [P, 2], F32)
            poolb = sb("poolb", [P, 4], F32)
            pools = sb("pools", [P, 4], F32)
            cntinv = sb("cntinv", [P, 4, 16], F32)
            s_w = Slot()
            LOAD("pool", win[:], win_in.rearrange("(kc p) n -> p kc n", p=P), "w0", [s_w])
            s_wq = Slot()
            R.op("pool", lambda e: e.memset(wuq[:, :, 768:800], 0.0), writes=[s_wq])
            R.dma("pool", wuq[:, :, 0:768], wuq_in.rearrange("(kc p) n -> p kc n", p=P), "w0", writes=[s_w], append_w=True)
            R.dma("pool", wukv[:], wukv_in.rearrange("(kc p) n -> p kc n", p=P), "w0", writes=[s_w], append_w=True)
            R.dma("pool", poolw[:], poolw_in.rearrange("g c d -> c g d"), "w0", writes=[s_w], append_w=True)
            R.dma("sp", qng[:], qng_in, "w1", writes=[s_w], append_w=True)
            R.dma("sp", kvng[:], kvng_in, "w1", writes=[s_w], append_w=True)
            R.dma("sp", poolb[:], poolb_in, "w1", writes=[s_w], append_w=True)
            R.dma("sp", pools[:], pools_in, "w1", writes=[s_w], append_w=True)
            R.dma("sp", cntinv[:], cntinv_in, "w1", writes=[s_w], append_w=True)
            xb = [sb(f"xb{i}", [P, NCH, TB], F32) for i in range(2)]
            s_xb = [Slot() for _ in range(2)]
            tabC = [sb(f"tabC{i}", [P, TB], F32) for i in range(2)]
            tabS = [sb(f"tabS{i}", [P, TB], F32) for i in range(2)]
            s_tab = [Slot() for _ in range(2)]
            sq = sb("sq", [P, NCH, TB], BF16)
            s_sq = [Slot() for _ in range(NCH)]
            rstd = sb("rstd", [P, TB], F32)
            s_rstd = Slot()
            rstq = sb("rstq", [P, TB], F32)
            s_rstq = Slot()
            rstk = sb("rstk", [P, TB], F32)
            s_rstk = Slot()
            tmp = [sb(f"tmp{i}", [P, TB], F32) for i in range(2)]
            s_tmp = [Slot() for _ in range(2)]
            h = sb("h", [P, NCH, TB], BF16)
            s_h = [Slot() for _ in range(NCH)]
            cq = sb("cq", [P, 5, TB], F32)
            s_cq = [Slot() for _ in range(5)]
            cn = sb("cn", [P, 5, TB], BF16)
            s_cn = [Slot() for _ in range(5)]
            u = sb("u", [P, 4, 16 + TB], F32)
            s_u = [Slot() for _ in range(4)]
            lv = [sb(f"lv{i}", [P, 16 + TB], F32) for i in range(2)]
            s_lv = [Slot() for _ in range(2)]
            pooled = sb("pooled", [P, 4, TB], BF16)
            s_pl = [Slot() for _ in range(4)]
            ptmp = sb("ptmp", [P, 16], F32)
            s_ptmp = Slot()
            qb = [sb(f"qb{i}", [P, TB], BF16) for i in range(2)]
            s_qb = [Slot() for _ in range(2)]
            t1 = [sb(f"t1{i}", [P, TB], F32) for i in range(2)]
            s_t1 = [Slot() for _ in range(2)]
            t2 = [sb(f"t2{i}", [P, TB], F32) for i in range(2)]
            s_t2 = [Slot() for _ in range(2)]
            qT = [sb(f"qT{i}", [P, 8, TB], BF16) for i in range(2)]
            s_qT = [[Slot() for hh in range(8)] for _ in range(2)]
            kT = [sb(f"kT{i}", [P, 8, TB], BF16) for i in range(2)]
            s_kT = [[Slot() for hh in range(8)] for _ in range(2)]
            s_kTr = [[Slot() for hh in range(8)] for _ in range(2)]
            vt = [sb(f"vt{i}", [P, 4, 8, 65], BF16) for i in range(2)]
            s_vt = [[Slot() for tt in range(4)] for _ in range(2)]
            catp = [sb(f"catp{i}", [P, 4, TB], BF16) for i in range(2)]
            s_catp = [[Slot() for g in range(4)] for _ in range(2)]
            for i in range(2):
                R.op("pool", lambda e, i=i: e.memset(vt[i][:], 1.0), writes=s_vt[i])
            R.op("pool", lambda e: e.memset(u[:], 0.0), writes=s_u)
            PB = Banks(range(8))
            for tb in range(DBG.get('l0p1_ntb', NTB)):
                b = tb % 2
                LOAD("sp", xb[b][:], xT_s[:, :, tb * TB:(tb + 1) * TB], f"xb{b}", [s_xb[b]])
                R.dma("sp", tabC[b][:], tab_s[0, :, tb * TB:(tb + 1) * TB], f"tab{b}", writes=[s_tab[b]])
                R.dma("sp", tabS[b][:], tab_s[1, :, tb * TB:(tb + 1) * TB], f"tab{b}", writes=[s_tab[b]], append_w=True)
                norm1_block(0, xb[b], s_xb[b], sq, s_sq, rstd, s_rstd, tmp, s_tmp, h, s_h, PB)
                if DBG.get('sec', 99) < 1:
                    continue
                for j in range(5):
                    k = PB.next()
                    MM(psb[k][:], [(win[:, kc, j * P:(j + 1) * P], h[:, kc, :]) for kc in range(NCH)], reads=s_h + [s_w], writes=[s_psb[k]])
                    CP("act", cq[:, j, :], psb[k][:], reads=[s_psb[k]], writes=[s_cq[j]])
                k = PB.next()
                rms_rstd(sq, s_sq, [cq[:, j, :] for j in range(3)], [[s_cq[j]] for j in range(3)], 3, 384, k, rstq, s_rstq, sq_eng="mix")
                for j in range(3):
                    STT(cn[:, j, :], cq[:, j, :], qng[:, j:j + 1], rstq[:], ALU.mult, ALU.mult, reads=[s_cq[j], s_rstq, s_w], writes=[s_cn[j]])
                k = PB.next()
                rms_rstd(sq[:, 3:5, :], s_sq[3:5], [cq[:, 3 + j, :] for j in range(2)], [[s_cq[3 + j]] for j in range(2)], 2, 256, k, rstk, s_rstk, sq_eng="mix")
                for j in range(2):
                    STT(cn[:, 3 + j, :], cq[:, 3 + j, :], kvng[:, j:j + 1], rstk[:], ALU.mult, ALU.mult,
                        reads=[s_cq[3 + j], s_rstk, s_w], writes=[s_cn[3 + j]])
                if DBG.get('sec', 99) < 2:
                    continue
                k = PB.next()
                MM(psb[k][64:96, :], [(win[:, kc, 640:672], h[:, kc, :]) for kc in range(NCH)], reads=s_h + [s_w], writes=[s_psb[k]])
                i2 = 0
                CP("act", qb[i2][64:96, :], psb[k][64:96, :], reads=[s_psb[k]], writes=[s_qb[i2]])
                k2 = PB.next()
                MM(psb[k2][64:96, :], [(rmat_bf[64:96, 0, 64:96], qb[i2][64:96, :])], reads=[s_qb[i2], s_c], writes=[s_psb[k2]])
                TT("dve", t1[i2][64:96, :], psb[k][64:96, :], tabC[b][64:96, :], ALU.mult, reads=[s_psb[k], s_tab[b]], writes=[s_t1[i2]])
                TT("dve", t2[i2][64:96, :], psb[k2][64:96, :], tabS[b][64:96, :], ALU.mult, reads=[s_psb[k2], s_tab[b]], writes=[s_t2[i2]])
                TT("pool", t1[i2][64:96, :], t1[i2][64:96, :], t2[i2][64:96, :], ALU.add, reads=[s_t2[i2]], writes=[s_t1[i2]])
                for hh in range(8):
                    CP("pool" if hh % 2 == 0 else "act", kT[b][64:96, hh, :], t1[i2][64:96, :], reads=[s_t1[i2]], writes=[s_kTr[b][hh]])
                if DBG.get('sec', 99) < 3:
                    continue
                for g in range(4):
                    k = PB.next()
                    MM(psb[k][:], [(win[:, kc, 672 + g * P:672 + (g + 1) * P], h[:, kc, :]) for kc in range(NCH)], reads=s_h + [s_w], writes=[s_psb[k]])
                    CP("act", u[:, g, 16:16 + TB], psb[k][:], reads=[s_psb[k]], writes=[s_u[g]])
                if DBG.get('sec', 99) < 4:
                    continue
                SUB = 99
                bank_q = {}

                def st1(hh):
                    k = PB.next()
                    bank_q[hh] = k
                    i2 = hh % 2
                    MM(psb[k][:], [(wuq[:, kc, hh * 96:hh * 96 + P], cn[:, kc, :]) for kc in range(3)], reads=s_cn[0:3] + [s_w, s_wq], writes=[s_psb[k]])
                    CP("act", qb[i2][:], psb[k][:], reads=[s_psb[k]], writes=[s_qb[i2]])

                def st2(hh):
                    k = bank_q[hh]
                    i2 = hh % 2
                    k2 = PB.next()
                    MM(psb[k2][:], [(rmat_bf[:, 0, :], qb[i2][:])], reads=[s_qb[i2], s_c], writes=[s_psb[k2]])
                    TT("dve", t1[i2][:], psb[k][:], tabC[b][:], ALU.mult, reads=[s_psb[k], s_tab[b]], writes=[s_t1[i2]])
                    TT("dve", t2[i2][:], psb[k2][:], tabS[b][:], ALU.mult, reads=[s_psb[k2], s_tab[b]], writes=[s_t2[i2]])
                    TT("pool", qT[b][:, hh, :], t1[i2][:], t2[i2][:], ALU.add, reads=[s_t1[i2], s_t2[i2]], writes=[s_qT[b][hh]])

                st1(0)
                for hh in range(8):
                    if hh + 1 < 8:
                        st1(hh + 1)
                    st2(hh)
                if SUB >= 4:
                    STORE("sp", q_s[0:96, :, tb * TB:(tb + 1) * TB], qT[b][0:96, :, :], f"qst{b}", reads=s_qT[b])
                if DBG.get('sec', 99) < 5:
                    continue
                for hh in range(8):
                    k = PB.next()
                    MM(psb[k][0:64, :], [(wukv[:, kc, hh * 128:hh * 128 + 64], cn[:, 3 + kc, :]) for kc in range(2)], reads=s_cn[3:5] + [s_w], writes=[s_psb[k]])
                    CP("act" if hh % 2 == 0 else "dve", kT[b][0:64, hh, :], psb[k][0:64, :], reads=[s_psb[k]], writes=[s_kT[b][hh]])
                STORE("sp", k_s[0:96, :, tb * TB:(tb + 1) * TB], kT[b][0:96, :, :], f"kst{b}", reads=s_kT[b] + s_kTr[b])
                if DBG.get('sec', 99) < 6:
                    continue
                for tt in range(4):
                    k = PB.next()
                    MM(psb[k][:].rearrange("p (h e) -> p h e", e=64), [(cn[:, 3 + kc, tt * P:(tt + 1) * P],
                                    wukv[:, kc, :].rearrange("p (h e) -> p h e", e=128)[:, :, 64:128]) for kc in range(2)],
                       reads=s_cn[3:5] + [s_w], writes=[s_psb[k]])
                    CP("dve" if tt % 2 == 0 else "act", vt[b][:, tt, :, 0:64], psb[k][:].rearrange("p (h e) -> p h e", e=64),
                       reads=[s_psb[k]], writes=[s_vt[b][tt]])
                STORE("sp", v_s[:, tb * 4 * 520:(tb + 1) * 4 * 520], vt[b][:].rearrange("p t h e -> p (t h e)"), f"vst{b}", reads=s_vt[b])
                if DBG.get('sec', 99) < 7:
                    continue
                for g in range(4):
                    w = 2 << g
                    src = u[:, g, :]
                    s_src = s_u[g]
                    sh = 1
                    lvl = 0
                    while sh < w:
                        dst = lv[lvl % 2]
                        TT("pool", dst[:, sh:16 + TB], src[:, sh:16 + TB], src[:, 0:16 + TB - sh], ALU.add,
                           reads=[s_src], writes=[s_lv[lvl % 2]])
                        src = dst
                        s_src = s_lv[lvl % 2]
                        sh *= 2
                        lvl += 1
                    STT(pooled[:, g, :], src[:, 16:16 + TB], 1.0 / w, u[:, g, 16:16 + TB], ALU.mult, ALU.subtract,
                        reads=[s_src, s_u[g]], writes=[s_pl[g]])
                    if tb == 0:
                        TT("dve", ptmp[:], src[:, 16:32], cntinv[:, g, :], ALU.mult, reads=[s_src, s_w], writes=[s_ptmp])
                        TT("dve", pooled[:, g, 0:16], ptmp[:], u[:, g, 16:32], ALU.subtract, reads=[s_ptmp, s_u[g], s_pl[g]], writes=[s_pl[g]])
                    k = PB.next()
                    MM(psb[k][:], [(poolw[:, g, :], pooled[:, g, :])], reads=[s_pl[g], s_w], writes=[s_psb[k]])
                    TS("dve", catp[b][:, g, :], psb[k][:], poolb[:, g:g + 1], pools[:, g:g + 1], ALU.add, ALU.mult,
                       reads=[s_psb[k], s_w], writes=[s_catp[b][g]])
                    CP("pool", u[:, g, 0:16], u[:, g, TB:TB + 16], reads=[], writes=[s_u[g]])
                STORE("sp", cat_s[:, 4:8, tb * TB:(tb + 1) * TB], catp[b][:], f"cpst{b}", reads=s_catp[b])
            R.barrier()
            R.replay()
        if stop == "L0P1":
            return nc

        with contextlib.ExitStack() as st:
            sb = lambda name, shape, dt: st.enter_context(nc.sbuf_tensor(_un(name), shape, dt))
            stg = [sb(f"stg{i}", [P, NCH, 256], BF16) for i in range(3)]
            s_stg = [Slot() for _ in range(3)]
            va = sb("va", [P, 32, 8, 65], BF16)
            s_va = Slot()
            LOAD("sp", va[:].rearrange("p t h e -> p (t h e)"), v_s[:, 0:32 * 520], "va", [s_va])
            vg = sb("vg", [P, 32, 8, P], BF16)
            s_vgh = [Slot() for _ in range(8)]
            R.op("pool", lambda e: e.memset(vg[:], 1.0), writes=s_vgh)
            for hh in range(8):
                off = 0 if hh % 2 == 0 else 64
                CP("pool" if hh % 2 == 0 else "dve", vg[:, :, hh, off:off + 64], va[:, :, hh, 0:64], reads=[s_va], writes=[s_vgh[hh]])
            qh = [sb(f"qh{i}", [P, S], BF16) for i in range(2)]
            kh = [sb(f"kh{i}", [P, S], BF16) for i in range(2)]
            s_qk = [Slot() for _ in range(2)]
            rinv = [sb(f"rinv{i}", [P, TB], F32) for i in range(2)]
            s_rinv = [Slot() for _ in range(2)]
            attn = sb("attn", [P, 4, S], BF16)
            s_attn = [Slot() for _ in range(4)]
            NPT = 6
            pt = [sb(f"ptx{i}", [P, TB], BF16) for i in range(NPT)]
            s_pt = [Slot() for _ in range(NPT)]
            SBK = [0, 1, 2, 3]
            OBK = [4, 5]
            LA = 2
            items = []
            for hh in range(8):
                for qblk in range(NTB):
                    nkt = 4 * qblk + 4
                    for kt in range(nkt):
                        items.append((hh, qblk, kt, nkt))
            prep_done = [0]

            def load_head(hh):
                i = hh % 2
                LOAD("sp", qh[i][0:96, :], q_s[0:96, hh, :], f"qh{i}", [s_qk[i]])
                R.dma("sp", kh[i][0:96, :], k_s[0:96, hh, :], f"qh{i}", writes=[s_qk[i]], append_w=True)

            def emitA(n):
                hh, qblk, kt, nkt = items[n]
                i = hh % 2
                if qblk == 0 and kt == 0:
                    if hh == 0:
                        load_head(0)
                    if hh + 1 < 8:
                        load_head(hh + 1)
                j = kt - 4 * qblk
                c0 = max(j, 0) * P
                sbk = SBK[n % len(SBK)]
                fns = [lambda e: e.matmul(psb[sbk][:, c0:TB], lhsT=kh[i][0:96, kt * P:(kt + 1) * P],
                                          rhs=qh[i][0:96, qblk * TB + c0:(qblk + 1) * TB], start=True, stop=(j < 0))]
                if j >= 0:
                    fns.append(lambda e: e.matmul(psb[sbk][:, c0:c0 + P], lhsT=ident_bf[:], rhs=maskb_bf[:], start=False, stop=True))
                R.group("pe", fns, reads=[s_qk[i], s_c], writes=[s_psb[sbk]])
                p_i = n % NPT
                ACT(pt[p_i][:, c0:TB], psb[sbk][:, c0:TB], AF.Exp, reads=[s_psb[sbk]], writes=[s_pt[p_i]], scale=SC0)

            def emitC(n):
                hh, qblk, kt, nkt = items[n]
                blk = hh * NTB + qblk
                ob = OBK[blk % 2]
                j = kt - 4 * qblk
                c0 = max(j, 0) * P
                p_i = n % NPT
                R.group("pe", [lambda e: e.matmul(psb[ob][:, c0:TB], lhsT=vg[:, kt, hh, :], rhs=pt[p_i][:, c0:TB],
                                                  start=(kt == 0), stop=(kt == nkt - 1))],
                        reads=[s_pt[p_i], s_vgh[hh]], writes=[s_psb[ob]])
                if kt == nkt - 1:
                    ri = blk % 2
                    odd = hh % 2
                    if odd == 0:
                        RECIP(rinv[ri][0:64, :], psb[ob][64:128, :], reads=[s_psb[ob]], writes=[s_rinv[ri]])
                        TT("dve", attn[0:64, hh // 2, qblk * TB:(qblk + 1) * TB], psb[ob][0:64, :], rinv[ri][0:64, :], ALU.mult,
                           reads=[s_psb[ob], s_rinv[ri]], writes=[s_attn[hh // 2]])
                    else:
                        RECIP(rinv[ri][64:128, :], psb[ob][0:64, :], reads=[s_psb[ob]], writes=[s_rinv[ri]])
                        TT("dve", attn[64:128, hh // 2, qblk * TB:(qblk + 1) * TB], psb[ob][64:128, :], rinv[ri][64:128, :], ALU.mult,
                           reads=[s_psb[ob], s_rinv[ri]], writes=[s_attn[hh // 2]])
                    if n_layers > 1 and prep_done[0] < NF and blk % 2 == 0:
                        wgu_prep(1, stg, s_stg, [prep_done[0]])
                        prep_done[0] += 1
                    if odd and qblk == NTB - 1:
                        STORE("sp", cat_s[:, hh // 2, :], attn[:, hh // 2, :], "ast", reads=[s_attn[hh // 2]])

            for n in range(len(items) + LA):
                if n < len(items):
                    emitA(n)
                if n - LA >= 0:
                    emitC(n - LA)
            R.barrier()
            R.replay()
        if stop == "L0P2":
            return nc

        post_phase(0, woutA_in, final=(n_layers == 1))
        if stop == "L0P3":
            return nc

        if n_layers > 1:
            SC1 = 64.0 ** -0.5
            with contextlib.ExitStack() as st:
                sb = lambda name, shape, dt: st.enter_context(nc.sbuf_tensor(_un(name), shape, dt))
                wqkv = sb("wqkv", [P, NCH, 3 * D], BF16)
                s_w = Slot()
                for j in range(3):
                    R.dma("pool", wqkv[:, :, j * D:(j + 1) * D], wqkv_in[:, j * D:(j + 1) * D].rearrange("(kc p) n -> p kc n", p=P),
                          "w0", writes=[s_w], append_w=(j > 0))
                xb = [sb(f"xb{i}", [P, NCH, TB], F32) for i in range(2)]
                s_xb = [Slot() for _ in range(2)]
                tabC = [sb(f"tabC{i}", [P, TB], F32) for i in range(2)]
                tabS = [sb(f"tabS{i}", [P, TB], F32) for i in range(2)]
                s_tab = [Slot() for _ in range(2)]
                sq = sb("sq", [P, NCH, TB], BF16)
                s_sq = [Slot() for _ in range(NCH)]
                rstd = sb("rstd", [P, TB], F32)
                s_rstd = Slot()
                tmp = [sb(f"tmp{i}", [P, TB], F32) for i in range(2)]
                s_tmp = [Slot() for _ in range(2)]
                h = sb("h", [P, NCH, TB], BF16)
                s_h = [Slot() for _ in range(NCH)]
                qb = [sb(f"qb{i}", [P, TB], BF16) for i in range(2)]
                s_qb = [Slot() for _ in range(2)]
                t1 = [sb(f"t1{i}", [P, TB], F32) for i in range(2)]
                s_t1 = [Slot() for _ in range(2)]
                t2 = [sb(f"t2{i}", [P, TB], F32) for i in range(2)]
                s_t2 = [Slot() for _ in range(2)]
                qk = [sb(f"qk{i}", [P, 16, TB], BF16) for i in range(2)]
                s_qkT = [[Slot() for c in range(16)] for _ in range(2)]
                vt = [sb(f"vt{i}", [P, 4, D], BF16) for i in range(2)]
                s_vt = [[Slot() for c in range(8)] for _ in range(2)]
                PB = Banks(range(8))
                for tb in range(NTB):
                    b = tb % 2
                    LOAD("sp", xb[b][:], xT_s[:, :, tb * TB:(tb + 1) * TB], f"xb{b}", [s_xb[b]])
                    R.dma("sp", tabC[b][:], tab_s[2, :, tb * TB:(tb + 1) * TB], f"tab{b}", writes=[s_tab[b]])
                    R.dma("sp", tabS[b][:], tab_s[3, :, tb * TB:(tb + 1) * TB], f"tab{b}", writes=[s_tab[b]], append_w=True)
                    norm1_block(1, xb[b], s_xb[b], sq, s_sq, rstd, s_rstd, tmp, s_tmp, h, s_h, PB)
                    bank_q = {}

                    def st1(c):
                        k = PB.next()
                        bank_q[c] = k
                        i2 = c % 2
                        MM(psb[k][:], [(wqkv[:, kc, c * P:(c + 1) * P], h[:, kc, :]) for kc in range(NCH)], reads=s_h + [s_w], writes=[s_psb[k]])
                        CP("act", qb[i2][:], psb[k][:], reads=[s_psb[k]], writes=[s_qb[i2]])

                    def st2(c):
                        k = bank_q[c]
                        i2 = c % 2
                        k2 = PB.next()
                        MM(psb[k2][:], [(rmat_bf[:, 1, :], qb[i2][:])], reads=[s_qb[i2], s_c], writes=[s_psb[k2]])
                        TT("dve", t1[i2][:], psb[k][:], tabC[b][:], ALU.mult, reads=[s_psb[k], s_tab[b]], writes=[s_t1[i2]])
                        TT("dve", t2[i2][:], psb[k2][:], tabS[b][:], ALU.mult, reads=[s_psb[k2], s_tab[b]], writes=[s_t2[i2]])
                        TT("pool", qk[b][:, c, :], t1[i2][:], t2[i2][:], ALU.add, reads=[s_t1[i2], s_t2[i2]], writes=[s_qkT[b][c]])

                    st1(0)
                    for c in range(16):
                        if c + 1 < 16:
                            st1(c + 1)
                        st2(c)
                    STORE("sp", q_s[:, :, tb * TB:(tb + 1) * TB], qk[b][:, 0:8, :], f"qst{b}", reads=s_qkT[b][0:8])
                    STORE("sp", k_s[:, :, tb * TB:(tb + 1) * TB], qk[b][:, 8:16, :], f"kst{b}", reads=s_qkT[b][8:16])
                    for tt in range(4):
                        for half in range(2):
                            k = PB.next()
                            MM(psb[k][:], [(h[:, kc, tt * P:(tt + 1) * P], wqkv[:, kc, 2 * D + half * TB:2 * D + (half + 1) * TB])
                                            for kc in range(NCH)], reads=s_h + [s_w], writes=[s_psb[k]])
                            CP("act" if half == 0 else "dve", vt[b][:, tt, half * TB:(half + 1) * TB], psb[k][:], reads=[s_psb[k]],
                               writes=[s_vt[b][tt * 2 + half]])
                    STORE("sp", v_s[:, tb * 4 * D:(tb + 1) * 4 * D], vt[b][:].rearrange("p t d -> p (t d)"), f"vst{b}", reads=s_vt[b])
                R.barrier()
                R.replay()
            if stop == "L1P1":
                return nc

            with contextlib.ExitStack() as st:
                sb = lambda name, shape, dt: st.enter_context(nc.sbuf_tensor(_un(name), shape, dt))
                va = sb("va", [P, 32, D], BF16)
                s_va = Slot()
                LOAD("sp", va[:].rearrange("p t d -> p (t d)"), v_s[:, :], "va", [s_va])
                qh = [sb(f"qh{i}", [P, S], BF16) for i in range(2)]
                kh = [sb(f"kh{i}", [P, S], BF16) for i in range(2)]
                s_qk = [Slot() for _ in range(2)]
                rinv = [sb(f"rinv{i}", [P, TB], F32) for i in range(2)]
                s_rinv = [Slot() for _ in range(2)]
                a0 = sb("a0", [P, TB], F32)
                a1 = sb("a1", [P, TB], F32)
                s_a0, s_a1 = Slot(), Slot()
                sqa = sb("sqa", [P, 1, TB], BF16)
                s_sqa = [Slot()]
                rsa = sb("rsa", [P, TB], F32)
                s_rsa = Slot()
                attn = [sb(f"attn{i}", [P, S], BF16) for i in range(2)]
                s_attn = [Slot() for _ in range(2)]
                NPT = 10
                ptp = [sb(f"pty{i}", [P, 2, TB], BF16) for i in range(NPT)]
                s_pt = [Slot() for _ in range(NPT)]
                tri2 = sb("tri2", [P, 2, P], BF16)
                s_tri2 = Slot()
                CP("pool", tri2[:, 0, :], tri_bf[:], reads=[s_c], writes=[s_tri2])
                CP("pool", tri2[:, 1, :], tri_bf[:], reads=[s_c, s_tri2], writes=[s_tri2])
                ones_f = sb("ones_f", [P, P], F32)
                s_onesf = Slot()
                R.op("pool", lambda e: e.memset(ones_f[:], 1.0), writes=[s_onesf])
                EAt = [sb(f"EA{b}", [P, 3, TB], F32) for b in range(2)]
                EA = [[EAt[b][:, k, :] for k in range(3)] for b in range(2)]
                s_EA = [[Slot() for k in range(3)] for b in range(2)]
                rv = [sb(f"rv{i}", [P, TB], F32) for i in range(2)]
                s_rv = [Slot() for _ in range(2)]
                PAIRS = [0, 2]
                LA = 2
                items = []
                for hh in range(8):
                    for qblk in range(NTB):
                        nkt = 4 * qblk + 4
                        for kt in range(nkt):
                            items.append((hh, qblk, kt, nkt))
                        items.append((hh, qblk, -1, nkt))
                        items.append((hh, qblk, -2, nkt))

                def load_head(hh):
                    i = hh % 2
                    LOAD("sp", qh[i][:], q_s[:, hh, :], f"qh{i}", [s_qk[i]])
                    R.dma("sp", kh[i][:], k_s[:, hh, :], f"qh{i}", writes=[s_qk[i]], append_w=True)

                def emitA(n):
                    hh, qblk, kt, nkt = items[n]
                    if kt < 0:
                        return
                    i = hh % 2
                    blk = hh * NTB + qblk
                    bb = blk % 2
                    if qblk == 0 and kt == 0:
                        if hh == 0:
                            load_head(0)
                        if hh + 1 < 8:
                            load_head(hh + 1)
                    j = kt - 4 * qblk
                    c0 = max(j, 0) * P
                    p0 = PAIRS[n % 2]
                    fns = [lambda e: e.matmul(psb[p0][:, c0:TB], lhsT=kh[i][0:64, kt * P:(kt + 1) * P],
                                              rhs=qh[i][0:64, qblk * TB + c0:(qblk + 1) * TB], start=True, stop=(j < 0)),
                           lambda e: e.matmul(psb[p0 + 1][:, c0:TB], lhsT=kh[i][64:128, kt * P:(kt + 1) * P],
                                              rhs=qh[i][64:128, qblk * TB + c0:(qblk + 1) * TB], start=True, stop=(j < 0))]
                    if j >= 0:
                        fns.append(lambda e: e.matmul(psb[p0][:, c0:c0 + P], lhsT=ident_bf[:], rhs=maskb_bf[:], start=False, stop=True))
                        fns.append(lambda e: e.matmul(psb[p0 + 1][:, c0:c0 + P], lhsT=ident_bf[:], rhs=maskb_bf[:], start=False, stop=True))
                    R.group("pe", fns, reads=[s_qk[i], s_c], writes=[s_psb[p0], s_psb[p0 + 1]])
                    p_i = n % NPT
                    ACT(ptp[p_i][:, :, c0:TB], psall[:, p0:p0 + 2, c0:TB], AF.Exp, reads=[s_psb[p0], s_psb[p0 + 1]],
                        writes=[s_pt[p_i]], scale=SC1)
                    if kt == 0:
                        CP("dve", EAt[bb][:, 0:2, c0:TB], ptp[p_i][:, :, c0:TB], reads=[s_pt[p_i]], writes=[s_EA[bb][0], s_EA[bb][1]])
                    elif kt % 2 == 0:
                        TT("dve", EAt[bb][:, 0:2, c0:TB], EAt[bb][:, 0:2, c0:TB], ptp[p_i][:, :, c0:TB], ALU.add,
                           reads=[s_pt[p_i]], writes=[s_EA[bb][0], s_EA[bb][1]])
                    else:
                        TT("dve", EA[bb][0][:, c0:TB], EA[bb][0][:, c0:TB], ptp[p_i][:, 0, c0:TB], ALU.add,
                           reads=[s_pt[p_i]], writes=[s_EA[bb][0]])
                        if kt == 1:
                            if c0 > 0:
                                R.op("pool", lambda e: e.memset(EA[bb][2][:, 0:c0], 0.0), writes=[s_EA[bb][2]])
                            CP("pool", EA[bb][2][:, c0:TB], ptp[p_i][:, 1, c0:TB], reads=[s_pt[p_i]], writes=[s_EA[bb][2]])
                        else:
                            TT("pool", EA[bb][2][:, c0:TB], EA[bb][2][:, c0:TB], ptp[p_i][:, 1, c0:TB], ALU.add,
                               reads=[s_pt[p_i]], writes=[s_EA[bb][2]])

                def emitC(n):
                    hh, qblk, kt, nkt = items[n]
                    i = hh % 2
                    blk = hh * NTB + qblk
                    bb = blk % 2
                    o0, o1 = (4, 5) if bb == 0 else (6, 7)
                    p0 = PAIRS[n % 2]
                    if kt >= 0:
                        j = kt - 4 * qblk
                        c0 = max(j, 0) * P
                        p_i = n % NPT
                        R.group("pe", [lambda e: e.matmul(psb[o0][:, c0:TB], lhsT=va[:, kt, hh * P:(hh + 1) * P], rhs=ptp[p_i][:, 0, c0:TB],
                                                          start=(kt == 0), stop=(kt == nkt - 1)),
                                       lambda e: e.matmul(psb[o1][:, c0:TB], lhsT=va[:, kt, hh * P:(hh + 1) * P], rhs=ptp[p_i][:, 1, c0:TB],
                                                          start=(kt == 0), stop=(kt == nkt - 1))],
                                reads=[s_pt[p_i], s_va], writes=[s_psb[o0], s_psb[o1]])
                    elif kt == -1:
                        R.group("pe", [lambda e: e.matmul(psb[p0][:], lhsT=ones_f[:], rhs=EA[bb][0][:], start=True, stop=True),
                                       lambda e: e.matmul(psb[p0 + 1][:], lhsT=ones_f[:], rhs=EA[bb][1][:], start=True, stop=False),
                                       lambda e: e.matmul(psb[p0 + 1][:], lhsT=ones_f[:], rhs=EA[bb][2][:], start=False, stop=True)],
                                reads=[s_EA[bb][0], s_EA[bb][1], s_EA[bb][2], s_onesf], writes=[s_psb[p0], s_psb[p0 + 1]])
                        for c in range(2):
                            ACT(rv[c][:], psb[p0 + c][:], AF.Ln, reads=[s_psb[p0 + c]], writes=[s_rv[c]])
                            ACT(rv[c][:], rv[c][:], AF.Exp, reads=[s_rv[c]], writes=[s_rv[c]], scale=-1.0)
                        TT("dve", a0[:], psb[o0][:], rv[0][:], ALU.mult, reads=[s_psb[o0], s_rv[0]], writes=[s_a0])
                        TT("dve", a1[:], psb[o1][:], rv[1][:], ALU.mult, reads=[s_psb[o1], s_rv[1]], writes=[s_a1])
                        STT(a0[:], a1[:], lam_neg[:, 0:1], a0[:], ALU.mult, ALU.add, reads=[s_a1, s_a0, s_lam], writes=[s_a0])
                        TT("dve", sqa[:, 0, :], a0[:], a0[:], ALU.mult, reads=[s_a0], writes=[s_sqa[0]])
                    else:
                        MM(psb[p0][:], [(ones_bf[:], sqa[:, 0, :])], reads=[s_sqa[0], s_ones], writes=[s_psb[p0]])
                        ACT(rsa[:], psb[p0][:], AF.Ln, reads=[s_psb[p0], s_ones], writes=[s_rsa], scale=1.0 / 128, bias=epsc[:])
                        ACT(rsa[:], rsa[:], AF.Exp, reads=[s_rsa], writes=[s_rsa], scale=-0.5)
                        STT(attn[i][:, qblk * TB:(qblk + 1) * TB], a0[:], subg2[:, 0:1], rsa[:], ALU.mult, ALU.mult,
                            reads=[s_a0, s_rsa, s_lam], writes=[s_attn[i]])
                        if qblk == NTB - 1:
                            STORE("sp", cat_s[:, hh, :], attn[i][:], f"ast{i}", reads=[s_attn[i]])

                for n in range(len(items) + LA):
                    if n < len(items):
                        emitA(n)
                    if n - LA >= 0:
                        emitC(n - LA)
                R.barrier()
                R.replay()
            if stop == "L1P2":
                return nc

            post_phase(1, woutD_in, final=True)

        gst.callback(lambda: None)
    return nc


def _consts():
    ident = np.eye(P, dtype=np.float32)
    tri = (np.arange(P)[None, :] >= np.arange(P)[:, None]).astype(np.float32)
    rm = np.zeros((2, P, P), np.float32)
    for i in range(16):
        rm[0, 80 + i, 64 + i] = -1.0
        rm[0, 64 + i, 80 + i] = 1.0
    for c in range(2):
        for i in range(8):
            rm[1, c * 64 + 8 + i, c * 64 + i] = -1.0
            rm[1, c * 64 + i, c * 64 + 8 + i] = 1.0
    theta = np.float32(500000.0)
    inv32 = (theta ** (-(np.arange(0, 32, 2, dtype=np.float32) / np.float32(32)))).astype(np.float32)
    inv16 = (theta ** (-(np.arange(0, 16, 2, dtype=np.float32) / np.float32(16)))).astype(np.float32)
    invrows = np.zeros((P, 2), np.float32)
    for i in range(16):
        invrows[64 + i, 0] = inv32[i]
        invrows[80 + i, 0] = inv32[i]
    for c in range(2):
        for i in range(8):
            invrows[c * 64 + i, 1] = inv16[i]
            invrows[c * 64 + 8 + i, 1] = inv16[i]
    cnt = np.zeros((P, 4, 16), np.float32)
    for g, w in enumerate((2, 4, 8, 16)):
        cnt[:, g, :] = (1.0 / np.minimum(np.arange(1, 17), w)).astype(np.float32)[None, :]
    maskb = np.where(np.arange(P)[None, :] >= np.arange(P)[:, None], 0.0, -30000.0).astype(np.float32)
    return ident, tri, rm, invrows, cnt, maskb


def _cols(v, n):
    return np.ascontiguousarray(np.asarray(v, np.float32).reshape(n, P).T)


_NC_CACHE = {}


def kernel(**inputs):
    g = lambda k: np.asarray(inputs[k])
    x = g("x").astype(np.float32)
    ident, tri, rm, invrows, cnt, maskb = _consts()
    if "nc" not in _NC_CACHE:
        _NC_CACHE["nc"] = build_program()
    nc = _NC_CACHE["nc"]
    shared = {
        "ada_w": np.ascontiguousarray(g("ada_w"), dtype=np.float32),
        "adab": np.ascontiguousarray(g("ada_b").astype(np.float32).reshape(2, 48, P).transpose(0, 2, 1)),
        "n1g": np.ascontiguousarray(g("norm1_g").astype(np.float32).reshape(2, NCH, P).transpose(0, 2, 1)),
        "n2g": np.ascontiguousarray(g("norm2_g").astype(np.float32).reshape(2, NCH, P).transpose(0, 2, 1)),
        "fng": _cols(g("final_norm_g"), NCH),
        "w_gu": np.ascontiguousarray(g("ffn_w_gate_up"), dtype=np.float32),
        "w_d": np.ascontiguousarray(g("ffn_w_down"), dtype=np.float32),
        "w_in": np.ascontiguousarray(g("mla_w_in")[0], dtype=np.float32),
        "qng": _cols(g("mla_q_norm_g")[0], 3),
        "kvng": _cols(g("mla_kv_norm_g")[0], 2),
        "w_uq": np.ascontiguousarray(g("mla_w_uq")[0], dtype=np.float32),
        "w_ukv": np.ascontiguousarray(g("mla_w_ukv")[0], dtype=np.float32),
        "pool_w": np.ascontiguousarray(g("pool_w")[0], dtype=np.float32),
        "pool_b": _cols(g("pool_b")[0].reshape(-1), 4),
        "pool_s": _cols(g("pool_scale")[0], 4),
        "w_outA": np.ascontiguousarray(g("mix_a_w_out")[0], dtype=np.float32),
        "w_qkv": np.ascontiguousarray(g("diff_w_qkv")[0], dtype=np.float32),
        "lamv": np.ascontiguousarray(np.broadcast_to(np.stack([
            g("diff_lambda_q1")[0], g("diff_lambda_k1")[0], g("diff_lambda_q2")[0], g("diff_lambda_k2")[0]]).astype(np.float32)[None],
            (P, 4, 64))),
        "subg": np.ascontiguousarray(g("diff_subln_g")[0].astype(np.float32).reshape(P, 1)),
        "w_outD": np.ascontiguousarray(g("diff_w_out")[0], dtype=np.float32),
        "ident_f": ident, "tri": tri, "maskb": maskb, "rmat": rm, "invrows": invrows, "cntinv": cnt,
    }
    c = g("c").astype(np.float32)
    pos = g("positions").astype(np.int32)
    in_maps = []
    ncores = int(_NC_CACHE.get("ncores", 8))
    for b in range(ncores):
        m = dict(shared)
        m["x"] = np.ascontiguousarray(x[b])
        m["cT"] = _cols(c[b], NCH)
        m["pos"] = np.ascontiguousarray(pos[b][None, :])
        in_maps.append(m)
    res = run_bass_kernel_spmd(nc, in_maps, core_ids=list(range(ncores)))
    return np.stack([np.asarray(r["out"], dtype=np.float32) for r in res.results], axis=0)
```

```python
import contextlib
import math
import numpy as np
import ml_dtypes
import concourse.bass as bass
import concourse.mybir as mybir
from concourse.bass_utils import run_bass_kernel_spmd

F32 = mybir.dt.float32
BF16 = mybir.dt.bfloat16
I32 = mybir.dt.int32
AF = mybir.ActivationFunctionType
ALU = mybir.AluOpType
AX = mybir.AxisListType

P = 128
S = 4096
D = 1024
NCH = D // P
TB = 512
NTB = S // TB
EPS = 1e-6


class Sem:
    def __init__(self, nc, stack, name):
        self.h = stack.enter_context(nc.semaphore(name))
        self.val = 0
        self.name = name


class Slot:
    __slots__ = ("w", "r", "name", "excl")

    def __init__(self, name="", excl=False):
        self.w = []
        self.r = []
        self.name = name
        self.excl = excl


class Queue:
    def __init__(self, name, sem):
        self.name = name
        self.sem = sem
        self.ops = []
        self.seen = {}


class Rec:
    def __init__(self, nc, stack):
        self.nc = nc
        self.stack = stack
        self.q = {}
        for n in ("pe", "act", "dve", "pool", "sp"):
            self.q[n] = Queue(n, Sem(nc, stack, "q_" + n))
        self.dma_sems = {}
        self.all_dma_toks = []

    def dsem(self, name):
        if name not in self.dma_sems:
            self.dma_sems[name] = Sem(self.nc, self.stack, "d_" + name)
        return self.dma_sems[name]

    def _deps(self, reads, writes):
        deps = []
        for s in reads:
            deps += s.w
            if s.excl:
                deps += s.r
        for s in writes:
            deps += s.w
            deps += s.r
        return deps

    def _prune(self, q, deps):
        best = {}
        for (sem, v) in deps:
            if q.name == "pe" and sem is q.sem:
                continue
            if v > q.seen.get(sem, 0) and v > best.get(sem, (None, 0))[1]:
                best[sem] = (sem, v)
        out = list(best.values())
        for (sem, v) in out:
            q.seen[sem] = v
        return out

    def op(self, qn, fn, reads=(), writes=(), extra=(), signal=True):
        q = self.q[qn]
        deps = self._deps(reads, writes) + list(extra)
        waits = self._prune(q, deps)
        tok = None
        if signal:
            q.sem.val += 1
            tok = (q.sem, q.sem.val)
            q.ops.append((waits, fn, q.sem, 1))
            for s in reads:
                s.r.append(tok)
            for s in writes:
                s.w = [tok]
                s.r = []
        else:
            q.ops.append((waits, fn, None, 0))
        return tok

    def group(self, qn, fns, reads=(), writes=(), extra=()):
        q = self.q[qn]
        deps = self._deps(reads, writes) + list(extra)
        waits = self._prune(q, deps)
        q.sem.val += 1
        tok = (q.sem, q.sem.val)
        n = len(fns)
        for i, fn in enumerate(fns):
            q.ops.append((waits if i == 0 else [], fn,
                          q.sem if i == n - 1 else None, 1))
        for s in reads:
            s.r.append(tok)
        for s in writes:
            s.w = [tok]
            s.r = []
        return tok

    def dma(self, qn, out, in_, semname, reads=(), writes=(), extra=(), append_w=False):
        q = self.q[qn]
        sem = self.dsem(semname)
        deps = self._deps(reads, [] if append_w else writes) + list(extra)
        waits = self._prune(q, deps)
        sem.val += 16
        tok = (sem, sem.val)

        def fn(eng, out=out, in_=in_):
            return eng.dma_start(out=out, in_=in_)
        q.ops.append((waits, fn, sem, 16))
        for s in reads:
            s.r.append(tok)
        for s in writes:
            if append_w:
                s.w = s.w + [tok]
            else:
                s.w = [tok]
                s.r = []
        self.all_dma_toks.append(tok)
        return tok

    def barrier(self):
        toks = [(q.sem, q.sem.val) for q in self.q.values() if q.sem.val > 0]
        toks += [(s, s.val) for s in self.dma_sems.values() if s.val > 0]
        for qn, q in self.q.items():
            waits = self._prune(q, toks)
            if waits:
                q.ops.append((waits, None, None, 0))

    def replay(self):
        nc = self.nc
        with nc.Block() as block:
            def run(q):
                def body(eng):
                    for (waits, fn, isem, amt) in q.ops:
                        for (sem, v) in waits:
                            eng.wait_ge(sem.h, v)
                        if fn is None:
                            continue
                        ins = fn(eng)
                        if isem is not None:
                            ins.then_inc(isem.h, amt)
                return body
            block.tensor(run(self.q["pe"]))
            block.scalar(run(self.q["act"]))
            block.vector(run(self.q["dve"]))
            block.gpsimd(run(self.q["pool"]))
            block.sync(run(self.q["sp"]))
        for q in self.q.values():
            q.ops = []


HQ = 8
DFF = 2816
NF = DFF // P
MAGIC = 12582912.0
C1 = 6.28125
C2 = 2.0 * math.pi - 6.28125
INV2PI = 1.0 / (2.0 * math.pi)
LAM_INIT1 = 0.8 - 0.6 * math.exp(-0.3 * 1)
DBG = {}


def build_program(n_layers=2, stop=None):
    nc = bass.Bass("TRN2", target_bir_lowering=False)

    def din(name, shape, dt=F32):
        return nc.dram_tensor(name, shape, dt, kind="ExternalInput").ap()

    def dscr(name, shape, dt):
        return nc.dram_tensor(name, shape, dt, kind="Internal").ap()

    x_in = din("x", [S, D])
    cT_in = din("cT", [P, NCH])
    pos_in = din("pos", [1, S], I32)
    ada_w = din("ada_w", [2, D, 6 * D])
    adab_in = din("adab", [2, P, 48])
    n1g_in = din("n1g", [2, P, NCH])
    n2g_in = din("n2g", [2, P, NCH])
    fng_in = din("fng", [P, NCH])
    wgu_in = din("w_gu", [2, D, 2 * DFF])
    wd_in = din("w_d", [2, DFF, D])
    win_in = din("w_in", [D, 1184])
    qng_in = din("qng", [P, 3])
    kvng_in = din("kvng", [P, 2])
    wuq_in = din("w_uq", [384, 768])
    wukv_in = din("w_ukv", [256, 1024])
    poolw_in = din("pool_w", [4, P, P])
    poolb_in = din("pool_b", [P, 4])
    pools_in = din("pool_s", [P, 4])
    woutA_in = din("w_outA", [D, D])
    wqkv_in = din("w_qkv", [D, 3 * D])
    lamv_in = din("lamv", [P, 4, 64])
    subg_in = din("subg", [P, 1])
    woutD_in = din("w_outD", [D, D])
    identf_in = din("ident_f", [P, P])
    tri_in = din("tri", [P, P])
    maskb_in = din("maskb", [P, P])
    rmat_in = din("rmat", [2, P, P])
    invrows_in = din("invrows", [P, 2])
    cntinv_in = din("cntinv", [P, 4, 16])
    out = nc.dram_tensor("out", [S, D], F32, kind="ExternalOutput").ap()

    xT_s = dscr("xT_s", [P, NCH, S], F32)
    tab_s = dscr("tab_s", [4, P, S], F32)
    wgu_s = dscr("wgu_s", [2, NF, P, NCH * 256], BF16)
    q_s = dscr("q_s", [P, 8, S], BF16)
    k_s = dscr("k_s", [P, 8, S], BF16)
    v_s = dscr("v_s", [P, 32 * 1024], BF16)
    cat_s = dscr("cat_s", [P, NCH, S], BF16)

    with contextlib.ExitStack() as gst:
        R = Rec(nc, gst)
        _uid = [0]

        def _un(name):
            _uid[0] += 1
            return f"sb{_uid[0]}_{name}"
        gsb = lambda name, shape, dt: gst.enter_context(nc.sbuf_tensor(_un(name), shape, dt))
        psall = gst.enter_context(nc.psum_tensor("psall", [P, 8, TB], F32))
        psb = [psall[:, i, :] for i in range(8)]
        s_psb = [Slot(f"psb{i}", excl=True) for i in range(8)]

        class Banks:
            def __init__(self, ids):
                self.ids = list(ids)
                self.i = 0

            def next(self):
                b = self.ids[self.i % len(self.ids)]
                self.i += 1
                return b

        def ACT(out_, in_, func, reads, writes, **kw):
            return R.op("act", lambda e: e.activation(out=out_, in_=in_, func=func, **kw), reads=reads, writes=writes)

        def TT(q, out_, in0, in1, op, reads, writes):
            return R.op(q, lambda e: e.tensor_tensor(out=out_, in0=in0, in1=in1, op=op), reads=reads, writes=writes)

        def TS(q, out_, in0, s1, s2, op0, op1, reads, writes):
            if op1 is None:
                return R.op(q, lambda e: e.tensor_single_scalar(out=out_, in_=in0, scalar=s1, op=op0), reads=reads, writes=writes)
            return R.op(q, lambda e: e.tensor_scalar(out=out_, in0=in0, scalar1=s1, scalar2=s2, op0=op0, op1=op1),
                        reads=reads, writes=writes)

        def STT(out_, in0, scalar, in1, op0, op1, reads, writes):
            return R.op("dve", lambda e: e.scalar_tensor_tensor(out=out_, in0=in0, scalar=scalar, in1=in1, op0=op0, op1=op1),
                        reads=reads, writes=writes)

        def CP(q, out_, in_, reads, writes):
            if q == "act":
                return R.op("act", lambda e: e.copy(out=out_, in_=in_), reads=reads, writes=writes)
            return R.op(q, lambda e: e.tensor_copy(out=out_, in_=in_), reads=reads, writes=writes)

        def MM(out_, pairs, reads, writes):
            n = len(pairs)
            fns = []
            for i, (l, r) in enumerate(pairs):
                fns.append(lambda e, l=l, r=r, i=i: e.matmul(out_, lhsT=l, rhs=r, start=(i == 0), stop=(i == n - 1)))
            return R.group("pe", fns, reads=reads, writes=writes)

        def RECIP(out_, in_, reads, writes):
            return R.op("dve", lambda e: e.reciprocal(out=out_, in_=in_), reads=reads, writes=writes)

        def LOAD(q, out_, in_, sem, writes, reads=()):
            return R.dma(q, out_, in_, sem, reads=reads, writes=writes)

        def STORE(q, out_, in_, sem, reads):
            return R.dma(q, out_, in_, sem, reads=reads, writes=[Slot()])

        identf = gsb("identf", [P, P], F32)
        ones_bf = gsb("ones_bf", [P, P], BF16)
        tri_bf = gsb("tri_bf", [P, P], BF16)
        rmat_bf = gsb("rmat_bf", [P, 2, P], BF16)
        maskb_bf = gsb("maskb_bf", [P, P], BF16)
        ident_bf = gsb("ident_bf", [P, P], BF16)
        fng = gsb("fng", [P, NCH], F32)
        AB = gsb("AB", [P, 2, 6, NCH], F32)
        halfpi = gsb("halfpi", [P, 1], F32)
        epsc = gsb("epsc", [P, 1], F32)
        lam_neg = gsb("lam_neg", [P, 1], F32)
        subg2 = gsb("subg2", [P, 1], F32)
        s_c = Slot("consts")
        s_ones = Slot("ones")
        s_AB = Slot("AB")
        s_lam = Slot("lam")
        LOAD("sp", identf[:], identf_in, "c0", [s_c])
        R.dma("sp", fng[:], fng_in, "c0", writes=[s_c], append_w=True)
        R.dma("pool", tri_bf[:], tri_in, "c1", writes=[s_c], append_w=True)
        R.dma("pool", maskb_bf[:], maskb_in, "c1", writes=[s_c], append_w=True)
        R.dma("pool", ident_bf[:], identf_in, "c1", writes=[s_c], append_w=True)
        R.dma("pool", rmat_bf[:], rmat_in.rearrange("j p m -> p j m"), "c1", writes=[s_c], append_w=True)
        R.op("pool", lambda e: e.memset(ones_bf[:], 1.0), writes=[s_ones])
        R.op("pool", lambda e: e.memset(halfpi[:], math.pi / 2.0), writes=[s_ones])
        s_ones.w = [(R.q["pool"].sem, R.q["pool"].sem.val)]
        R.op("pool", lambda e: e.memset(epsc[:], EPS), writes=[Slot()])
        s_ones.w = [(R.q["pool"].sem, R.q["pool"].sem.val)]

        def rms_rstd(sq_tile, s_sq, src_chunks, s_src, n, dim, bank, rstd_tile, s_rstd, np_=P, sq_eng="act"):
            for c in range(n):
                if sq_eng == "act" or c % 2 == 0:
                    ACT(sq_tile[:np_, c, :], src_chunks[c], AF.Square, reads=s_src[c], writes=[s_sq[c]])
                else:
                    TT("pool", sq_tile[:np_, c, :], src_chunks[c], src_chunks[c], ALU.mult, reads=s_src[c], writes=[s_sq[c]])
            MM(psb[bank][:np_, :], [(ones_bf[:np_, :np_], sq_tile[:np_, c, :]) for c in range(n)],
               reads=s_sq[:n] + [s_ones], writes=[s_psb[bank]])
            ACT(rstd_tile[:np_, :], psb[bank][:np_, :], AF.Ln, reads=[s_psb[bank], s_ones], writes=[s_rstd],
                scale=1.0 / dim, bias=epsc[:np_, :])
            ACT(rstd_tile[:np_, :], rstd_tile[:np_, :], AF.Exp, reads=[s_rstd], writes=[s_rstd], scale=-0.5)

        with contextlib.ExitStack() as st:
            sb = lambda name, shape, dt: st.enter_context(nc.sbuf_tensor(_un(name), shape, dt))
            cT = sb("cT", [P, NCH], F32)
            cond = sb("cond", [P, NCH], F32)
            adab = sb("adab", [P, 2, 48], F32)
            n1g = sb("n1g", [P, 2, NCH], F32)
            n2g = sb("n2g", [P, 2, NCH], F32)
            modt = sb("modt", [P, 2, 48], F32)
            s_in = Slot()
            s_cond = Slot()
            s_modt = Slot()
            LOAD("sp", cT[:], cT_in, "a0", [s_in])
            R.dma("sp", adab[:], adab_in.rearrange("l p m -> p l m"), "a0", writes=[s_in], append_w=True)
            R.dma("sp", n1g[:], n1g_in.rearrange("l p m -> p l m"), "a0", writes=[s_in], append_w=True)
            R.dma("sp", n2g[:], n2g_in.rearrange("l p m -> p l m"), "a0", writes=[s_in], append_w=True)
            ACT(cond[:], cT[:], AF.Silu, reads=[s_in], writes=[s_cond])
            aw = [sb(f"aw{i}", [P, NCH, 512], F32) for i in range(2)]
            s_aw = [Slot() for _ in range(2)]
            for l in range(n_layers):
                for mg in range(12):
                    i = (l * 12 + mg) % 2
                    LOAD("sp", aw[i][:], ada_w[l, :, mg * 512:(mg + 1) * 512].rearrange("(kc p) n -> p kc n", p=P),
                         f"aw{i}", [s_aw[i]])
                    for j in range(4):
                        m = mg * 4 + j
                        MM(psb[l][:, m:m + 1], [(aw[i][:, kc, j * P:(j + 1) * P], cond[:, kc:kc + 1]) for kc in range(NCH)],
                           reads=[s_aw[i], s_cond], writes=[s_psb[l]])
                TT("dve", modt[:, l, :], psb[l][:, 0:48], adab[:, l, :], ALU.add, reads=[s_psb[l], s_in], writes=[s_modt])
                STT(AB[:, l, 0, :], modt[:, l, 8:16], 1.0, n1g[:, l, :], ALU.add, ALU.mult, reads=[s_modt, s_in], writes=[s_AB])
                CP("dve", AB[:, l, 1, :], modt[:, l, 0:8], reads=[s_modt], writes=[s_AB])
                CP("dve", AB[:, l, 2, :], modt[:, l, 16:24], reads=[s_modt], writes=[s_AB])
                STT(AB[:, l, 3, :], modt[:, l, 32:40], 1.0, n2g[:, l, :], ALU.add, ALU.mult, reads=[s_modt, s_in], writes=[s_AB])
                CP("dve", AB[:, l, 4, :], modt[:, l, 24:32], reads=[s_modt], writes=[s_AB])
                CP("dve", AB[:, l, 5, :], modt[:, l, 40:48], reads=[s_modt], writes=[s_AB])
            lamv = sb("lamv", [P, 4, 64], F32)
            lprod = sb("lprod", [P, 2, 64], F32)
            lsum = sb("lsum", [P, 2], F32)
            subg = sb("subg", [P, 1], F32)
            s_lv = Slot()
            s_lp = Slot()
            LOAD("sp", lamv[:], lamv_in, "a1", [s_lv])
            R.dma("sp", subg[:], subg_in, "a1", writes=[s_lv], append_w=True)
            TT("dve", lprod[:, 0, :], lamv[:, 0, :], lamv[:, 1, :], ALU.mult, reads=[s_lv], writes=[s_lp])
            TT("dve", lprod[:, 1, :], lamv[:, 2, :], lamv[:, 3, :], ALU.mult, reads=[s_lv], writes=[s_lp])
            R.op("dve", lambda e: e.reduce_sum(out=lsum[:], in_=lprod[:], axis=AX.X), reads=[s_lp], writes=[s_lp])
            ACT(lsum[:], lsum[:], AF.Exp, reads=[s_lp], writes=[s_lp])
            TT("dve", lam_neg[:], lsum[:, 1:2], lsum[:, 0:1], ALU.subtract, reads=[s_lp], writes=[s_lam])
            TS("dve", lam_neg[:], lam_neg[:], -LAM_INIT1, None, ALU.add, None, reads=[s_lam], writes=[s_lam])
            TS("dve", subg2[:], subg[:], 1.0 - LAM_INIT1, None, ALU.mult, None, reads=[s_lv], writes=[s_lam])

            posi = sb("posi", [P, S], I32)
            posf = sb("posf", [P, S], F32)
            invrows = sb("invrows", [P, 2], F32)
            ang = sb("ang", [P, S], F32)
            kk = sb("kk", [P, S], F32)
            rr = sb("rr", [P, S], F32)
            tS = sb("tS", [P, S], F32)
            tC = sb("tC", [P, S], F32)
            s_pos, s_ang, s_kk, s_rr, s_tS, s_tC = Slot(), Slot(), Slot(), Slot(), Slot(), Slot()
            LOAD("sp", posi[:], pos_in.broadcast_to([P, S]), "a2", [s_pos])
            R.dma("sp", invrows[:], invrows_in, "a2", writes=[s_pos], append_w=True)
            CP("dve", posf[:], posi[:], reads=[s_pos], writes=[s_pos])
            for j in range(2):
                TS("dve", ang[:], posf[:], invrows[:, j:j + 1], None, ALU.mult, None, reads=[s_pos], writes=[s_ang])
                TS("dve", kk[:], ang[:], INV2PI, MAGIC, ALU.mult, ALU.add, reads=[s_ang], writes=[s_kk])
                TS("dve", kk[:], kk[:], -MAGIC, None, ALU.add, None, reads=[s_kk], writes=[s_kk])
                STT(rr[:], kk[:], -C1, ang[:], ALU.mult, ALU.add, reads=[s_kk, s_ang], writes=[s_rr])
                STT(rr[:], kk[:], -C2, rr[:], ALU.mult, ALU.add, reads=[s_kk, s_rr], writes=[s_rr])
                ACT(tS[:], rr[:], AF.Sin, reads=[s_rr], writes=[s_tS], scale=0.5)
                ACT(rr[:], rr[:], AF.Abs, reads=[s_rr], writes=[s_rr])
                ACT(tC[:], rr[:], AF.Sin, reads=[s_rr, s_ones], writes=[s_tC], scale=-0.5, bias=halfpi[:])
                STT(tS[:], tS[:], 2.0, tC[:], ALU.mult, ALU.mult, reads=[s_tS, s_tC], writes=[s_tS])
                ACT(tC[:], rr[:], AF.Sin, reads=[s_rr, s_ones], writes=[s_tC], scale=-1.0, bias=halfpi[:])
                STORE("sp", tab_s[2 * j], tC[:], "a3", reads=[s_tC])
                STORE("sp", tab_s[2 * j + 1], tS[:], "a3", reads=[s_tS])
            R.barrier()
            R.replay()
        if stop == "A0":
            return nc

        def wgu_prep(l, stg, s_stg, flist):
            for f in flist:
                i = f % len(stg)
                for half in range(2):
                    src = wgu_in[l, :, half * DFF + f * P: half * DFF + (f + 1) * P].rearrange("(kc p) n -> p kc n", p=P)
                    R.dma("pool", stg[i][:, :, half * P:(half + 1) * P], src, f"wgst{i}",
                          writes=[s_stg[i]], append_w=(half == 1))
                R.dma("pool", wgu_s[l, f], stg[i][:].rearrange("p kc n -> p (kc n)"), f"wgsto{i}", reads=[s_stg[i]], writes=[Slot()])

        with contextlib.ExitStack() as st:
            sb = lambda name, shape, dt: st.enter_context(nc.sbuf_tensor(_un(name), shape, dt))
            stg = [sb(f"stg{i}", [P, NCH, 256], BF16) for i in range(3)]
            s_stg = [Slot() for _ in range(3)]
            xin = [sb(f"xin{i}", [P, 4, D], F32) for i in range(2)]
            s_xin = [Slot() for _ in range(2)]
            xT = [sb(f"xT{i}", [P, NCH, TB], F32) for i in range(2)]
            s_xT = [[Slot() for c in range(NCH)] for i in range(2)]
            PB = Banks(range(8))
            for tb in range(NTB):
                b = tb % 2
                LOAD("sp", xin[b][:], x_in[tb * TB:(tb + 1) * TB, :].rearrange("(t p) d -> p t d", p=P), f"xin{b}", [s_xin[b]])
                for c in range(NCH):
                    k = PB.next()
                    fns = [lambda e, k=k, t=t, c=c, b=b: e.transpose(
                        out=psb[k][:, t * P:(t + 1) * P], in_=xin[b][:, t, c * P:(c + 1) * P], identity=identf[:])
                        for t in range(4)]
                    R.group("pe", fns, reads=[s_xin[b], s_c], writes=[s_psb[k]])
                    CP("act" if c % 2 == 0 else "dve", xT[b][:, c, :], psb[k][:], reads=[s_psb[k]], writes=[s_xT[b][c]])
                STORE("sp", xT_s[:, :, tb * TB:(tb + 1) * TB], xT[b][:], f"xTst{b}", reads=s_xT[b])
                wgu_prep(0, stg, s_stg, range(tb * 3, min(NF, tb * 3 + 3)))
            R.barrier()
            R.replay()
        if stop == "X0":
            return nc

        def load_tab(tile, j, tb, sem, slot):
            LOAD("sp", tile[:], tab_s[j, :, tb * TB:(tb + 1) * TB], sem, [slot])

        def ffn_front(l, xb, s_xb, cat, s_cat, wout, s_wout, sq, s_sq, rstd, s_rstd, tmp, s_tmp, h2, s_h2, PB):
            for c in range(NCH):
                k = PB.next()
                MM(psb[k][:], [(wout[:, kc, c * P:(c + 1) * P], cat[:, kc, :]) for kc in range(NCH)],
                   reads=s_cat + [s_wout], writes=[s_psb[k]])
                STT(xb[:, c, :], psb[k][:], AB[:, l, 2, c:c + 1], xb[:, c, :], ALU.mult, ALU.add,
                    reads=[s_psb[k], s_AB], writes=[s_xb[c]])
            k = PB.next()
            rms_rstd(sq, s_sq, [xb[:, c, :] for c in range(NCH)], [[s_xb[c]] for c in range(NCH)], NCH, D, k, rstd, s_rstd,
                     sq_eng="mix")
            for c in range(NCH):
                t = c % 2
                STT(tmp[t][:], xb[:, c, :], AB[:, l, 3, c:c + 1], rstd[:], ALU.mult, ALU.mult,
                    reads=[s_xb[c], s_rstd, s_AB], writes=[s_tmp[t]])
                ACT(h2[:, c, :], tmp[t][:], AF.Identity, reads=[s_tmp[t], s_AB], writes=[s_h2[c]], bias=AB[:, l, 4, c:c + 1], scale=1.0)

        def ffn_gateup(l, h2, s_h2, actb, s_act, sg, s_sg, wg, s_wg, PB, wgi):
            for f in range(NF):
                i = wgi[0] % len(wg)
                wgi[0] += 1
                LOAD("sp", wg[i][:].rearrange("p kc n -> p (kc n)"), wgu_s[l, f], f"wg{i}", [s_wg[i]])
                kg = PB.next()
                ku = PB.next()
                MM(psb[kg][:], [(wg[i][:, kc, 0:P], h2[:, kc, :]) for kc in range(NCH)], reads=s_h2 + [s_wg[i]], writes=[s_psb[kg]])
                MM(psb[ku][:], [(wg[i][:, kc, P:2 * P], h2[:, kc, :]) for kc in range(NCH)], reads=s_h2 + [s_wg[i]], writes=[s_psb[ku]])
                t = f % 2
                ACT(sg[t][:], psb[kg][:], AF.Silu, reads=[s_psb[kg]], writes=[s_sg[t]])
                TT("dve", actb[:, f, :], sg[t][:], psb[ku][:], ALU.mult, reads=[s_sg[t], s_psb[ku]], writes=[s_act[f]])

        def ffn_down(l, xb, s_xb, wd, s_wd, actb, s_act, PB):
            for c in range(NCH):
                k = PB.next()
                MM(psb[k][:], [(wd[:, f, c * P:(c + 1) * P], actb[:, f, :]) for f in range(NF)], reads=s_act + [s_wd], writes=[s_psb[k]])
                STT(xb[:, c, :], psb[k][:], AB[:, l, 5, c:c + 1], xb[:, c, :], ALU.mult, ALU.add,
                    reads=[s_psb[k], s_AB], writes=[s_xb[c]])

        def norm1_block(l, xb, s_xb, sq, s_sq, rstd, s_rstd, tmp, s_tmp, h, s_h, PB):
            k = PB.next()
            rms_rstd(sq, s_sq, [xb[:, c, :] for c in range(NCH)], [[s_xb]] * NCH, NCH, D, k, rstd, s_rstd, sq_eng="mix")
            for c in range(NCH):
                t = c % 2
                STT(tmp[t][:], xb[:, c, :], AB[:, l, 0, c:c + 1], rstd[:], ALU.mult, ALU.mult,
                    reads=[s_xb, s_rstd, s_AB], writes=[s_tmp[t]])
                ACT(h[:, c, :], tmp[t][:], AF.Identity, reads=[s_tmp[t], s_AB], writes=[s_h[c]], bias=AB[:, l, 1, c:c + 1], scale=1.0)

        def post_phase(l, woutX_in, final):
            with contextlib.ExitStack() as st:
                sb = lambda name, shape, dt: st.enter_context(nc.sbuf_tensor(_un(name), shape, dt))
                wd = sb("wd", [P, NF, D], BF16)
                wout = sb("wout", [P, NCH, D], BF16)
                s_wd, s_wout = Slot(), Slot()
                LOAD("pool", wout[:], woutX_in.rearrange("(kc p) n -> p kc n", p=P), "wout", [s_wout])
                LOAD("pool", wd[:], wd_in[l].rearrange("(f p) n -> p f n", p=P), "wd", [s_wd])
                wg = [sb(f"wg{i}", [P, NCH, 256], BF16) for i in range(4)]
                s_wg = [Slot() for _ in range(4)]
                xb = [sb(f"xb{i}", [P, NCH, TB], F32) for i in range(2)]
                s_xb = [[Slot() for c in range(NCH)] for i in range(2)]
                cat = [sb(f"cat{i}", [P, NCH, TB], BF16) for i in range(2)]
                s_cat = [Slot() for _ in range(2)]
                sq = sb("sq", [P, NCH, TB], BF16)
                s_sq = [Slot() for _ in range(NCH)]
                rstd = sb("rstd", [P, TB], F32)
                s_rstd = Slot()
                tmp = [sb(f"tmp{i}", [P, TB], F32) for i in range(2)]
                s_tmp = [Slot() for _ in range(2)]
                h2 = [sb(f"h2{i}", [P, NCH, TB], BF16) for i in range(2)]
                s_h2 = [[Slot() for _ in range(NCH)] for i in range(2)]
                actb = sb("actb", [P, NF, TB], BF16)
                s_act = [Slot() for _ in range(NF)]
                sg = [sb(f"sg{i}", [P, TB], F32) for i in range(2)]
                s_sg = [Slot() for _ in range(2)]
                if final:
                    ot = sb("ot", [P, 4, D], F32)
                    s_ot = [Slot() for _ in range(8)]
                PB = Banks(range(8))
                wgi = [0]
                def front(tb):
                    b = tb % 2
                    R.dma("sp", xb[b][:], xT_s[:, :, tb * TB:(tb + 1) * TB], f"xb{b}", writes=s_xb[b])
                    LOAD("sp", cat[b][:], cat_s[:, :, tb * TB:(tb + 1) * TB], f"cat{b}", [s_cat[b]])
                    ffn_front(l, xb[b], s_xb[b], cat[b], [s_cat[b]], wout, s_wout, sq, s_sq, rstd, s_rstd, tmp, s_tmp,
                              h2[b], s_h2[b], PB)

                front(0)
                for tb in range(NTB):
                    b = tb % 2
                    ffn_gateup(l, h2[b], s_h2[b], actb, s_act, sg, s_sg, wg, s_wg, PB, wgi)
                    if tb + 1 < NTB:
                        front(tb + 1)
                    ffn_down(l, xb[b], s_xb[b], wd, s_wd, actb, s_act, PB)
                    if not final:
                        STORE("sp", xT_s[:, :, tb * TB:(tb + 1) * TB], xb[b][:], f"xbst{b}", reads=s_xb[b])
                    else:
                        k = PB.next()
                        rms_rstd(sq, s_sq, [xb[b][:, c, :] for c in range(NCH)], [[s_xb[b][c]] for c in range(NCH)], NCH, D, k,
                                 rstd, s_rstd, sq_eng="mix")
                        for c in range(NCH):
                            STT(xb[b][:, c, :], xb[b][:, c, :], fng[:, c:c + 1], rstd[:], ALU.mult, ALU.mult,
                                reads=[s_rstd, s_c], writes=[s_xb[b][c]])
                        for t in range(4):
                            for half in range(2):
                                k = PB.next()
                                fns = [lambda e, k=k, t=t, j=j, half=half, b=b: e.transpose(
                                    out=psb[k][:, j * P:(j + 1) * P], in_=xb[b][:, half * 4 + j, t * P:(t + 1) * P],
                                    identity=identf[:]) for j in range(4)]
                                R.group("pe", fns, reads=s_xb[b][half * 4:half * 4 + 4] + [s_c], writes=[s_psb[k]])
                                CP("act", ot[:, t, half * TB:(half + 1) * TB], psb[k][:], reads=[s_psb[k]], writes=[s_ot[t * 2 + half]])
                        STORE("sp", out[tb * TB:(tb + 1) * TB, :].rearrange("(t p) d -> p t d", p=P), ot[:], "ost", reads=s_ot)
                R.barrier()
                R.replay()

        SC0 = 96.0 ** -0.5
        with contextlib.ExitStack() as st:
            sb = lambda name, shape, dt: st.enter_context(nc.sbuf_tensor(_un(name), shape, dt))
            win = sb("win", [P, NCH, 1184], BF16)
            wuq = sb("wuq", [P, 3, 800], BF16)
            wukv = sb("wukv", [P, 2, 1024], BF16)
            poolw = sb("poolw", [P, 4, P], BF16)
            qng = sb("qng", [P, 3], F32)
            kvng = sb("kvng", [P, 2], F32)
            poolb = sb("poolb", [P, 4], F32)
            pools = sb("pools", [P, 4], F32)
            cntinv = sb("cntinv", [P, 4, 16], F32)
            s_w = Slot()
            LOAD("pool", win[:], win_in.rearrange("(kc p) n -> p kc n", p=P), "w0", [s_w])
            s_wq = Slot()
            R.op("pool", lambda e: e.memset(wuq[:, :, 768:800], 0.0), writes=[s_wq])
            R.dma("pool", wuq[:, :, 0:768], wuq_in.rearrange("(kc p) n -> p kc n", p=P), "w0", writes=[s_w], append_w=True)
            R.dma("pool", wukv[:], wukv_in.rearrange("(kc p) n -> p kc n", p=P), "w0", writes=[s_w], append_w=True)
            R.dma("pool", poolw[:], poolw_in.rearrange("g c d -> c g d"), "w0", writes=[s_w], append_w=True)
            R.dma("sp", qng[:], qng_in, "w1", writes=[s_w], append_w=True)
            R.dma("sp", kvng[:], kvng_in, "w1", writes=[s_w], append_w=True)
            R.dma("sp", poolb[:], poolb_in, "w1", writes=[s_w], append_w=True)
            R.dma("sp", pools[:], pools_in, "w1", writes=[s_w], append_w=True)
            R.dma("sp", cntinv[:], cntinv_in, "w1", writes=[s_w], append_w=True)
            xb = [sb(f"xb{i}", [P, NCH, TB], F32) for i in range(2)]
            s_xb = [Slot() for _ in range(2)]
            tabC = [sb(f"tabC{i}", [P, TB], F32) for i in range(2)]
            tabS = [sb(f"tabS{i}", [P, TB], F32) for i in range(2)]
            s_tab = [Slot() for _ in range(2)]
            sq = sb("sq", [P, NCH, TB], BF16)
            s_sq = [Slot() for _ in range(NCH)]
            rstd = sb("rstd", [P, TB], F32)
            s_rstd = Slot()
            rstq = sb("rstq", [P, TB], F32)
            s_rstq = Slot()
            rstk = sb("rstk", [P, TB], F32)
            s_rstk = Slot()
            tmp = [sb(f"tmp{i}", [P, TB], F32) for i in range(2)]
            s_tmp = [Slot() for _ in range(2)]
            h = sb("h", [P, NCH, TB], BF16)
            s_h = [Slot() for _ in range(NCH)]
            cq = sb("cq", [P, 5, TB], F32)
            s_cq = [Slot() for _ in range(5)]
            cn = sb("cn", [P, 5, TB], BF16)
            s_cn = [Slot() for _ in range(5)]
            u = sb("u", [P, 4, 16 + TB], F32)
            s_u = [Slot() for _ in range(4)]
            lv = [sb(f"lv{i}", [P, 16 + TB], F32) for i in range(2)]
            s_lv = [Slot() for _ in range(2)]
            pooled = sb("pooled", [P, 4, TB], BF16)
            s_pl = [Slot() for _ in range(4)]
            ptmp = sb("ptmp", [P, 16], F32)
            s_ptmp = Slot()
            qb = [sb(f"qb{i}", [P, TB], BF16) for i in range(2)]
            s_qb = [Slot() for _ in range(2)]
            t1 = [sb(f"t1{i}", [P, TB], F32) for i in range(2)]
            s_t1 = [Slot() for _ in range(2)]
            t2 = [sb(f"t2{i}", [P, TB], F32) for i in range(2)]
            s_t2 = [Slot() for _ in range(2)]
            qT = [sb(f"qT{i}", [P, 8, TB], BF16) for i in range(2)]
            s_qT = [[Slot() for hh in range(8)] for _ in range(2)]
            kT = [sb(f"kT{i}", [P, 8, TB], BF16) for i in range(2)]
            s_kT = [[Slot() for hh in range(8)] for _ in range(2)]
            s_kTr = [[Slot() for hh in range(8)] for _ in range(2)]
            vt = [sb(f"vt{i}", [P, 4, 8, 65], BF16) for i in range(2)]
            s_vt = [[Slot() for tt in range(4)] for _ in range(2)]
            catp = [sb(f"catp{i}", [P, 4, TB], BF16) for i in range(2)]
            s_catp = [[Slot() for g in range(4)] for _ in range(2)]
            for i in range(2):
                R.op("pool", lambda e, i=i: e.memset(vt[i][:], 1.0), writes=s_vt[i])
            R.op("pool", lambda e: e.memset(u[:], 0.0), writes=s_u)
            PB = Banks(range(8))
            for tb in range(DBG.get('l0p1_ntb', NTB)):
                b = tb % 2
                LOAD("sp", xb[b][:], xT_s[:, :, tb * TB:(tb + 1) * TB], f"xb{b}", [s_xb[b]])
                R.dma("sp", tabC[b][:], tab_s[0, :, tb * TB:(tb + 1) * TB], f"tab{b}", writes=[s_tab[b]])
                R.dma("sp", tabS[b][:], tab_s[1, :, tb * TB:(tb + 1) * TB], f"tab{b}", writes=[s_tab[b]], append_w=True)
                norm1_block(0, xb[b], s_xb[b], sq, s_sq, rstd, s_rstd, tmp, s_tmp, h, s_h, PB)
                if DBG.get('sec', 99) < 1:
                    continue
                for j in range(5):
                    k = PB.next()
                    MM(psb[k][:], [(win[:, kc, j * P:(j + 1) * P], h[:, kc, :]) for kc in range(NCH)], reads=s_h + [s_w], writes=[s_psb[k]])
                    CP("act", cq[:, j, :], psb[k][:], reads=[s_psb[k]], writes=[s_cq[j]])
                k = PB.next()
                rms_rstd(sq, s_sq, [cq[:, j, :] for j in range(3)], [[s_cq[j]] for j in range(3)], 3, 384, k, rstq, s_rstq, sq_eng="mix")
                for j in range(3):
                    STT(cn[:, j, :], cq[:, j, :], qng[:, j:j + 1], rstq[:], ALU.mult, ALU.mult, reads=[s_cq[j], s_rstq, s_w], writes=[s_cn[j]])
                k = PB.next()
                rms_rstd(sq[:, 3:5, :], s_sq[3:5], [cq[:, 3 + j, :] for j in range(2)], [[s_cq[3 + j]] for j in range(2)], 2, 256, k, rstk, s_rstk, sq_eng="mix")
                for j in range(2):
                    STT(cn[:, 3 + j, :], cq[:, 3 + j, :], kvng[:, j:j + 1], rstk[:], ALU.mult, ALU.mult,
                        reads=[s_cq[3 + j], s_rstk, s_w], writes=[s_cn[3 + j]])
                if DBG.get('sec', 99) < 2:
                    continue
                k = PB.next()
                MM(psb[k][64:96, :], [(win[:, kc, 640:672], h[:, kc, :]) for kc in range(NCH)], reads=s_h + [s_w], writes=[s_psb[k]])
                i2 = 0
                CP("act", qb[i2][64:96, :], psb[k][64:96, :], reads=[s_psb[k]], writes=[s_qb[i2]])
                k2 = PB.next()
                MM(psb[k2][64:96, :], [(rmat_bf[64:96, 0, 64:96], qb[i2][64:96, :])], reads=[s_qb[i2], s_c], writes=[s_psb[k2]])
                TT("dve", t1[i2][64:96, :], psb[k][64:96, :], tabC[b][64:96, :], ALU.mult, reads=[s_psb[k], s_tab[b]], writes=[s_t1[i2]])
                TT("dve", t2[i2][64:96, :], psb[k2][64:96, :], tabS[b][64:96, :], ALU.mult, reads=[s_psb[k2], s_tab[b]], writes=[s_t2[i2]])
                TT("pool", t1[i2][64:96, :], t1[i2][64:96, :], t2[i2][64:96, :], ALU.add, reads=[s_t2[i2]], writes=[s_t1[i2]])
                for hh in range(8):
                    CP("pool" if hh % 2 == 0 else "act", kT[b][64:96, hh, :], t1[i2][64:96, :], reads=[s_t1[i2]], writes=[s_kTr[b][hh]])
                if DBG.get('sec', 99) < 3:
                    continue
                for g in range(4):
                    k = PB.next()
                    MM(psb[k][:], [(win[:, kc, 672 + g * P:672 + (g + 1) * P], h[:, kc, :]) for kc in range(NCH)], reads=s_h + [s_w], writes=[s_psb[k]])
                    CP("act", u[:, g, 16:16 + TB], psb[k][:], reads=[s_psb[k]], writes=[s_u[g]])
                if DBG.get('sec', 99) < 4:
                    continue
                SUB = 99
                bank_q = {}

                def st1(hh):
                    k = PB.next()
                    bank_q[hh] = k
                    i2 = hh % 2
                    MM(psb[k][:], [(wuq[:, kc, hh * 96:hh * 96 + P], cn[:, kc, :]) for kc in range(3)], reads=s_cn[0:3] + [s_w, s_wq], writes=[s_psb[k]])
                    CP("act", qb[i2][:], psb[k][:], reads=[s_psb[k]], writes=[s_qb[i2]])

                def st2(hh):
                    k = bank_q[hh]
                    i2 = hh % 2
                    k2 = PB.next()
                    MM(psb[k2][:], [(rmat_bf[:, 0, :], qb[i2][:])], reads=[s_qb[i2], s_c], writes=[s_psb[k2]])
                    TT("dve", t1[i2][:], psb[k][:], tabC[b][:], ALU.mult, reads=[s_psb[k], s_tab[b]], writes=[s_t1[i2]])
                    TT("dve", t2[i2][:], psb[k2][:], tabS[b][:], ALU.mult, reads=[s_psb[k2], s_tab[b]], writes=[s_t2[i2]])
                    TT("pool", qT[b][:, hh, :], t1[i2][:], t2[i2][:], ALU.add, reads=[s_t1[i2], s_t2[i2]], writes=[s_qT[b][hh]])

                st1(0)
                for hh in range(8):
                    if hh + 1 < 8:
                        st1(hh + 1)
                    st2(hh)
                if SUB >= 4:
                    STORE("sp", q_s[0:96, :, tb * TB:(tb + 1) * TB], qT[b][0:96, :, :], f"qst{b}", reads=s_qT[b])
                if DBG.get('sec', 99) < 5:
                    continue
                for hh in range(8):
                    k = PB.next()
                    MM(psb[k][0:64, :], [(wukv[:, kc, hh * 128:hh * 128 + 64], cn[:, 3 + kc, :]) for kc in range(2)], reads=s_cn[3:5] + [s_w], writes=[s_psb[k]])
                    CP("act" if hh % 2 == 0 else "dve", kT[b][0:64, hh, :], psb[k][0:64, :], reads=[s_psb[k]], writes=[s_kT[b][hh]])
                STORE("sp", k_s[0:96, :, tb * TB:(tb + 1) * TB], kT[b][0:96, :, :], f"kst{b}", reads=s_kT[b] + s_kTr[b])
                if DBG.get('sec', 99) < 6:
                    continue
                for tt in range(4):
                    k = PB.next()
                    MM(psb[k][:].rearrange("p (h e) -> p h e", e=64), [(cn[:, 3 + kc, tt * P:(tt + 1) * P],
                                    wukv[:, kc, :].rearrange("p (h e) -> p h e", e=128)[:, :, 64:128]) for kc in range(2)],
                       reads=s_cn[3:5] + [s_w], writes=[s_psb[k]])
                    CP("dve" if tt % 2 == 0 else "act", vt[b][:, tt, :, 0:64], psb[k][:].rearrange("p (h e) -> p h e", e=64),
                       reads=[s_psb[k]], writes=[s_vt[b][tt]])
                STORE("sp", v_s[:, tb * 4 * 520:(tb + 1) * 4 * 520], vt[b][:].rearrange("p t h e -> p (t h e)"), f"vst{b}", reads=s_vt[b])
                if DBG.get('sec', 99) < 7:
                    continue
                for g in range(4):
                    w = 2 << g
                    src = u[:, g, :]
                    s_src = s_u[g]
                    sh = 1
                    lvl = 0
                    while sh < w:
                        dst = lv[lvl % 2]
                        TT("pool", dst[:, sh:16 + TB], src[:, sh:16 + TB], src[:, 0:16 + TB - sh], ALU.add,
                           reads=[s_src], writes=[s_lv[lvl % 2]])
                        src = dst
                        s_src = s_lv[lvl % 2]
                        sh *= 2
                        lvl += 1
                    STT(pooled[:, g, :], src[:, 16:16 + TB], 1.0 / w, u[:, g, 16:16 + TB], ALU.mult, ALU.subtract,
                        reads=[s_src, s_u[g]], writes=[s_pl[g]])
                    if tb == 0:
                        TT("dve", ptmp[:], src[:, 16:32], cntinv[:, g, :], ALU.mult, reads=[s_src, s_w], writes=[s_ptmp])
                        TT("dve", pooled[:, g, 0:16], ptmp[:], u[:, g, 16:32], ALU.subtract, reads=[s_ptmp, s_u[g], s_pl[g]], writes=[s_pl[g]])
                    k = PB.next()
                    MM(psb[k][:], [(poolw[:, g, :], pooled[:, g, :])], reads=[s_pl[g], s_w], writes=[s_psb[k]])
                    TS("dve", catp[b][:, g, :], psb[k][:], poolb[:, g:g + 1], pools[:, g:g + 1], ALU.add, ALU.mult,
                       reads=[s_psb[k], s_w], writes=[s_catp[b][g]])
                    CP("pool", u[:, g, 0:16], u[:, g, TB:TB + 16], reads=[], writes=[s_u[g]])
                STORE("sp", cat_s[:, 4:8, tb * TB:(tb + 1) * TB], catp[b][:], f"cpst{b}", reads=s_catp[b])
            R.barrier()
            R.replay()
        if stop == "L0P1":
            return nc

        with contextlib.ExitStack() as st:
            sb = lambda name, shape, dt: st.enter_context(nc.sbuf_tensor(_un(name), shape, dt))
            stg = [sb(f"stg{i}", [P, NCH, 256], BF16) for i in range(3)]
            s_stg = [Slot() for _ in range(3)]
            va = sb("va", [P, 32, 8, 65], BF16)
            s_va = Slot()
            LOAD("sp", va[:].rearrange("p t h e -> p (t h e)"), v_s[:, 0:32 * 520], "va", [s_va])
            vg = sb("vg", [P, 32, 8, P], BF16)
            s_vgh = [Slot() for _ in range(8)]
            R.op("pool", lambda e: e.memset(vg[:], 1.0), writes=s_vgh)
            for hh in range(8):
                off = 0 if hh % 2 == 0 else 64
                CP("pool" if hh % 2 == 0 else "dve", vg[:, :, hh, off:off + 64], va[:, :, hh, 0:64], reads=[s_va], writes=[s_vgh[hh]])
            qh = [sb(f"qh{i}", [P, S], BF16) for i in range(2)]
            kh = [sb(f"kh{i}", [P, S], BF16) for i in range(2)]
            s_qk = [Slot() for _ in range(2)]
            rinv = [sb(f"rinv{i}", [P, TB], F32) for i in range(2)]
            s_rinv = [Slot() for _ in range(2)]
            attn = sb("attn", [P, 4, S], BF16)
            s_attn = [Slot() for _ in range(4)]
            NPT = 6
            pt = [sb(f"ptx{i}", [P, TB], BF16) for i in range(NPT)]
            s_pt = [Slot() for _ in range(NPT)]
            SBK = [0, 1, 2, 3]
            OBK = [4, 5]
            LA = 2
            items = []
            for hh in range(8):
                for qblk in range(NTB):
                    nkt = 4 * qblk + 4
                    for kt in range(nkt):
                        items.append((hh, qblk, kt, nkt))
            prep_done = [0]

            def load_head(hh):
                i = hh % 2
                LOAD("sp", qh[i][0:96, :], q_s[0:96, hh, :], f"qh{i}", [s_qk[i]])
                R.dma("sp", kh[i][0:96, :], k_s[0:96, hh, :], f"qh{i}", writes=[s_qk[i]], append_w=True)

            def emitA(n):
                hh, qblk, kt, nkt = items[n]
                i = hh % 2
                if qblk == 0 and kt == 0:
                    if hh == 0:
                        load_head(0)
                    if hh + 1 < 8:
                        load_head(hh + 1)
                j = kt - 4 * qblk
                c0 = max(j, 0) * P
                sbk = SBK[n % len(SBK)]
                fns = [lambda e: e.matmul(psb[sbk][:, c0:TB], lhsT=kh[i][0:96, kt * P:(kt + 1) * P],
                                          rhs=qh[i][0:96, qblk * TB + c0:(qblk + 1) * TB], start=True, stop=(j < 0))]
                if j >= 0:
                    fns.append(lambda e: e.matmul(psb[sbk][:, c0:c0 + P], lhsT=ident_bf[:], rhs=maskb_bf[:], start=False, stop=True))
                R.group("pe", fns, reads=[s_qk[i], s_c], writes=[s_psb[sbk]])
                p_i = n % NPT
                ACT(pt[p_i][:, c0:TB], psb[sbk][:, c0:TB], AF.Exp, reads=[s_psb[sbk]], writes=[s_pt[p_i]], scale=SC0)

            def emitC(n):
                hh, qblk, kt, nkt = items[n]
                blk = hh * NTB + qblk
                ob = OBK[blk % 2]
                j = kt - 4 * qblk
                c0 = max(j, 0) * P
                p_i = n % NPT
                R.group("pe", [lambda e: e.matmul(psb[ob][:, c0:TB], lhsT=vg[:, kt, hh, :], rhs=pt[p_i][:, c0:TB],
                                                  start=(kt == 0), stop=(kt == nkt - 1))],
                        reads=[s_pt[p_i], s_vgh[hh]], writes=[s_psb[ob]])
                if kt == nkt - 1:
                    ri = blk % 2
                    odd = hh % 2
                    if odd == 0:
                        RECIP(rinv[ri][0:64, :], psb[ob][64:128, :], reads=[s_psb[ob]], writes=[s_rinv[ri]])
                        TT("dve", attn[0:64, hh // 2, qblk * TB:(qblk + 1) * TB], psb[ob][0:64, :], rinv[ri][0:64, :], ALU.mult,
                           reads=[s_psb[ob], s_rinv[ri]], writes=[s_attn[hh // 2]])
                    else:
                        RECIP(rinv[ri][64:128, :], psb[ob][0:64, :], reads=[s_psb[ob]], writes=[s_rinv[ri]])
                        TT("dve", attn[64:128, hh // 2, qblk * TB:(qblk + 1) * TB], psb[ob][64:128, :], rinv[ri][64:128, :], ALU.mult,
                           reads=[s_psb[ob], s_rinv[ri]], writes=[s_attn[hh // 2]])
                    if n_layers > 1 and prep_done[0] < NF and blk % 2 == 0:
                        wgu_prep(1, stg, s_stg, [prep_done[0]])
                        prep_done[0] += 1
                    if odd and qblk == NTB - 1:
                        STORE("sp", cat_s[:, hh // 2, :], attn[:, hh // 2, :], "ast", reads=[s_attn[hh // 2]])

            for n in range(len(items) + LA):
                if n < len(items):
                    emitA(n)
                if n - LA >= 0:
                    emitC(n - LA)
            R.barrier()
            R.replay()
        if stop == "L0P2":
            return nc

        post_phase(0, woutA_in, final=(n_layers == 1))
        if stop == "L0P3":
            return nc

        if n_layers > 1:
            SC1 = 64.0 ** -0.5
            with contextlib.ExitStack() as st:
                sb = lambda name, shape, dt: st.enter_context(nc.sbuf_tensor(_un(name), shape, dt))
                wqkv = sb("wqkv", [P, NCH, 3 * D], BF16)
                s_w = Slot()
                for j in range(3):
                    R.dma("pool", wqkv[:, :, j * D:(j + 1) * D], wqkv_in[:, j * D:(j + 1) * D].rearrange("(kc p) n -> p kc n", p=P),
                          "w0", writes=[s_w], append_w=(j > 0))
                xb = [sb(f"xb{i}", [P, NCH, TB], F32) for i in range(2)]
                s_xb = [Slot() for _ in range(2)]
                tabC = [sb(f"tabC{i}", [P, TB], F32) for i in range(2)]
                tabS = [sb(f"tabS{i}", [P, TB], F32) for i in range(2)]
                s_tab = [Slot() for _ in range(2)]
                sq = sb("sq", [P, NCH, TB], BF16)
                s_sq = [Slot() for _ in range(NCH)]
                rstd = sb("rstd", [P, TB], F32)
                s_rstd = Slot()
                tmp = [sb(f"tmp{i}", [P, TB], F32) for i in range(2)]
                s_tmp = [Slot() for _ in range(2)]
                h = sb("h", [P, NCH, TB], BF16)
                s_h = [Slot() for _ in range(NCH)]
                qb = [sb(f"qb{i}", [P, TB], BF16) for i in range(2)]
                s_qb = [Slot() for _ in range(2)]
                t1 = [sb(f"t1{i}", [P, TB], F32) for i in range(2)]
                s_t1 = [Slot() for _ in range(2)]
                t2 = [sb(f"t2{i}", [P, TB], F32) for i in range(2)]
                s_t2 = [Slot() for _ in range(2)]
                qk = [sb(f"qk{i}", [P, 16, TB], BF16) for i in range(2)]
                s_qkT = [[Slot() for c in range(16)] for _ in range(2)]
                vt = [sb(f"vt{i}", [P, 4, D], BF16) for i in range(2)]
                s_vt = [[Slot() for c in range(8)] for _ in range(2)]
                PB = Banks(range(8))
                for tb in range(NTB):
                    b = tb % 2
                    LOAD("sp", xb[b][:], xT_s[:, :, tb * TB:(tb + 1) * TB], f"xb{b}", [s_xb[b]])
                    R.dma("sp", tabC[b][:], tab_s[2, :, tb * TB:(tb + 1) * TB], f"tab{b}", writes=[s_tab[b]])
                    R.dma("sp", tabS[b][:], tab_s[3, :, tb * TB:(tb + 1) * TB], f"tab{b}", writes=[s_tab[b]], append_w=True)
                    norm1_block(1, xb[b], s_xb[b], sq, s_sq, rstd, s_rstd, tmp, s_tmp, h, s_h, PB)
                    bank_q = {}

                    def st1(c):
                        k = PB.next()
                        bank_q[c] = k
                        i2 = c % 2
                        MM(psb[k][:], [(wqkv[:, kc, c * P:(c + 1) * P], h[:, kc, :]) for kc in range(NCH)], reads=s_h + [s_w], writes=[s_psb[k]])
                        CP("act", qb[i2][:], psb[k][:], reads=[s_psb[k]], writes=[s_qb[i2]])

                    def st2(c):
                        k = bank_q[c]
                        i2 = c % 2
                        k2 = PB.next()
                        MM(psb[k2][:], [(rmat_bf[:, 1, :], qb[i2][:])], reads=[s_qb[i2], s_c], writes=[s_psb[k2]])
                        TT("dve", t1[i2][:], psb[k][:], tabC[b][:], ALU.mult, reads=[s_psb[k], s_tab[b]], writes=[s_t1[i2]])
                        TT("dve", t2[i2][:], psb[k2][:], tabS[b][:], ALU.mult, reads=[s_psb[k2], s_tab[b]], writes=[s_t2[i2]])
                        TT("pool", qk[b][:, c, :], t1[i2][:], t2[i2][:], ALU.add, reads=[s_t1[i2], s_t2[i2]], writes=[s_qkT[b][c]])

                    st1(0)
                    for c in range(16):
                        if c + 1 < 16:
                            st1(c + 1)
                        st2(c)
                    STORE("sp", q_s[:, :, tb * TB:(tb + 1) * TB], qk[b][:, 0:8, :], f"qst{b}", reads=s_qkT[b][0:8])
                    STORE("sp", k_s[:, :, tb * TB:(tb + 1) * TB], qk[b][:, 8:16, :], f"kst{b}", reads=s_qkT[b][8:16])
                    for tt in range(4):
                        for half in range(2):
                            k = PB.next()
                            MM(psb[k][:], [(h[:, kc, tt * P:(tt + 1) * P], wqkv[:, kc, 2 * D + half * TB:2 * D + (half + 1) * TB])
                                            for kc in range(NCH)], reads=s_h + [s_w], writes=[s_psb[k]])
                            CP("act" if half == 0 else "dve", vt[b][:, tt, half * TB:(half + 1) * TB], psb[k][:], reads=[s_psb[k]],
                               writes=[s_vt[b][tt * 2 + half]])
                    STORE("sp", v_s[:, tb * 4 * D:(tb + 1) * 4 * D], vt[b][:].rearrange("p t d -> p (t d)"), f"vst{b}", reads=s_vt[b])
                R.barrier()
                R.replay()
            if stop == "L1P1":
                return nc

            with contextlib.ExitStack() as st:
                sb = lambda name, shape, dt: st.enter_context(nc.sbuf_tensor(_un(name), shape, dt))
                va = sb("va", [P, 32, D], BF16)
                s_va = Slot()
                LOAD("sp", va[:].rearrange("p t d -> p (t d)"), v_s[:, :], "va", [s_va])
                qh = [sb(f"qh{i}", [P, S], BF16) for i in range(2)]
                kh = [sb(f"kh{i}", [P, S], BF16) for i in range(2)]
                s_qk = [Slot() for _ in range(2)]
                rinv = [sb(f"rinv{i}", [P, TB], F32) for i in range(2)]
                s_rinv = [Slot() for _ in range(2)]
                a0 = sb("a0", [P, TB], F32)
                a1 = sb("a1", [P, TB], F32)
                s_a0, s_a1 = Slot(), Slot()
                sqa = sb("sqa", [P, 1, TB], BF16)
                s_sqa = [Slot()]
                rsa = sb("rsa", [P, TB], F32)
                s_rsa = Slot()
                attn = [sb(f"attn{i}", [P, S], BF16) for i in range(2)]
                s_attn = [Slot() for _ in range(2)]
                NPT = 10
                ptp = [sb(f"pty{i}", [P, 2, TB], BF16) for i in range(NPT)]
                s_pt = [Slot() for _ in range(NPT)]
                tri2 = sb("tri2", [P, 2, P], BF16)
                s_tri2 = Slot()
                CP("pool", tri2[:, 0, :], tri_bf[:], reads=[s_c], writes=[s_tri2])
                CP("pool", tri2[:, 1, :], tri_bf[:], reads=[s_c, s_tri2], writes=[s_tri2])
                ones_f = sb("ones_f", [P, P], F32)
                s_onesf = Slot()
                R.op("pool", lambda e: e.memset(ones_f[:], 1.0), writes=[s_onesf])
                EAt = [sb(f"EA{b}", [P, 3, TB], F32) for b in range(2)]
                EA = [[EAt[b][:, k, :] for k in range(3)] for b in range(2)]
                s_EA = [[Slot() for k in range(3)] for b in range(2)]
                rv = [sb(f"rv{i}", [P, TB], F32) for i in range(2)]
                s_rv = [Slot() for _ in range(2)]
                PAIRS = [0, 2]
                LA = 2
                items = []
                for hh in range(8):
                    for qblk in range(NTB):
                        nkt = 4 * qblk + 4
                        for kt in range(nkt):
                            items.append((hh, qblk, kt, nkt))
                        items.append((hh, qblk, -1, nkt))
                        items.append((hh, qblk, -2, nkt))

                def load_head(hh):
                    i = hh % 2
                    LOAD("sp", qh[i][:], q_s[:, hh, :], f"qh{i}", [s_qk[i]])
                    R.dma("sp", kh[i][:], k_s[:, hh, :], f"qh{i}", writes=[s_qk[i]], append_w=True)

                def emitA(n):
                    hh, qblk, kt, nkt = items[n]
                    if kt < 0:
                        return
                    i = hh % 2
                    blk = hh * NTB + qblk
                    bb = blk % 2
                    if qblk == 0 and kt == 0:
                        if hh == 0:
                            load_head(0)
                        if hh + 1 < 8:
                            load_head(hh + 1)
                    j = kt - 4 * qblk
                    c0 = max(j, 0) * P
                    p0 = PAIRS[n % 2]
                    fns = [lambda e: e.matmul(psb[p0][:, c0:TB], lhsT=kh[i][0:64, kt * P:(kt + 1) * P],
                                              rhs=qh[i][0:64, qblk * TB + c0:(qblk + 1) * TB], start=True, stop=(j < 0)),
                           lambda e: e.matmul(psb[p0 + 1][:, c0:TB], lhsT=kh[i][64:128, kt * P:(kt + 1) * P],
                                              rhs=qh[i][64:128, qblk * TB + c0:(qblk + 1) * TB], start=True, stop=(j < 0))]
                    if j >= 0:
                        fns.append(lambda e: e.matmul(psb[p0][:, c0:c0 + P], lhsT=ident_bf[:], rhs=maskb_bf[:], start=False, stop=True))
                        fns.append(lambda e: e.matmul(psb[p0 + 1][:, c0:c0 + P], lhsT=ident_bf[:], rhs=maskb_bf[:], start=False, stop=True))
                    R.group("pe", fns, reads=[s_qk[i], s_c], writes=[s_psb[p0], s_psb[p0 + 1]])
                    p_i = n % NPT
                    ACT(ptp[p_i][:, :, c0:TB], psall[:, p0:p0 + 2, c0:TB], AF.Exp, reads=[s_psb[p0], s_psb[p0 + 1]],
                        writes=[s_pt[p_i]], scale=SC1)
                    if kt == 0:
                        CP("dve", EA[bb][0][:, c0:TB], ptp[p_i][:, 0, c0:TB], reads=[s_pt[p_i]], writes=[s_EA[bb][0]])
                    else:
                        TT("dve", EA[bb][0][:, c0:TB], EA[bb][0][:, c0:TB], ptp[p_i][:, 0, c0:TB], ALU.add,
                           reads=[s_pt[p_i]], writes=[s_EA[bb][0]])

                def emitC(n):
                    hh, qblk, kt, nkt = items[n]
                    i = hh % 2
                    blk = hh * NTB + qblk
                    bb = blk % 2
                    o0, o1, s0b, s1b = 4, 5, 6, 7
                    p0 = PAIRS[n % 2]
                    if kt >= 0:
                        j = kt - 4 * qblk
                        c0 = max(j, 0) * P
                        p_i = n % NPT
                        R.group("pe", [lambda e: e.matmul(psb[o0][:, c0:TB], lhsT=va[:, kt, hh * P:(hh + 1) * P], rhs=ptp[p_i][:, 0, c0:TB],
                                                          start=(kt == 0), stop=(kt == nkt - 1)),
                                       lambda e: e.matmul(psb[o1][:, c0:TB], lhsT=va[:, kt, hh * P:(hh + 1) * P], rhs=ptp[p_i][:, 1, c0:TB],
                                                          start=(kt == 0), stop=(kt == nkt - 1)),
                                       lambda e: e.matmul(psb[s1b][:, c0:TB], lhsT=ones_bf[:], rhs=ptp[p_i][:, 1, c0:TB],
                                                          start=(kt == 0), stop=(kt == nkt - 1))],
                                reads=[s_pt[p_i], s_va, s_ones], writes=[s_psb[o0], s_psb[o1], s_psb[s1b]])
                    elif kt == -1:
                        MM(psb[s0b][:], [(ones_f[:], EA[bb][0][:])], reads=[s_EA[bb][0], s_onesf], writes=[s_psb[s0b]])
                        for c, sbk in ((0, s0b), (1, s1b)):
                            ACT(rv[c][:], psb[sbk][:], AF.Ln, reads=[s_psb[sbk]], writes=[s_rv[c]])
                            ACT(rv[c][:], rv[c][:], AF.Exp, reads=[s_rv[c]], writes=[s_rv[c]], scale=-1.0)
                        TT("dve", a0[:], psb[o0][:], rv[0][:], ALU.mult, reads=[s_psb[o0], s_rv[0]], writes=[s_a0])
                        TT("dve", a1[:], psb[o1][:], rv[1][:], ALU.mult, reads=[s_psb[o1], s_rv[1]], writes=[s_a1])
                        STT(a0[:], a1[:], lam_neg[:, 0:1], a0[:], ALU.mult, ALU.add, reads=[s_a1, s_a0, s_lam], writes=[s_a0])
                        TT("pool", sqa[:, 0, :], a0[:], a0[:], ALU.mult, reads=[s_a0], writes=[s_sqa[0]])
                    else:
                        MM(psb[p0][:], [(ones_bf[:], sqa[:, 0, :])], reads=[s_sqa[0], s_ones], writes=[s_psb[p0]])
                        ACT(rsa[:], psb[p0][:], AF.Ln, reads=[s_psb[p0], s_ones], writes=[s_rsa], scale=1.0 / 128, bias=epsc[:])
                        ACT(rsa[:], rsa[:], AF.Exp, reads=[s_rsa], writes=[s_rsa], scale=-0.5)
                        STT(attn[i][:, qblk * TB:(qblk + 1) * TB], a0[:], subg2[:, 0:1], rsa[:], ALU.mult, ALU.mult,
                            reads=[s_a0, s_rsa, s_lam], writes=[s_attn[i]])
                        if qblk == NTB - 1:
                            STORE("sp", cat_s[:, hh, :], attn[i][:], f"ast{i}", reads=[s_attn[i]])

                for n in range(len(items) + LA):
                    if n < len(items):
                        emitA(n)
                    if n - LA >= 0:
                        emitC(n - LA)
                R.barrier()
                R.replay()
            if stop == "L1P2":
                return nc

            post_phase(1, woutD_in, final=True)

        gst.callback(lambda: None)
    return nc


def _consts():
    ident = np.eye(P, dtype=np.float32)
    tri = (np.arange(P)[None, :] >= np.arange(P)[:, None]).astype(np.float32)
    rm = np.zeros((2, P, P), np.float32)
    for i in range(16):
        rm[0, 80 + i, 64 + i] = -1.0
        rm[0, 64 + i, 80 + i] = 1.0
    for c in range(2):
        for i in range(8):
            rm[1, c * 64 + 8 + i, c * 64 + i] = -1.0
            rm[1, c * 64 + i, c * 64 + 8 + i] = 1.0
    theta = np.float32(500000.0)
    inv32 = (theta ** (-(np.arange(0, 32, 2, dtype=np.float32) / np.float32(32)))).astype(np.float32)
    inv16 = (theta ** (-(np.arange(0, 16, 2, dtype=np.float32) / np.float32(16)))).astype(np.float32)
    invrows = np.zeros((P, 2), np.float32)
    for i in range(16):
        invrows[64 + i, 0] = inv32[i]
        invrows[80 + i, 0] = inv32[i]
    for c in range(2):
        for i in range(8):
            invrows[c * 64 + i, 1] = inv16[i]
            invrows[c * 64 + 8 + i, 1] = inv16[i]
    cnt = np.zeros((P, 4, 16), np.float32)
    for g, w in enumerate((2, 4, 8, 16)):
        cnt[:, g, :] = (1.0 / np.minimum(np.arange(1, 17), w)).astype(np.float32)[None, :]
    maskb = np.where(np.arange(P)[None, :] >= np.arange(P)[:, None], 0.0, -30000.0).astype(np.float32)
    return ident, tri, rm, invrows, cnt, maskb


def _cols(v, n):
    return np.ascontiguousarray(np.asarray(v, np.float32).reshape(n, P).T)


_NC_CACHE = {}


def kernel(**inputs):
    g = lambda k: np.asarray(inputs[k])
    x = g("x").astype(np.float32)
    ident, tri, rm, invrows, cnt, maskb = _consts()
    if "nc" not in _NC_CACHE:
        _NC_CACHE["nc"] = build_program()
    nc = _NC_CACHE["nc"]
    shared = {
        "ada_w": np.ascontiguousarray(g("ada_w"), dtype=np.float32),
        "adab": np.ascontiguousarray(g("ada_b").astype(np.float32).reshape(2, 48, P).transpose(0, 2, 1)),
        "n1g": np.ascontiguousarray(g("norm1_g").astype(np.float32).reshape(2, NCH, P).transpose(0, 2, 1)),
        "n2g": np.ascontiguousarray(g("norm2_g").astype(np.float32).reshape(2, NCH, P).transpose(0, 2, 1)),
        "fng": _cols(g("final_norm_g"), NCH),
        "w_gu": np.ascontiguousarray(g("ffn_w_gate_up"), dtype=np.float32),
        "w_d": np.ascontiguousarray(g("ffn_w_down"), dtype=np.float32),
        "w_in": np.ascontiguousarray(g("mla_w_in")[0], dtype=np.float32),
        "qng": _cols(g("mla_q_norm_g")[0], 3),
        "kvng": _cols(g("mla_kv_norm_g")[0], 2),
        "w_uq": np.ascontiguousarray(g("mla_w_uq")[0], dtype=np.float32),
        "w_ukv": np.ascontiguousarray(g("mla_w_ukv")[0], dtype=np.float32),
        "pool_w": np.ascontiguousarray(g("pool_w")[0], dtype=np.float32),
        "pool_b": _cols(g("pool_b")[0].reshape(-1), 4),
        "pool_s": _cols(g("pool_scale")[0], 4),
        "w_outA": np.ascontiguousarray(g("mix_a_w_out")[0], dtype=np.float32),
        "w_qkv": np.ascontiguousarray(g("diff_w_qkv")[0], dtype=np.float32),
        "lamv": np.ascontiguousarray(np.broadcast_to(np.stack([
            g("diff_lambda_q1")[0], g("diff_lambda_k1")[0], g("diff_lambda_q2")[0], g("diff_lambda_k2")[0]]).astype(np.float32)[None],
            (P, 4, 64))),
        "subg": np.ascontiguousarray(g("diff_subln_g")[0].astype(np.float32).reshape(P, 1)),
        "w_outD": np.ascontiguousarray(g("diff_w_out")[0], dtype=np.float32),
        "ident_f": ident, "tri": tri, "maskb": maskb, "rmat": rm, "invrows": invrows, "cntinv": cnt,
    }
    c = g("c").astype(np.float32)
    pos = g("positions").astype(np.int32)
    in_maps = []
    ncores = int(_NC_CACHE.get("ncores", 8))
    for b in range(ncores):
        m = dict(shared)
        m["x"] = np.ascontiguousarray(x[b])
        m["cT"] = _cols(c[b], NCH)
        m["pos"] = np.ascontiguousarray(pos[b][None, :])
        in_maps.append(m)
    res = run_bass_kernel_spmd(nc, in_maps, core_ids=list(range(ncores)))
    return np.stack([np.asarray(r["out"], dtype=np.float32) for r in res.results], axis=0)
```

```python
import contextlib
import math
import numpy as np
import ml_dtypes
import concourse.bass as bass
import concourse.mybir as mybir
from concourse.bass_utils import run_bass_kernel_spmd

F32 = mybir.dt.float32
BF16 = mybir.dt.bfloat16
I32 = mybir.dt.int32
AF = mybir.ActivationFunctionType
ALU = mybir.AluOpType
AX = mybir.AxisListType

P = 128
S = 4096
D = 1024
NCH = D // P
TB = 512
NTB = S // TB
EPS = 1e-6


class Sem:
    def __init__(self, nc, stack, name):
        self.h = stack.enter_context(nc.semaphore(name))
        self.val = 0
        self.name = name


class Slot:
    __slots__ = ("w", "r", "name", "excl")

    def __init__(self, name="", excl=False):
        self.w = []
        self.r = []
        self.name = name
        self.excl = excl


class Queue:
    def __init__(self, name, sem):
        self.name = name
        self.sem = sem
        self.ops = []
        self.seen = {}


class Rec:
    def __init__(self, nc, stack):
        self.nc = nc
        self.stack = stack
        self.q = {}
        for n in ("pe", "act", "dve", "pool", "sp"):
            self.q[n] = Queue(n, Sem(nc, stack, "q_" + n))
        self.dma_sems = {}
        self.all_dma_toks = []

    def dsem(self, name):
        if name not in self.dma_sems:
            self.dma_sems[name] = Sem(self.nc, self.stack, "d_" + name)
        return self.dma_sems[name]

    def _deps(self, reads, writes):
        deps = []
        for s in reads:
            deps += s.w
            if s.excl:
                deps += s.r
        for s in writes:
            deps += s.w
            deps += s.r
        return deps

    def _prune(self, q, deps):
        best = {}
        for (sem, v) in deps:
            if q.name == "pe" and sem is q.sem:
                continue
            if v > q.seen.get(sem, 0) and v > best.get(sem, (None, 0))[1]:
                best[sem] = (sem, v)
        out = list(best.values())
        for (sem, v) in out:
            q.seen[sem] = v
        return out

    def op(self, qn, fn, reads=(), writes=(), extra=(), signal=True):
        q = self.q[qn]
        deps = self._deps(reads, writes) + list(extra)
        waits = self._prune(q, deps)
        tok = None
        if signal:
            q.sem.val += 1
            tok = (q.sem, q.sem.val)
            q.ops.append((waits, fn, q.sem, 1))
            for s in reads:
                s.r.append(tok)
            for s in writes:
                s.w = [tok]
                s.r = []
        else:
            q.ops.append((waits, fn, None, 0))
        return tok

    def group(self, qn, fns, reads=(), writes=(), extra=()):
        q = self.q[qn]
        deps = self._deps(reads, writes) + list(extra)
        waits = self._prune(q, deps)
        q.sem.val += 1
        tok = (q.sem, q.sem.val)
        n = len(fns)
        for i, fn in enumerate(fns):
            q.ops.append((waits if i == 0 else [], fn,
                          q.sem if i == n - 1 else None, 1))
        for s in reads:
            s.r.append(tok)
        for s in writes:
            s.w = [tok]
            s.r = []
        return tok

    def dma(self, qn, out, in_, semname, reads=(), writes=(), extra=(), append_w=False):
        q = self.q[qn]
        sem = self.dsem(semname)
        deps = self._deps(reads, [] if append_w else writes) + list(extra)
        waits = self._prune(q, deps)
        sem.val += 16
        tok = (sem, sem.val)

        def fn(eng, out=out, in_=in_):
            return eng.dma_start(out=out, in_=in_)
        q.ops.append((waits, fn, sem, 16))
        for s in reads:
            s.r.append(tok)
        for s in writes:
            if append_w:
                s.w = s.w + [tok]
            else:
                s.w = [tok]
                s.r = []
        self.all_dma_toks.append(tok)
        return tok

    def barrier(self):
        toks = [(q.sem, q.sem.val) for q in self.q.values() if q.sem.val > 0]
        toks += [(s, s.val) for s in self.dma_sems.values() if s.val > 0]
        for qn, q in self.q.items():
            waits = self._prune(q, toks)
            if waits:
                q.ops.append((waits, None, None, 0))

    def replay(self):
        nc = self.nc
        with nc.Block() as block:
            def run(q):
                def body(eng):
                    for (waits, fn, isem, amt) in q.ops:
                        for (sem, v) in waits:
                            eng.wait_ge(sem.h, v)
                        if fn is None:
                            continue
                        ins = fn(eng)
                        if isem is not None:
                            ins.then_inc(isem.h, amt)
                return body
            block.tensor(run(self.q["pe"]))
            block.scalar(run(self.q["act"]))
            block.vector(run(self.q["dve"]))
            block.gpsimd(run(self.q["pool"]))
            block.sync(run(self.q["sp"]))
        for q in self.q.values():
            q.ops = []


HQ = 8
DFF = 2816
NF = DFF // P
MAGIC = 12582912.0
C1 = 6.28125
C2 = 2.0 * math.pi - 6.28125
INV2PI = 1.0 / (2.0 * math.pi)
LAM_INIT1 = 0.8 - 0.6 * math.exp(-0.3 * 1)
DBG = {}


def build_program(n_layers=2, stop=None):
    nc = bass.Bass("TRN2", target_bir_lowering=False)

    def din(name, shape, dt=F32):
        return nc.dram_tensor(name, shape, dt, kind="ExternalInput").ap()

    def dscr(name, shape, dt):
        return nc.dram_tensor(name, shape, dt, kind="Internal").ap()

    x_in = din("x", [S, D])
    cT_in = din("cT", [P, NCH])
    pos_in = din("pos", [1, S], I32)
    ada_w = din("ada_w", [2, D, 6 * D])
    adab_in = din("adab", [2, P, 48])
    n1g_in = din("n1g", [2, P, NCH])
    n2g_in = din("n2g", [2, P, NCH])
    fng_in = din("fng", [P, NCH])
    wgu_in = din("w_gu", [2, D, 2 * DFF])
    wd_in = din("w_d", [2, DFF, D])
    win_in = din("w_in", [D, 1184])
    qng_in = din("qng", [P, 3])
    kvng_in = din("kvng", [P, 2])
    wuq_in = din("w_uq", [384, 768])
    wukv_in = din("w_ukv", [256, 1024])
    poolw_in = din("pool_w", [4, P, P])
    poolb_in = din("pool_b", [P, 4])
    pools_in = din("pool_s", [P, 4])
    woutA_in = din("w_outA", [D, D])
    wqkv_in = din("w_qkv", [D, 3 * D])
    lamv_in = din("lamv", [P, 4, 64])
    subg_in = din("subg", [P, 1])
    woutD_in = din("w_outD", [D, D])
    identf_in = din("ident_f", [P, P])
    tri_in = din("tri", [P, P])
    maskb_in = din("maskb", [P, P])
    rmat_in = din("rmat", [2, P, P])
    invrows_in = din("invrows", [P, 2])
    cntinv_in = din("cntinv", [P, 4, 16])
    out = nc.dram_tensor("out", [S, D], F32, kind="ExternalOutput").ap()

    xT_s = dscr("xT_s", [P, NCH, S], F32)
    tab_s = dscr("tab_s", [4, P, S], F32)
    wgu_s = dscr("wgu_s", [2, NF, P, NCH * 256], BF16)
    q_s = dscr("q_s", [P, 8, S], BF16)
    k_s = dscr("k_s", [P, 8, S], BF16)
    v_s = dscr("v_s", [P, 32 * 1024], BF16)
    cat_s = dscr("cat_s", [P, NCH, S], BF16)

    with contextlib.ExitStack() as gst:
        R = Rec(nc, gst)
        _uid = [0]

        def _un(name):
            _uid[0] += 1
            return f"sb{_uid[0]}_{name}"
        gsb = lambda name, shape, dt: gst.enter_context(nc.sbuf_tensor(_un(name), shape, dt))
        psall = gst.enter_context(nc.psum_tensor("psall", [P, 8, TB], F32))
        psb = [psall[:, i, :] for i in range(8)]
        s_psb = [Slot(f"psb{i}", excl=True) for i in range(8)]

        class Banks:
            def __init__(self, ids):
                self.ids = list(ids)
                self.i = 0

            def next(self):
                b = self.ids[self.i % len(self.ids)]
                self.i += 1
                return b

        def ACT(out_, in_, func, reads, writes, **kw):
            return R.op("act", lambda e: e.activation(out=out_, in_=in_, func=func, **kw), reads=reads, writes=writes)

        def TT(q, out_, in0, in1, op, reads, writes):
            return R.op(q, lambda e: e.tensor_tensor(out=out_, in0=in0, in1=in1, op=op), reads=reads, writes=writes)

        def TS(q, out_, in0, s1, s2, op0, op1, reads, writes):
            if op1 is None:
                return R.op(q, lambda e: e.tensor_single_scalar(out=out_, in_=in0, scalar=s1, op=op0), reads=reads, writes=writes)
            return R.op(q, lambda e: e.tensor_scalar(out=out_, in0=in0, scalar1=s1, scalar2=s2, op0=op0, op1=op1),
                        reads=reads, writes=writes)

        def STT(out_, in0, scalar, in1, op0, op1, reads, writes):
            return R.op("dve", lambda e: e.scalar_tensor_tensor(out=out_, in0=in0, scalar=scalar, in1=in1, op0=op0, op1=op1),
                        reads=reads, writes=writes)

        def CP(q, out_, in_, reads, writes):
            if q == "act":
                return R.op("act", lambda e: e.copy(out=out_, in_=in_), reads=reads, writes=writes)
            return R.op(q, lambda e: e.tensor_copy(out=out_, in_=in_), reads=reads, writes=writes)

        def MM(out_, pairs, reads, writes):
            n = len(pairs)
            fns = []
            for i, (l, r) in enumerate(pairs):
                fns.append(lambda e, l=l, r=r, i=i: e.matmul(out_, lhsT=l, rhs=r, start=(i == 0), stop=(i == n - 1)))
            return R.group("pe", fns, reads=reads, writes=writes)

        def RECIP(out_, in_, reads, writes):
            return R.op("dve", lambda e: e.reciprocal(out=out_, in_=in_), reads=reads, writes=writes)

        def LOAD(q, out_, in_, sem, writes, reads=()):
            return R.dma(q, out_, in_, sem, reads=reads, writes=writes)

        def STORE(q, out_, in_, sem, reads):
            return R.dma(q, out_, in_, sem, reads=reads, writes=[Slot()])

        identf = gsb("identf", [P, P], F32)
        ones_bf = gsb("ones_bf", [P, P], BF16)
        tri_bf = gsb("tri_bf", [P, P], BF16)
        rmat_bf = gsb("rmat_bf", [P, 2, P], BF16)
        maskb_bf = gsb("maskb_bf", [P, P], BF16)
        ident_bf = gsb("ident_bf", [P, P], BF16)
        fng = gsb("fng", [P, NCH], F32)
        AB = gsb("AB", [P, 2, 6, NCH], F32)
        halfpi = gsb("halfpi", [P, 1], F32)
        epsc = gsb("epsc", [P, 1], F32)
        lam_neg = gsb("lam_neg", [P, 1], F32)
        subg2 = gsb("subg2", [P, 1], F32)
        s_c = Slot("consts")
        s_ones = Slot("ones")
        s_AB = Slot("AB")
        s_lam = Slot("lam")
        LOAD("sp", identf[:], identf_in, "c0", [s_c])
        R.dma("sp", fng[:], fng_in, "c0", writes=[s_c], append_w=True)
        R.dma("pool", tri_bf[:], tri_in, "c1", writes=[s_c], append_w=True)
        R.dma("pool", maskb_bf[:], maskb_in, "c1", writes=[s_c], append_w=True)
        R.dma("pool", ident_bf[:], identf_in, "c1", writes=[s_c], append_w=True)
        R.dma("pool", rmat_bf[:], rmat_in.rearrange("j p m -> p j m"), "c1", writes=[s_c], append_w=True)
        R.op("pool", lambda e: e.memset(ones_bf[:], 1.0), writes=[s_ones])
        R.op("pool", lambda e: e.memset(halfpi[:], math.pi / 2.0), writes=[s_ones])
        s_ones.w = [(R.q["pool"].sem, R.q["pool"].sem.val)]
        R.op("pool", lambda e: e.memset(epsc[:], EPS), writes=[Slot()])
        s_ones.w = [(R.q["pool"].sem, R.q["pool"].sem.val)]

        def rms_rstd(sq_tile, s_sq, src_chunks, s_src, n, dim, bank, rstd_tile, s_rstd, np_=P, sq_eng="act"):
            for c in range(n):
                if sq_eng == "act" or c % 2 == 0:
                    ACT(sq_tile[:np_, c, :], src_chunks[c], AF.Square, reads=s_src[c], writes=[s_sq[c]])
                else:
                    TT("pool", sq_tile[:np_, c, :], src_chunks[c], src_chunks[c], ALU.mult, reads=s_src[c], writes=[s_sq[c]])
            MM(psb[bank][:np_, :], [(ones_bf[:np_, :np_], sq_tile[:np_, c, :]) for c in range(n)],
               reads=s_sq[:n] + [s_ones], writes=[s_psb[bank]])
            ACT(rstd_tile[:np_, :], psb[bank][:np_, :], AF.Ln, reads=[s_psb[bank], s_ones], writes=[s_rstd],
                scale=1.0 / dim, bias=epsc[:np_, :])
            ACT(rstd_tile[:np_, :], rstd_tile[:np_, :], AF.Exp, reads=[s_rstd], writes=[s_rstd], scale=-0.5)

        st = contextlib.ExitStack()
        if True:
            sb = lambda name, shape, dt: st.enter_context(nc.sbuf_tensor(_un(name), shape, dt))
            cT = sb("cT", [P, NCH], F32)
            cond = sb("cond", [P, NCH], F32)
            adab = sb("adab", [P, 2, 48], F32)
            n1g = sb("n1g", [P, 2, NCH], F32)
            n2g = sb("n2g", [P, 2, NCH], F32)
            modt = sb("modt", [P, 2, 48], F32)
            s_in = Slot()
            s_cond = Slot()
            s_modt = Slot()
            LOAD("sp", cT[:], cT_in, "a0", [s_in])
            R.dma("sp", adab[:], adab_in.rearrange("l p m -> p l m"), "a0", writes=[s_in], append_w=True)
            R.dma("sp", n1g[:], n1g_in.rearrange("l p m -> p l m"), "a0", writes=[s_in], append_w=True)
            R.dma("sp", n2g[:], n2g_in.rearrange("l p m -> p l m"), "a0", writes=[s_in], append_w=True)
            ACT(cond[:], cT[:], AF.Silu, reads=[s_in], writes=[s_cond])
            aw = [sb(f"aw{i}", [P, NCH, 512], F32) for i in range(2)]
            s_aw = [Slot() for _ in range(2)]

            modrow = sb("modrow", [1, 6 * D], F32)
            s_mrow = [Slot() for _ in range(12)]

            def adaln_step(l, mg):
                i = (l * 12 + mg) % 2
                LOAD("sp", aw[i][:], ada_w[l, :, mg * 512:(mg + 1) * 512].rearrange("(kc p) n -> p kc n", p=P),
                     f"aw{i}", [s_aw[i]])
                bk = mg % 2
                MM(psb[bk][0:1, :], [(cond[:, kc:kc + 1], aw[i][:, kc, :]) for kc in range(NCH)],
                   reads=[s_aw[i], s_cond], writes=[s_psb[bk]])
                CP("act", modrow[0:1, mg * 512:(mg + 1) * 512], psb[bk][0:1, :], reads=[s_psb[bk]], writes=[s_mrow[mg]])

            def adaln_finish(l):
                fns = [lambda e, m=m: e.matmul(psb[l][:, m:m + 1], lhsT=modrow[0:1, m * P:(m + 1) * P], rhs=identf[0:1, 0:1],
                                               start=True, stop=True) for m in range(48)]
                R.group("pe", fns, reads=s_mrow + [s_c], writes=[s_psb[l]])
                TT("dve", modt[:, l, :], psb[l][:, 0:48], adab[:, l, :], ALU.add, reads=[s_psb[l], s_in], writes=[s_modt])
                STT(AB[:, l, 0, :], modt[:, l, 8:16], 1.0, n1g[:, l, :], ALU.add, ALU.mult, reads=[s_modt, s_in], writes=[s_AB])
                CP("dve", AB[:, l, 1, :], modt[:, l, 0:8], reads=[s_modt], writes=[s_AB])
                CP("dve", AB[:, l, 2, :], modt[:, l, 16:24], reads=[s_modt], writes=[s_AB])
                STT(AB[:, l, 3, :], modt[:, l, 32:40], 1.0, n2g[:, l, :], ALU.add, ALU.mult, reads=[s_modt, s_in], writes=[s_AB])
                CP("dve", AB[:, l, 4, :], modt[:, l, 24:32], reads=[s_modt], writes=[s_AB])
                CP("dve", AB[:, l, 5, :], modt[:, l, 40:48], reads=[s_modt], writes=[s_AB])

            ada_steps = []
            for l in range(n_layers):
                for mg in range(12):
                    ada_steps.append(lambda l=l, mg=mg: adaln_step(l, mg))
                ada_steps.append(lambda l=l: adaln_finish(l))
            lamv = sb("lamv", [P, 4, 64], F32)
            lprod = sb("lprod", [P, 2, 64], F32)
            lsum = sb("lsum", [P, 2], F32)
            subg = sb("subg", [P, 1], F32)
            s_lv = Slot()
            s_lp = Slot()
            LOAD("sp", lamv[:], lamv_in, "a1", [s_lv])
            R.dma("sp", subg[:], subg_in, "a1", writes=[s_lv], append_w=True)
            TT("dve", lprod[:, 0, :], lamv[:, 0, :], lamv[:, 1, :], ALU.mult, reads=[s_lv], writes=[s_lp])
            TT("dve", lprod[:, 1, :], lamv[:, 2, :], lamv[:, 3, :], ALU.mult, reads=[s_lv], writes=[s_lp])
            R.op("dve", lambda e: e.reduce_sum(out=lsum[:], in_=lprod[:], axis=AX.X), reads=[s_lp], writes=[s_lp])
            ACT(lsum[:], lsum[:], AF.Exp, reads=[s_lp], writes=[s_lp])
            TT("dve", lam_neg[:], lsum[:, 1:2], lsum[:, 0:1], ALU.subtract, reads=[s_lp], writes=[s_lam])
            TS("dve", lam_neg[:], lam_neg[:], -LAM_INIT1, None, ALU.add, None, reads=[s_lam], writes=[s_lam])
            TS("dve", subg2[:], subg[:], 1.0 - LAM_INIT1, None, ALU.mult, None, reads=[s_lv], writes=[s_lam])

            HW = S // 2
            posi = sb("posi", [P, HW], I32)
            posf = sb("posf", [P, HW], F32)
            invrows = sb("invrows", [P, 2], F32)
            ang = sb("ang", [P, HW], F32)
            kk = sb("kk", [P, HW], F32)
            rr = sb("rr", [P, HW], F32)
            tS = sb("tS", [P, HW], F32)
            tC = sb("tC", [P, HW], F32)
            s_pos, s_ang, s_kk, s_rr, s_tS, s_tC = Slot(), Slot(), Slot(), Slot(), Slot(), Slot()
            s_inv = Slot()
            LOAD("sp", invrows[:], invrows_in, "a2", [s_inv])
            for hf in range(2):
                LOAD("sp", posi[:], pos_in[:, hf * HW:(hf + 1) * HW].broadcast_to([P, HW]), "a2", [s_pos])
                CP("dve", posf[:], posi[:], reads=[s_pos], writes=[s_ang])
                for j in range(2):
                    TS("dve", ang[:], posf[:], invrows[:, j:j + 1], None, ALU.mult, None, reads=[s_ang, s_inv], writes=[s_kk])
                    TS("dve", kk[:], ang[:], INV2PI, MAGIC, ALU.mult, ALU.add, reads=[s_kk], writes=[s_kk])
                    TS("dve", kk[:], kk[:], -MAGIC, None, ALU.add, None, reads=[s_kk], writes=[s_kk])
                    STT(rr[:], kk[:], -C1, ang[:], ALU.mult, ALU.add, reads=[s_kk], writes=[s_rr])
                    STT(rr[:], kk[:], -C2, rr[:], ALU.mult, ALU.add, reads=[s_kk, s_rr], writes=[s_rr])
                    ACT(tS[:], rr[:], AF.Sin, reads=[s_rr], writes=[s_tS], scale=0.5)
                    ACT(rr[:], rr[:], AF.Abs, reads=[s_rr], writes=[s_rr])
                    ACT(tC[:], rr[:], AF.Sin, reads=[s_rr, s_ones], writes=[s_tC], scale=-0.5, bias=halfpi[:])
                    STT(tS[:], tS[:], 2.0, tC[:], ALU.mult, ALU.mult, reads=[s_tS, s_tC], writes=[s_tS])
                    ACT(tC[:], rr[:], AF.Sin, reads=[s_rr, s_ones], writes=[s_tC], scale=-1.0, bias=halfpi[:])
                    STORE("pool", tab_s[2 * j, :, hf * HW:(hf + 1) * HW], tC[:], "a3", reads=[s_tC])
                    STORE("pool", tab_s[2 * j + 1, :, hf * HW:(hf + 1) * HW], tS[:], "a3", reads=[s_tS])

        def wgu_prep(l, stg, s_stg, flist):
            for f in flist:
                i = f % len(stg)
                for half in range(2):
                    src = wgu_in[l, :, half * DFF + f * P: half * DFF + (f + 1) * P].rearrange("(kc p) n -> p kc n", p=P)
                    R.dma("pool", stg[i][:, :, half * P:(half + 1) * P], src, f"wgst{i}",
                          writes=[s_stg[i]], append_w=(half == 1))
                R.dma("pool", wgu_s[l, f], stg[i][:].rearrange("p kc n -> p (kc n)"), f"wgsto{i}", reads=[s_stg[i]], writes=[Slot()])

        if True:
            stg = [sb(f"stg{i}", [P, NCH, 256], BF16) for i in range(3)]
            s_stg = [Slot() for _ in range(3)]
            xin = [sb(f"xin{i}", [P, 4, D], F32) for i in range(2)]
            s_xin = [Slot() for _ in range(2)]
            xT = [sb(f"xT{i}", [P, NCH, TB], F32) for i in range(2)]
            s_xT = [[Slot() for c in range(NCH)] for i in range(2)]
            PB = Banks(range(2, 8))
            ai = 0

            def ldx(tb):
                b = tb % 2
                LOAD("sp", xin[b][:], x_in[tb * TB:(tb + 1) * TB, :].rearrange("(t p) d -> p t d", p=P), f"xin{b}", [s_xin[b]])

            ldx(0)
            for tb in range(NTB):
                b = tb % 2
                if tb + 1 < NTB:
                    ldx(tb + 1)
                na = 4 if tb < 2 else 3
                for _ in range(na):
                    if ai < len(ada_steps):
                        ada_steps[ai]()
                        ai += 1
                for c in range(NCH):
                    k = PB.next()
                    fns = [lambda e, k=k, t=t, c=c, b=b: e.transpose(
                        out=psb[k][:, t * P:(t + 1) * P], in_=xin[b][:, t, c * P:(c + 1) * P], identity=identf[:])
                        for t in range(4)]
                    R.group("pe", fns, reads=[s_xin[b], s_c], writes=[s_psb[k]])
                    CP("act" if c % 2 == 0 else "dve", xT[b][:, c, :], psb[k][:], reads=[s_psb[k]], writes=[s_xT[b][c]])
                STORE("sp", xT_s[:, :, tb * TB:(tb + 1) * TB], xT[b][:], f"xTst{b}", reads=s_xT[b])
            while ai < len(ada_steps):
                ada_steps[ai]()
                ai += 1
            R.barrier()
            R.replay()
            st.close()
        if stop == "X0":
            return nc

        def load_tab(tile, j, tb, sem, slot):
            LOAD("sp", tile[:], tab_s[j, :, tb * TB:(tb + 1) * TB], sem, [slot])

        def ffn_front(l, xb, s_xb, cat, s_cat, wout, s_wout, sq, s_sq, rstd, s_rstd, tmp, s_tmp, h2, s_h2, PB):
            for c in range(NCH):
                k = PB.next()
                MM(psb[k][:], [(wout[:, kc, c * P:(c + 1) * P], cat[:, kc, :]) for kc in range(NCH)],
                   reads=s_cat + [s_wout], writes=[s_psb[k]])
                STT(xb[:, c, :], psb[k][:], AB[:, l, 2, c:c + 1], xb[:, c, :], ALU.mult, ALU.add,
                    reads=[s_psb[k], s_AB], writes=[s_xb[c]])
            k = PB.next()
            rms_rstd(sq, s_sq, [xb[:, c, :] for c in range(NCH)], [[s_xb[c]] for c in range(NCH)], NCH, D, k, rstd, s_rstd,
                     sq_eng="mix")
            for c in range(NCH):
                t = c % 2
                STT(tmp[t][:], xb[:, c, :], AB[:, l, 3, c:c + 1], rstd[:], ALU.mult, ALU.mult,
                    reads=[s_xb[c], s_rstd, s_AB], writes=[s_tmp[t]])
                ACT(h2[:, c, :], tmp[t][:], AF.Identity, reads=[s_tmp[t], s_AB], writes=[s_h2[c]], bias=AB[:, l, 4, c:c + 1], scale=1.0)

        def ffn_gateup(l, h2, s_h2, actb, s_act, sg, s_sg, wg, s_wg, PB, wgi):
            for f in range(NF):
                i = wgi[0] % len(wg)
                wgi[0] += 1
                LOAD("sp", wg[i][:].rearrange("p kc n -> p (kc n)"), wgu_s[l, f], f"wg{i}", [s_wg[i]])
                kg = PB.next()
                ku = PB.next()
                MM(psb[kg][:], [(wg[i][:, kc, 0:P], h2[:, kc, :]) for kc in range(NCH)], reads=s_h2 + [s_wg[i]], writes=[s_psb[kg]])
                MM(psb[ku][:], [(wg[i][:, kc, P:2 * P], h2[:, kc, :]) for kc in range(NCH)], reads=s_h2 + [s_wg[i]], writes=[s_psb[ku]])
                t = f % 2
                ACT(sg[t][:], psb[kg][:], AF.Silu, reads=[s_psb[kg]], writes=[s_sg[t]])
                TT("dve", actb[:, f, :], sg[t][:], psb[ku][:], ALU.mult, reads=[s_sg[t], s_psb[ku]], writes=[s_act[f]])

        def ffn_down(l, xb, s_xb, wd, s_wd, actb, s_act, PB):
            for c in range(NCH):
                k = PB.next()
                MM(psb[k][:], [(wd[:, f, c * P:(c + 1) * P], actb[:, f, :]) for f in range(NF)], reads=s_act + [s_wd], writes=[s_psb[k]])
                STT(xb[:, c, :], psb[k][:], AB[:, l, 5, c:c + 1], xb[:, c, :], ALU.mult, ALU.add,
                    reads=[s_psb[k], s_AB], writes=[s_xb[c]])

        def norm1_block(l, xb, s_xb, sq, s_sq, rstd, s_rstd, tmp, s_tmp, h, s_h, PB):
            k = PB.next()
            rms_rstd(sq, s_sq, [xb[:, c, :] for c in range(NCH)], [[s_xb]] * NCH, NCH, D, k, rstd, s_rstd, sq_eng="mix")
            for c in range(NCH):
                t = c % 2
                STT(tmp[t][:], xb[:, c, :], AB[:, l, 0, c:c + 1], rstd[:], ALU.mult, ALU.mult,
                    reads=[s_xb, s_rstd, s_AB], writes=[s_tmp[t]])
                ACT(h[:, c, :], tmp[t][:], AF.Identity, reads=[s_tmp[t], s_AB], writes=[s_h[c]], bias=AB[:, l, 1, c:c + 1], scale=1.0)

        def post_phase(l, woutX_in, final):
            with contextlib.ExitStack() as st:
                sb = lambda name, shape, dt: st.enter_context(nc.sbuf_tensor(_un(name), shape, dt))
                wd = sb("wd", [P, NF, D], BF16)
                wout = sb("wout", [P, NCH, D], BF16)
                s_wd, s_wout = Slot(), Slot()
                LOAD("pool", wout[:], woutX_in.rearrange("(kc p) n -> p kc n", p=P), "wout", [s_wout])
                LOAD("pool", wd[:], wd_in[l].rearrange("(f p) n -> p f n", p=P), "wd", [s_wd])
                wg = [sb(f"wg{i}", [P, NCH, 256], BF16) for i in range(4)]
                s_wg = [Slot() for _ in range(4)]
                xb = [sb(f"xb{i}", [P, NCH, TB], F32) for i in range(2)]
                s_xb = [[Slot() for c in range(NCH)] for i in range(2)]
                cat = [sb(f"cat{i}", [P, NCH, TB], BF16) for i in range(2)]
                s_cat = [Slot() for _ in range(2)]
                sq = sb("sq", [P, NCH, TB], BF16)
                s_sq = [Slot() for _ in range(NCH)]
                rstd = sb("rstd", [P, TB], F32)
                s_rstd = Slot()
                tmp = [sb(f"tmp{i}", [P, TB], F32) for i in range(2)]
                s_tmp = [Slot() for _ in range(2)]
                h2 = [sb(f"h2{i}", [P, NCH, TB], BF16) for i in range(2)]
                s_h2 = [[Slot() for _ in range(NCH)] for i in range(2)]
                actb = sb("actb", [P, NF, TB], BF16)
                s_act = [Slot() for _ in range(NF)]
                sg = [sb(f"sg{i}", [P, TB], F32) for i in range(2)]
                s_sg = [Slot() for _ in range(2)]
                if final:
                    ot = sb("ot", [P, 4, D], F32)
                    s_ot = [Slot() for _ in range(8)]
                PB = Banks(range(8))
                wgi = [0]
                def front(tb):
                    b = tb % 2
                    R.dma("sp", xb[b][:], xT_s[:, :, tb * TB:(tb + 1) * TB], f"xb{b}", writes=s_xb[b])
                    LOAD("sp", cat[b][:], cat_s[:, :, tb * TB:(tb + 1) * TB], f"cat{b}", [s_cat[b]])
                    ffn_front(l, xb[b], s_xb[b], cat[b], [s_cat[b]], wout, s_wout, sq, s_sq, rstd, s_rstd, tmp, s_tmp,
                              h2[b], s_h2[b], PB)

                front(0)
                for tb in range(NTB):
                    b = tb % 2
                    ffn_gateup(l, h2[b], s_h2[b], actb, s_act, sg, s_sg, wg, s_wg, PB, wgi)
                    if tb + 1 < NTB:
                        front(tb + 1)
                    ffn_down(l, xb[b], s_xb[b], wd, s_wd, actb, s_act, PB)
                    if not final:
                        STORE("sp", xT_s[:, :, tb * TB:(tb + 1) * TB], xb[b][:], f"xbst{b}", reads=s_xb[b])
                    else:
                        k = PB.next()
                        rms_rstd(sq, s_sq, [xb[b][:, c, :] for c in range(NCH)], [[s_xb[b][c]] for c in range(NCH)], NCH, D, k,
                                 rstd, s_rstd, sq_eng="mix")
                        for c in range(NCH):
                            STT(xb[b][:, c, :], xb[b][:, c, :], fng[:, c:c + 1], rstd[:], ALU.mult, ALU.mult,
                                reads=[s_rstd, s_c], writes=[s_xb[b][c]])
                        for t in range(4):
                            for half in range(2):
                                k = PB.next()
                                fns = [lambda e, k=k, t=t, j=j, half=half, b=b: e.transpose(
                                    out=psb[k][:, j * P:(j + 1) * P], in_=xb[b][:, half * 4 + j, t * P:(t + 1) * P],
                                    identity=identf[:]) for j in range(4)]
                                R.group("pe", fns, reads=s_xb[b][half * 4:half * 4 + 4] + [s_c], writes=[s_psb[k]])
                                CP("act", ot[:, t, half * TB:(half + 1) * TB], psb[k][:], reads=[s_psb[k]], writes=[s_ot[t * 2 + half]])
                        STORE("sp", out[tb * TB:(tb + 1) * TB, :].rearrange("(t p) d -> p t d", p=P), ot[:], "ost", reads=s_ot)
                R.barrier()
                R.replay()

        SC0 = 96.0 ** -0.5
        with contextlib.ExitStack() as st:
            sb = lambda name, shape, dt: st.enter_context(nc.sbuf_tensor(_un(name), shape, dt))
            win = sb("win", [P, NCH, 1184], BF16)
            wuq = sb("wuq", [P, 3, 800], BF16)
            wukv = sb("wukv", [P, 2, 1024], BF16)
            poolw = sb("poolw", [P, 4, P], BF16)
            qng = sb("qng", [P, 3], F32)
            kvng = sb("kvng", [P, 2], F32)
            poolb = sb("poolb", [P, 4], F32)
            pools = sb("pools", [P, 4], F32)
            cntinv = sb("cntinv", [P, 4, 16], F32)
            s_w = Slot()
            LOAD("pool", win[:], win_in.rearrange("(kc p) n -> p kc n", p=P), "w0", [s_w])
            s_wq = Slot()
            R.op("pool", lambda e: e.memset(wuq[:, :, 768:800], 0.0), writes=[s_wq])
            R.dma("pool", wuq[:, :, 0:768], wuq_in.rearrange("(kc p) n -> p kc n", p=P), "w0", writes=[s_w], append_w=True)
            R.dma("pool", wukv[:], wukv_in.rearrange("(kc p) n -> p kc n", p=P), "w0", writes=[s_w], append_w=True)
            R.dma("pool", poolw[:], poolw_in.rearrange("g c d -> c g d"), "w0", writes=[s_w], append_w=True)
            R.dma("sp", qng[:], qng_in, "w1", writes=[s_w], append_w=True)
            R.dma("sp", kvng[:], kvng_in, "w1", writes=[s_w], append_w=True)
            R.dma("sp", poolb[:], poolb_in, "w1", writes=[s_w], append_w=True)
            R.dma("sp", pools[:], pools_in, "w1", writes=[s_w], append_w=True)
            R.dma("sp", cntinv[:], cntinv_in, "w1", writes=[s_w], append_w=True)
            xb = [sb(f"xb{i}", [P, NCH, TB], F32) for i in range(2)]
            s_xb = [Slot() for _ in range(2)]
            tabC = [sb(f"tabC{i}", [P, TB], F32) for i in range(2)]
            tabS = [sb(f"tabS{i}", [P, TB], F32) for i in range(2)]
            s_tab = [Slot() for _ in range(2)]
            sq = sb("sq", [P, NCH, TB], BF16)
            s_sq = [Slot() for _ in range(NCH)]
            rstd = sb("rstd", [P, TB], F32)
            s_rstd = Slot()
            rstq = sb("rstq", [P, TB], F32)
            s_rstq = Slot()
            rstk = sb("rstk", [P, TB], F32)
            s_rstk = Slot()
            tmp = [sb(f"tmp{i}", [P, TB], F32) for i in range(2)]
            s_tmp = [Slot() for _ in range(2)]
            h = sb("h", [P, NCH, TB], BF16)
            s_h = [Slot() for _ in range(NCH)]
            cq = sb("cq", [P, 5, TB], F32)
            s_cq = [Slot() for _ in range(5)]
            cn = sb("cn", [P, 5, TB], BF16)
            s_cn = [Slot() for _ in range(5)]
            u = sb("u", [P, 4, 16 + TB], F32)
            s_u = [Slot() for _ in range(4)]
            lv = [sb(f"lv{i}", [P, 16 + TB], F32) for i in range(2)]
            s_lv = [Slot() for _ in range(2)]
            pooled = sb("pooled", [P, 4, TB], BF16)
            s_pl = [Slot() for _ in range(4)]
            ptmp = sb("ptmp", [P, 16], F32)
            s_ptmp = Slot()
            qb = [sb(f"qb{i}", [P, TB], BF16) for i in range(2)]
            s_qb = [Slot() for _ in range(2)]
            t1 = [sb(f"t1{i}", [P, TB], F32) for i in range(2)]
            s_t1 = [Slot() for _ in range(2)]
            t2 = [sb(f"t2{i}", [P, TB], F32) for i in range(2)]
            s_t2 = [Slot() for _ in range(2)]
            qT = [sb(f"qT{i}", [P, 8, TB], BF16) for i in range(2)]
            s_qT = [[Slot() for hh in range(8)] for _ in range(2)]
            kT = [sb(f"kT{i}", [P, 8, TB], BF16) for i in range(2)]
            s_kT = [[Slot() for hh in range(8)] for _ in range(2)]
            s_kTr = [[Slot() for hh in range(8)] for _ in range(2)]
            vt = [sb(f"vt{i}", [P, 4, 8, 65], BF16) for i in range(2)]
            s_vt = [[Slot() for tt in range(4)] for _ in range(2)]
            catp = [sb(f"catp{i}", [P, 4, TB], BF16) for i in range(2)]
            s_catp = [[Slot() for g in range(4)] for _ in range(2)]
            for i in range(2):
                R.op("pool", lambda e, i=i: e.memset(vt[i][:], 1.0), writes=s_vt[i])
            R.op("pool", lambda e: e.memset(u[:], 0.0), writes=s_u)
            PB = Banks(range(8))
            def ld0(tb):
                b = tb % 2
                LOAD("sp", xb[b][:], xT_s[:, :, tb * TB:(tb + 1) * TB], f"xb{b}", [s_xb[b]])
                R.dma("sp", tabC[b][:], tab_s[0, :, tb * TB:(tb + 1) * TB], f"tab{b}", writes=[s_tab[b]])
                R.dma("sp", tabS[b][:], tab_s[1, :, tb * TB:(tb + 1) * TB], f"tab{b}", writes=[s_tab[b]], append_w=True)

            ld0(0)
            for tb in range(NTB):
                b = tb % 2
                norm1_block(0, xb[b], s_xb[b], sq, s_sq, rstd, s_rstd, tmp, s_tmp, h, s_h, PB)
                if tb + 1 < NTB:
                    ld0(tb + 1)
                if DBG.get('sec', 99) < 1:
                    continue
                for j in range(5):
                    k = PB.next()
                    MM(psb[k][:], [(win[:, kc, j * P:(j + 1) * P], h[:, kc, :]) for kc in range(NCH)], reads=s_h + [s_w], writes=[s_psb[k]])
                    CP("act", cq[:, j, :], psb[k][:], reads=[s_psb[k]], writes=[s_cq[j]])
                k = PB.next()
                rms_rstd(sq, s_sq, [cq[:, j, :] for j in range(3)], [[s_cq[j]] for j in range(3)], 3, 384, k, rstq, s_rstq, sq_eng="mix")
                for j in range(3):
                    STT(cn[:, j, :], cq[:, j, :], qng[:, j:j + 1], rstq[:], ALU.mult, ALU.mult, reads=[s_cq[j], s_rstq, s_w], writes=[s_cn[j]])
                k = PB.next()
                rms_rstd(sq[:, 3:5, :], s_sq[3:5], [cq[:, 3 + j, :] for j in range(2)], [[s_cq[3 + j]] for j in range(2)], 2, 256, k, rstk, s_rstk, sq_eng="mix")
                for j in range(2):
                    STT(cn[:, 3 + j, :], cq[:, 3 + j, :], kvng[:, j:j + 1], rstk[:], ALU.mult, ALU.mult,
                        reads=[s_cq[3 + j], s_rstk, s_w], writes=[s_cn[3 + j]])
                if DBG.get('sec', 99) < 2:
                    continue
                k = PB.next()
                MM(psb[k][64:96, :], [(win[:, kc, 640:672], h[:, kc, :]) for kc in range(NCH)], reads=s_h + [s_w], writes=[s_psb[k]])
                i2 = 0
                CP("act", qb[i2][64:96, :], psb[k][64:96, :], reads=[s_psb[k]], writes=[s_qb[i2]])
                k2 = PB.next()
                MM(psb[k2][64:96, :], [(rmat_bf[64:96, 0, 64:96], qb[i2][64:96, :])], reads=[s_qb[i2], s_c], writes=[s_psb[k2]])
                TT("dve", t1[i2][64:96, :], psb[k][64:96, :], tabC[b][64:96, :], ALU.mult, reads=[s_psb[k], s_tab[b]], writes=[s_t1[i2]])
                TT("dve", t2[i2][64:96, :], psb[k2][64:96, :], tabS[b][64:96, :], ALU.mult, reads=[s_psb[k2], s_tab[b]], writes=[s_t2[i2]])
                TT("pool", t1[i2][64:96, :], t1[i2][64:96, :], t2[i2][64:96, :], ALU.add, reads=[s_t2[i2]], writes=[s_t1[i2]])
                for hh in range(8):
                    CP("pool" if hh % 2 == 0 else "act", kT[b][64:96, hh, :], t1[i2][64:96, :], reads=[s_t1[i2]], writes=[s_kTr[b][hh]])
                if DBG.get('sec', 99) < 3:
                    continue
                for g in range(4):
                    k = PB.next()
                    MM(psb[k][:], [(win[:, kc, 672 + g * P:672 + (g + 1) * P], h[:, kc, :]) for kc in range(NCH)], reads=s_h + [s_w], writes=[s_psb[k]])
                    CP("act", u[:, g, 16:16 + TB], psb[k][:], reads=[s_psb[k]], writes=[s_u[g]])
                if DBG.get('sec', 99) < 4:
                    continue
                SUB = 99
                bank_q = {}

                def st1(hh):
                    k = PB.next()
                    bank_q[hh] = k
                    i2 = hh % 2
                    MM(psb[k][:], [(wuq[:, kc, hh * 96:hh * 96 + P], cn[:, kc, :]) for kc in range(3)], reads=s_cn[0:3] + [s_w, s_wq], writes=[s_psb[k]])
                    CP("act", qb[i2][:], psb[k][:], reads=[s_psb[k]], writes=[s_qb[i2]])

                def st2(hh):
                    k = bank_q[hh]
                    i2 = hh % 2
                    k2 = PB.next()
                    MM(psb[k2][:], [(rmat_bf[:, 0, :], qb[i2][:])], reads=[s_qb[i2], s_c], writes=[s_psb[k2]])
                    TT("dve", t1[i2][:], psb[k][:], tabC[b][:], ALU.mult, reads=[s_psb[k], s_tab[b]], writes=[s_t1[i2]])
                    TT("dve", t2[i2][:], psb[k2][:], tabS[b][:], ALU.mult, reads=[s_psb[k2], s_tab[b]], writes=[s_t2[i2]])
                    TT("pool", qT[b][:, hh, :], t1[i2][:], t2[i2][:], ALU.add, reads=[s_t1[i2], s_t2[i2]], writes=[s_qT[b][hh]])

                st1(0)
                for hh in range(8):
                    if hh + 1 < 8:
                        st1(hh + 1)
                    st2(hh)
                if SUB >= 4:
                    STORE("sp", q_s[0:96, :, tb * TB:(tb + 1) * TB], qT[b][0:96, :, :], f"qst{b}", reads=s_qT[b])
                if DBG.get('sec', 99) < 5:
                    continue
                for hh in range(8):
                    k = PB.next()
                    MM(psb[k][0:64, :], [(wukv[:, kc, hh * 128:hh * 128 + 64], cn[:, 3 + kc, :]) for kc in range(2)], reads=s_cn[3:5] + [s_w], writes=[s_psb[k]])
                    CP("act" if hh % 2 == 0 else "dve", kT[b][0:64, hh, :], psb[k][0:64, :], reads=[s_psb[k]], writes=[s_kT[b][hh]])
                STORE("sp", k_s[0:96, :, tb * TB:(tb + 1) * TB], kT[b][0:96, :, :], f"kst{b}", reads=s_kT[b] + s_kTr[b])
                if DBG.get('sec', 99) < 6:
                    continue
                for tt in range(4):
                    k = PB.next()
                    MM(psb[k][:].rearrange("p (h e) -> p h e", e=64), [(cn[:, 3 + kc, tt * P:(tt + 1) * P],
                                    wukv[:, kc, :].rearrange("p (h e) -> p h e", e=128)[:, :, 64:128]) for kc in range(2)],
                       reads=s_cn[3:5] + [s_w], writes=[s_psb[k]])
                    CP("dve" if tt % 2 == 0 else "act", vt[b][:, tt, :, 0:64], psb[k][:].rearrange("p (h e) -> p h e", e=64),
                       reads=[s_psb[k]], writes=[s_vt[b][tt]])
                STORE("sp", v_s[:, tb * 4 * 520:(tb + 1) * 4 * 520], vt[b][:].rearrange("p t h e -> p (t h e)"), f"vst{b}", reads=s_vt[b])
                if DBG.get('sec', 99) < 7:
                    continue
                for g in range(4):
                    w = 2 << g
                    src = u[:, g, :]
                    s_src = s_u[g]
                    sh = 1
                    lvl = 0
                    while sh < w:
                        dst = lv[lvl % 2]
                        TT("pool", dst[:, sh:16 + TB], src[:, sh:16 + TB], src[:, 0:16 + TB - sh], ALU.add,
                           reads=[s_src], writes=[s_lv[lvl % 2]])
                        src = dst
                        s_src = s_lv[lvl % 2]
                        sh *= 2
                        lvl += 1
                    STT(pooled[:, g, :], src[:, 16:16 + TB], 1.0 / w, u[:, g, 16:16 + TB], ALU.mult, ALU.subtract,
                        reads=[s_src, s_u[g]], writes=[s_pl[g]])
                    if tb == 0:
                        TT("dve", ptmp[:], src[:, 16:32], cntinv[:, g, :], ALU.mult, reads=[s_src, s_w], writes=[s_ptmp])
                        TT("dve", pooled[:, g, 0:16], ptmp[:], u[:, g, 16:32], ALU.subtract, reads=[s_ptmp, s_u[g], s_pl[g]], writes=[s_pl[g]])
                    k = PB.next()
                    MM(psb[k][:], [(poolw[:, g, :], pooled[:, g, :])], reads=[s_pl[g], s_w], writes=[s_psb[k]])
                    TS("dve", catp[b][:, g, :], psb[k][:], poolb[:, g:g + 1], pools[:, g:g + 1], ALU.add, ALU.mult,
                       reads=[s_psb[k], s_w], writes=[s_catp[b][g]])
                    CP("pool", u[:, g, 0:16], u[:, g, TB:TB + 16], reads=[], writes=[s_u[g]])
                STORE("sp", cat_s[:, 4:8, tb * TB:(tb + 1) * TB], catp[b][:], f"cpst{b}", reads=s_catp[b])
            R.barrier()
            R.replay()
        if stop == "L0P1":
            return nc

        with contextlib.ExitStack() as st:
            sb = lambda name, shape, dt: st.enter_context(nc.sbuf_tensor(_un(name), shape, dt))
            stg = [sb(f"stg{i}", [P, NCH, 256], BF16) for i in range(3)]
            s_stg = [Slot() for _ in range(3)]
            va = sb("va", [P, 32, 8, 65], BF16)
            s_va = Slot()
            LOAD("sp", va[:].rearrange("p t h e -> p (t h e)"), v_s[:, 0:32 * 520], "va", [s_va])
            vg = sb("vg", [P, 32, 8, P], BF16)
            s_vgh = [Slot() for _ in range(8)]
            R.op("pool", lambda e: e.memset(vg[:], 1.0), writes=s_vgh)
            for hh in range(8):
                off = 0 if hh % 2 == 0 else 64
                CP("pool" if hh % 2 == 0 else "dve", vg[:, :, hh, off:off + 64], va[:, :, hh, 0:64], reads=[s_va], writes=[s_vgh[hh]])
            qh = [sb(f"qh{i}", [P, S], BF16) for i in range(2)]
            kh = [sb(f"kh{i}", [P, S], BF16) for i in range(2)]
            s_qk = [Slot() for _ in range(2)]
            rinv = [sb(f"rinv{i}", [P, TB], F32) for i in range(2)]
            s_rinv = [Slot() for _ in range(2)]
            attn = sb("attn", [P, 4, S], BF16)
            s_attn = [Slot() for _ in range(4)]
            NPT = 6
            pt = [sb(f"ptx{i}", [P, TB], BF16) for i in range(NPT)]
            s_pt = [Slot() for _ in range(NPT)]
            SBK = [0, 1, 2, 3]
            OBK = [4, 5]
            LA = 2
            items = []
            for hh in range(8):
                for qblk in range(NTB):
                    nkt = 4 * qblk + 4
                    for kt in range(nkt):
                        items.append((hh, qblk, kt, nkt))
            prep_done = [0]
            prep_list = [(pl, pf) for pl in range(n_layers) for pf in range(NF)]
            stg2, s_stg2 = stg, s_stg

            def load_head(hh):
                i = hh % 2
                LOAD("sp", qh[i][0:96, :], q_s[0:96, hh, :], f"qh{i}", [s_qk[i]])
                R.dma("sp", kh[i][0:96, :], k_s[0:96, hh, :], f"qh{i}", writes=[s_qk[i]], append_w=True)

            def emitA(n):
                hh, qblk, kt, nkt = items[n]
                i = hh % 2
                if qblk == 0 and kt == 0:
                    if hh == 0:
                        load_head(0)
                    if hh + 1 < 8:
                        load_head(hh + 1)
                j = kt - 4 * qblk
                c0 = max(j, 0) * P
                sbk = SBK[n % len(SBK)]
                fns = [lambda e: e.matmul(psb[sbk][:, c0:TB], lhsT=kh[i][0:96, kt * P:(kt + 1) * P],
                                          rhs=qh[i][0:96, qblk * TB + c0:(qblk + 1) * TB], start=True, stop=(j < 0))]
                if j >= 0:
                    fns.append(lambda e: e.matmul(psb[sbk][:, c0:c0 + P], lhsT=ident_bf[:], rhs=maskb_bf[:], start=False, stop=True))
                R.group("pe", fns, reads=[s_qk[i], s_c], writes=[s_psb[sbk]])
                p_i = n % NPT
                ACT(pt[p_i][:, c0:TB], psb[sbk][:, c0:TB], AF.Exp, reads=[s_psb[sbk]], writes=[s_pt[p_i]], scale=SC0)

            def emitC(n):
                hh, qblk, kt, nkt = items[n]
                blk = hh * NTB + qblk
                ob = OBK[blk % 2]
                j = kt - 4 * qblk
                c0 = max(j, 0) * P
                p_i = n % NPT
                R.group("pe", [lambda e: e.matmul(psb[ob][:, c0:TB], lhsT=vg[:, kt, hh, :], rhs=pt[p_i][:, c0:TB],
                                                  start=(kt == 0), stop=(kt == nkt - 1))],
                        reads=[s_pt[p_i], s_vgh[hh]], writes=[s_psb[ob]])
                if kt == nkt - 1:
                    ri = blk % 2
                    odd = hh % 2
                    if odd == 0:
                        RECIP(rinv[ri][0:64, :], psb[ob][64:128, :], reads=[s_psb[ob]], writes=[s_rinv[ri]])
                        TT("dve", attn[0:64, hh // 2, qblk * TB:(qblk + 1) * TB], psb[ob][0:64, :], rinv[ri][0:64, :], ALU.mult,
                           reads=[s_psb[ob], s_rinv[ri]], writes=[s_attn[hh // 2]])
                    else:
                        RECIP(rinv[ri][64:128, :], psb[ob][0:64, :], reads=[s_psb[ob]], writes=[s_rinv[ri]])
                        TT("dve", attn[64:128, hh // 2, qblk * TB:(qblk + 1) * TB], psb[ob][64:128, :], rinv[ri][64:128, :], ALU.mult,
                           reads=[s_psb[ob], s_rinv[ri]], writes=[s_attn[hh // 2]])
                    if prep_done[0] < len(prep_list):
                        pl, pf = prep_list[prep_done[0]]
                        wgu_prep(pl, stg2, s_stg2, [pf])
                        prep_done[0] += 1
                    if odd and qblk == NTB - 1:
                        STORE("sp", cat_s[:, hh // 2, :], attn[:, hh // 2, :], "ast", reads=[s_attn[hh // 2]])

            for n in range(len(items) + LA):
                if n < len(items):
                    emitA(n)
                if n - LA >= 0:
                    emitC(n - LA)
            R.barrier()
            R.replay()
        if stop == "L0P2":
            return nc

        post_phase(0, woutA_in, final=(n_layers == 1))
        if stop == "L0P3":
            return nc

        if n_layers > 1:
            SC1 = 64.0 ** -0.5
            with contextlib.ExitStack() as st:
                sb = lambda name, shape, dt: st.enter_context(nc.sbuf_tensor(_un(name), shape, dt))
                wqkv = sb("wqkv", [P, NCH, 3 * D], BF16)
                s_w = Slot()
                for j in range(3):
                    R.dma("pool", wqkv[:, :, j * D:(j + 1) * D], wqkv_in[:, j * D:(j + 1) * D].rearrange("(kc p) n -> p kc n", p=P),
                          "w0", writes=[s_w], append_w=(j > 0))
                xb = [sb(f"xb{i}", [P, NCH, TB], F32) for i in range(2)]
                s_xb = [Slot() for _ in range(2)]
                tabC = [sb(f"tabC{i}", [P, TB], F32) for i in range(2)]
                tabS = [sb(f"tabS{i}", [P, TB], F32) for i in range(2)]
                s_tab = [Slot() for _ in range(2)]
                sq = sb("sq", [P, NCH, TB], BF16)
                s_sq = [Slot() for _ in range(NCH)]
                rstd = sb("rstd", [P, TB], F32)
                s_rstd = Slot()
                tmp = [sb(f"tmp{i}", [P, TB], F32) for i in range(2)]
                s_tmp = [Slot() for _ in range(2)]
                h = sb("h", [P, NCH, TB], BF16)
                s_h = [Slot() for _ in range(NCH)]
                qb = [sb(f"qb{i}", [P, TB], BF16) for i in range(2)]
                s_qb = [Slot() for _ in range(2)]
                t1 = [sb(f"t1{i}", [P, TB], F32) for i in range(2)]
                s_t1 = [Slot() for _ in range(2)]
                t2 = [sb(f"t2{i}", [P, TB], F32) for i in range(2)]
                s_t2 = [Slot() for _ in range(2)]
                qk = [sb(f"qk{i}", [P, 16, TB], BF16) for i in range(2)]
                s_qkT = [[Slot() for c in range(16)] for _ in range(2)]
                vt = [sb(f"vt{i}", [P, 4, D], BF16) for i in range(2)]
                s_vt = [[Slot() for c in range(8)] for _ in range(2)]
                PB = Banks(range(8))
                def ld1(tb):
                    b = tb % 2
                    LOAD("sp", xb[b][:], xT_s[:, :, tb * TB:(tb + 1) * TB], f"xb{b}", [s_xb[b]])
                    R.dma("sp", tabC[b][:], tab_s[2, :, tb * TB:(tb + 1) * TB], f"tab{b}", writes=[s_tab[b]])
                    R.dma("sp", tabS[b][:], tab_s[3, :, tb * TB:(tb + 1) * TB], f"tab{b}", writes=[s_tab[b]], append_w=True)

                ld1(0)
                for tb in range(NTB):
                    b = tb % 2
                    norm1_block(1, xb[b], s_xb[b], sq, s_sq, rstd, s_rstd, tmp, s_tmp, h, s_h, PB)
                    if tb + 1 < NTB:
                        ld1(tb + 1)
                    bank_q = {}

                    def st1(c):
                        k = PB.next()
                        bank_q[c] = k
                        i2 = c % 2
                        MM(psb[k][:], [(wqkv[:, kc, c * P:(c + 1) * P], h[:, kc, :]) for kc in range(NCH)], reads=s_h + [s_w], writes=[s_psb[k]])
                        CP("act", qb[i2][:], psb[k][:], reads=[s_psb[k]], writes=[s_qb[i2]])

                    def st2(c):
                        k = bank_q[c]
                        i2 = c % 2
                        k2 = PB.next()
                        MM(psb[k2][:], [(rmat_bf[:, 1, :], qb[i2][:])], reads=[s_qb[i2], s_c], writes=[s_psb[k2]])
                        TT("dve", t1[i2][:], psb[k][:], tabC[b][:], ALU.mult, reads=[s_psb[k], s_tab[b]], writes=[s_t1[i2]])
                        TT("dve", t2[i2][:], psb[k2][:], tabS[b][:], ALU.mult, reads=[s_psb[k2], s_tab[b]], writes=[s_t2[i2]])
                        TT("pool", qk[b][:, c, :], t1[i2][:], t2[i2][:], ALU.add, reads=[s_t1[i2], s_t2[i2]], writes=[s_qkT[b][c]])

                    st1(0)
                    for c in range(16):
                        if c + 1 < 16:
                            st1(c + 1)
                        st2(c)
                    STORE("sp", q_s[:, :, tb * TB:(tb + 1) * TB], qk[b][:, 0:8, :], f"qst{b}", reads=s_qkT[b][0:8])
                    STORE("sp", k_s[:, :, tb * TB:(tb + 1) * TB], qk[b][:, 8:16, :], f"kst{b}", reads=s_qkT[b][8:16])
                    for tt in range(4):
                        for half in range(2):
                            k = PB.next()
                            MM(psb[k][:], [(h[:, kc, tt * P:(tt + 1) * P], wqkv[:, kc, 2 * D + half * TB:2 * D + (half + 1) * TB])
                                            for kc in range(NCH)], reads=s_h + [s_w], writes=[s_psb[k]])
                            CP("act" if half == 0 else "dve", vt[b][:, tt, half * TB:(half + 1) * TB], psb[k][:], reads=[s_psb[k]],
                               writes=[s_vt[b][tt * 2 + half]])
                    STORE("sp", v_s[:, tb * 4 * D:(tb + 1) * 4 * D], vt[b][:].rearrange("p t d -> p (t d)"), f"vst{b}", reads=s_vt[b])
                R.barrier()
                R.replay()
            if stop == "L1P1":
                return nc

            with contextlib.ExitStack() as st:
                sb = lambda name, shape, dt: st.enter_context(nc.sbuf_tensor(_un(name), shape, dt))
                va = sb("va", [P, 32, D], BF16)
                s_va = Slot()
                LOAD("sp", va[:].rearrange("p t d -> p (t d)"), v_s[:, :], "va", [s_va])
                qh = [sb(f"qh{i}", [P, S], BF16) for i in range(2)]
                kh = [sb(f"kh{i}", [P, S], BF16) for i in range(2)]
                s_qk = [Slot() for _ in range(2)]
                rinv = [sb(f"rinv{i}", [P, TB], F32) for i in range(2)]
                s_rinv = [Slot() for _ in range(2)]
                a0 = sb("a0", [P, TB], F32)
                a1 = sb("a1", [P, TB], F32)
                s_a0, s_a1 = Slot(), Slot()
                sqa = sb("sqa", [P, 1, TB], BF16)
                s_sqa = [Slot()]
                rsa = sb("rsa", [P, TB], F32)
                s_rsa = Slot()
                attn = [sb(f"attn{i}", [P, S], BF16) for i in range(2)]
                s_attn = [Slot() for _ in range(2)]
                NPT = 10
                ptp = [sb(f"pty{i}", [P, 2, TB], BF16) for i in range(NPT)]
                s_pt = [Slot() for _ in range(NPT)]
                tri2 = sb("tri2", [P, 2, P], BF16)
                s_tri2 = Slot()
                CP("pool", tri2[:, 0, :], tri_bf[:], reads=[s_c], writes=[s_tri2])
                CP("pool", tri2[:, 1, :], tri_bf[:], reads=[s_c, s_tri2], writes=[s_tri2])
                ones_f = sb("ones_f", [P, P], F32)
                s_onesf = Slot()
                R.op("pool", lambda e: e.memset(ones_f[:], 1.0), writes=[s_onesf])
                EAt = [sb(f"EA{b}", [P, 3, TB], F32) for b in range(2)]
                EA = [[EAt[b][:, k, :] for k in range(3)] for b in range(2)]
                s_EA = [[Slot() for k in range(3)] for b in range(2)]
                rv = [sb(f"rv{i}", [P, TB], F32) for i in range(2)]
                s_rv = [Slot() for _ in range(2)]
                PAIRS = [0, 2]
                LA = 2
                items = []
                for hh in range(8):
                    for qblk in range(NTB):
                        nkt = 4 * qblk + 4
                        for kt in range(nkt):
                            items.append((hh, qblk, kt, nkt))
                        items.append((hh, qblk, -1, nkt))
                        items.append((hh, qblk, -2, nkt))

                def load_head(hh):
                    i = hh % 2
                    LOAD("sp", qh[i][:], q_s[:, hh, :], f"qh{i}", [s_qk[i]])
                    R.dma("sp", kh[i][:], k_s[:, hh, :], f"qh{i}", writes=[s_qk[i]], append_w=True)

                def emitA(n):
                    hh, qblk, kt, nkt = items[n]
                    if kt < 0:
                        return
                    i = hh % 2
                    blk = hh * NTB + qblk
                    bb = blk % 2
                    if qblk == 0 and kt == 0:
                        if hh == 0:
                            load_head(0)
                        if hh + 1 < 8:
                            load_head(hh + 1)
                    j = kt - 4 * qblk
                    c0 = max(j, 0) * P
                    p0 = PAIRS[n % 2]
                    fns = [lambda e: e.matmul(psb[p0][:, c0:TB], lhsT=kh[i][0:64, kt * P:(kt + 1) * P],
                                              rhs=qh[i][0:64, qblk * TB + c0:(qblk + 1) * TB], start=True, stop=(j < 0)),
                           lambda e: e.matmul(psb[p0 + 1][:, c0:TB], lhsT=kh[i][64:128, kt * P:(kt + 1) * P],
                                              rhs=qh[i][64:128, qblk * TB + c0:(qblk + 1) * TB], start=True, stop=(j < 0))]
                    if j >= 0:
                        fns.append(lambda e: e.matmul(psb[p0][:, c0:c0 + P], lhsT=ident_bf[:], rhs=maskb_bf[:], start=False, stop=True))
                        fns.append(lambda e: e.matmul(psb[p0 + 1][:, c0:c0 + P], lhsT=ident_bf[:], rhs=maskb_bf[:], start=False, stop=True))
                    R.group("pe", fns, reads=[s_qk[i], s_c], writes=[s_psb[p0], s_psb[p0 + 1]])
                    p_i = n % NPT
                    ACT(ptp[p_i][:, :, c0:TB], psall[:, p0:p0 + 2, c0:TB], AF.Exp, reads=[s_psb[p0], s_psb[p0 + 1]],
                        writes=[s_pt[p_i]], scale=SC1)
                    if kt == 0:
                        CP("dve", EA[bb][0][:, c0:TB], ptp[p_i][:, 0, c0:TB], reads=[s_pt[p_i]], writes=[s_EA[bb][0]])
                    else:
                        TT("dve", EA[bb][0][:, c0:TB], EA[bb][0][:, c0:TB], ptp[p_i][:, 0, c0:TB], ALU.add,
                           reads=[s_pt[p_i]], writes=[s_EA[bb][0]])

                def emitC(n):
                    hh, qblk, kt, nkt = items[n]
                    i = hh % 2
                    blk = hh * NTB + qblk
                    bb = blk % 2
                    o0, o1, s0b, s1b = 4, 5, 6, 7
                    p0 = PAIRS[n % 2]
                    if kt >= 0:
                        j = kt - 4 * qblk
                        c0 = max(j, 0) * P
                        p_i = n % NPT
                        R.group("pe", [lambda e: e.matmul(psb[o0][:, c0:TB], lhsT=va[:, kt, hh * P:(hh + 1) * P], rhs=ptp[p_i][:, 0, c0:TB],
                                                          start=(kt == 0), stop=(kt == nkt - 1)),
                                       lambda e: e.matmul(psb[o1][:, c0:TB], lhsT=va[:, kt, hh * P:(hh + 1) * P], rhs=ptp[p_i][:, 1, c0:TB],
                                                          start=(kt == 0), stop=(kt == nkt - 1)),
                                       lambda e: e.matmul(psb[s1b][:, c0:TB], lhsT=ones_bf[:], rhs=ptp[p_i][:, 1, c0:TB],
                                                          start=(kt == 0), stop=(kt == nkt - 1))],
                                reads=[s_pt[p_i], s_va, s_ones], writes=[s_psb[o0], s_psb[o1], s_psb[s1b]])
                    elif kt == -1:
                        MM(psb[s0b][:], [(ones_f[:], EA[bb][0][:])], reads=[s_EA[bb][0], s_onesf], writes=[s_psb[s0b]])
                        for c, sbk in ((0, s0b), (1, s1b)):
                            ACT(rv[c][:], psb[sbk][:], AF.Ln, reads=[s_psb[sbk]], writes=[s_rv[c]])
                            ACT(rv[c][:], rv[c][:], AF.Exp, reads=[s_rv[c]], writes=[s_rv[c]], scale=-1.0)
                        TT("dve", a0[:], psb[o0][:], rv[0][:], ALU.mult, reads=[s_psb[o0], s_rv[0]], writes=[s_a0])
                        TT("dve", a1[:], psb[o1][:], rv[1][:], ALU.mult, reads=[s_psb[o1], s_rv[1]], writes=[s_a1])
                        STT(a0[:], a1[:], lam_neg[:, 0:1], a0[:], ALU.mult, ALU.add, reads=[s_a1, s_a0, s_lam], writes=[s_a0])
                        TT("pool", sqa[:, 0, :], a0[:], a0[:], ALU.mult, reads=[s_a0], writes=[s_sqa[0]])
                    else:
                        MM(psb[p0][:], [(ones_bf[:], sqa[:, 0, :])], reads=[s_sqa[0], s_ones], writes=[s_psb[p0]])
                        ACT(rsa[:], psb[p0][:], AF.Ln, reads=[s_psb[p0], s_ones], writes=[s_rsa], scale=1.0 / 128, bias=epsc[:])
                        ACT(rsa[:], rsa[:], AF.Exp, reads=[s_rsa], writes=[s_rsa], scale=-0.5)
                        STT(attn[i][:, qblk * TB:(qblk + 1) * TB], a0[:], subg2[:, 0:1], rsa[:], ALU.mult, ALU.mult,
                            reads=[s_a0, s_rsa, s_lam], writes=[s_attn[i]])
                        if qblk == NTB - 1:
                            STORE("sp", cat_s[:, hh, :], attn[i][:], f"ast{i}", reads=[s_attn[i]])

                for n in range(len(items) + LA):
                    if n < len(items):
                        emitA(n)
                    if n - LA >= 0:
                        emitC(n - LA)
                R.barrier()
                R.replay()
            if stop == "L1P2":
                return nc

            post_phase(1, woutD_in, final=True)

        gst.callback(lambda: None)
    return nc


def _consts():
    ident = np.eye(P, dtype=np.float32)
    tri = (np.arange(P)[None, :] >= np.arange(P)[:, None]).astype(np.float32)
    rm = np.zeros((2, P, P), np.float32)
    for i in range(16):
        rm[0, 80 + i, 64 + i] = -1.0
        rm[0, 64 + i, 80 + i] = 1.0
    for c in range(2):
        for i in range(8):
            rm[1, c * 64 + 8 + i, c * 64 + i] = -1.0
            rm[1, c * 64 + i, c * 64 + 8 + i] = 1.0
    theta = np.float32(500000.0)
    inv32 = (theta ** (-(np.arange(0, 32, 2, dtype=np.float32) / np.float32(32)))).astype(np.float32)
    inv16 = (theta ** (-(np.arange(0, 16, 2, dtype=np.float32) / np.float32(16)))).astype(np.float32)
    invrows = np.zeros((P, 2), np.float32)
    for i in range(16):
        invrows[64 + i, 0] = inv32[i]
        invrows[80 + i, 0] = inv32[i]
    for c in range(2):
        for i in range(8):
            invrows[c * 64 + i, 1] = inv16[i]
            invrows[c * 64 + 8 + i, 1] = inv16[i]
    cnt = np.zeros((P, 4, 16), np.float32)
    for g, w in enumerate((2, 4, 8, 16)):
        cnt[:, g, :] = (1.0 / np.minimum(np.arange(1, 17), w)).astype(np.float32)[None, :]
    maskb = np.where(np.arange(P)[None, :] >= np.arange(P)[:, None], 0.0, -30000.0).astype(np.float32)
    return ident, tri, rm, invrows, cnt, maskb


def _cols(v, n):
    return np.ascontiguousarray(np.asarray(v, np.float32).reshape(n, P).T)


_NC_CACHE = {}


def kernel(**inputs):
    g = lambda k: np.asarray(inputs[k])
    x = g("x").astype(np.float32)
    ident, tri, rm, invrows, cnt, maskb = _consts()
    if "nc" not in _NC_CACHE:
        _NC_CACHE["nc"] = build_program()
    nc = _NC_CACHE["nc"]
    shared = {
        "ada_w": np.ascontiguousarray(g("ada_w"), dtype=np.float32),
        "adab": np.ascontiguousarray(g("ada_b").astype(np.float32).reshape(2, 48, P).transpose(0, 2, 1)),
        "n1g": np.ascontiguousarray(g("norm1_g").astype(np.float32).reshape(2, NCH, P).transpose(0, 2, 1)),
        "n2g": np.ascontiguousarray(g("norm2_g").astype(np.float32).reshape(2, NCH, P).transpose(0, 2, 1)),
        "fng": _cols(g("final_norm_g"), NCH),
        "w_gu": np.ascontiguousarray(g("ffn_w_gate_up"), dtype=np.float32),
        "w_d": np.ascontiguousarray(g("ffn_w_down"), dtype=np.float32),
        "w_in": np.ascontiguousarray(g("mla_w_in")[0], dtype=np.float32),
        "qng": _cols(g("mla_q_norm_g")[0], 3),
        "kvng": _cols(g("mla_kv_norm_g")[0], 2),
        "w_uq": np.ascontiguousarray(g("mla_w_uq")[0], dtype=np.float32),
        "w_ukv": np.ascontiguousarray(g("mla_w_ukv")[0], dtype=np.float32),
        "pool_w": np.ascontiguousarray(g("pool_w")[0], dtype=np.float32),
        "pool_b": _cols(g("pool_b")[0].reshape(-1), 4),
        "pool_s": _cols(g("pool_scale")[0], 4),
        "w_outA": np.ascontiguousarray(g("mix_a_w_out")[0], dtype=np.float32),
        "w_qkv": np.ascontiguousarray(g("diff_w_qkv")[0], dtype=np.float32),
        "lamv": np.ascontiguousarray(np.broadcast_to(np.stack([
            g("diff_lambda_q1")[0], g("diff_lambda_k1")[0], g("diff_lambda_q2")[0], g("diff_lambda_k2")[0]]).astype(np.float32)[None],
            (P, 4, 64))),
        "subg": np.ascontiguousarray(g("diff_subln_g")[0].astype(np.float32).reshape(P, 1)),
        "w_outD": np.ascontiguousarray(g("diff_w_out")[0], dtype=np.float32),
        "ident_f": ident, "tri": tri, "maskb": maskb, "rmat": rm, "invrows": invrows, "cntinv": cnt,
    }
    c = g("c").astype(np.float32)
    pos = g("positions").astype(np.int32)
    in_maps = []
    ncores = int(_NC_CACHE.get("ncores", 8))
    for b in range(ncores):
        m = dict(shared)
        m["x"] = np.ascontiguousarray(x[b])
        m["cT"] = _cols(c[b], NCH)
        m["pos"] = np.ascontiguousarray(pos[b][None, :])
        in_maps.append(m)
    res = run_bass_kernel_spmd(nc, in_maps, core_ids=list(range(ncores)))
    return np.stack([np.asarray(r["out"], dtype=np.float32) for r in res.results], axis=0)
```
